# Optimizing a Trainium2 kernel written in Bass

```python
import jax
import jax.numpy as jnp
from jax import lax
import numpy as np

D_MODEL = 1024
BATCH = 1
SEQ = 16384
DEPTH = 4

A_HEADS = 8
A_HEAD_DIM = 64
A_WIDTH = A_HEADS * A_HEAD_DIM
A_DECAY_LORA = 64
A_ICLR_LORA = 64
A_VRES_LORA = 32
A_GATE_LORA = 128
A_GN_EPS = 64e-5
B_HEADS = 8
B_NOPE_DIM = 64
B_ROPE_DIM = 32
B_V_DIM = 64
B_Q_RANK = 384
B_KV_RANK = 256
B_WIDTH = B_HEADS * B_V_DIM
ROPE_THETA = 10000.0
Q_BLOCK = 128
C_HEADS = 4
C_EXPAND = 128
C_HEAD_DIM = 128
C_FDIM = C_HEADS * C_EXPAND
C_WIDTH = C_HEADS * C_HEAD_DIM
C_CHUNK = 64
C_MIN_FORGET = 1e-6
D_FF = 2816
N_BRANCH = 3
MACARON_WEIGHT = 0.5
NORM_EPS = 1e-6

A_SPLITS = (A_WIDTH, A_WIDTH, A_WIDTH, A_DECAY_LORA, A_ICLR_LORA, A_GATE_LORA)
REST_SPLITS = (B_Q_RANK, B_KV_RANK, B_ROPE_DIM, C_FDIM, C_FDIM, C_WIDTH, C_WIDTH, N_BRANCH * D_MODEL)
A_COLS = sum(A_SPLITS)
N_IN = A_COLS + sum(REST_SPLITS)

kernel_name = 'hybrid_rwkv7_mla_hgrn2_macaron'


def split_cols(t, sizes):
    return jnp.split(t, np.cumsum(sizes)[:-1].tolist(), axis=-1)


def rms_norm(x, g):
    x32 = x.astype(jnp.float32)
    y = x32 * lax.rsqrt(jnp.mean(x32 * x32, axis=-1, keepdims=True) + NORM_EPS)
    return (y * g.astype(jnp.float32)).astype(x.dtype)


def token_shift(p):
    return jnp.pad(p, ((0, 0), (1, 0), (0, 0)))[:, :-1]


def shift_mix(p, mu):
    return p + (token_shift(p) - p) * mu


def swiglu(x, w_gate, w_up, w_down):
    return (jax.nn.silu(x @ w_gate) * (x @ w_up)) @ w_down


def rope_tables(positions):
    inv_freq = ROPE_THETA ** (-jnp.arange(0, B_ROPE_DIM, 2, dtype=jnp.float32) / B_ROPE_DIM)
    ang = positions.astype(jnp.float32)[..., None] * inv_freq
    return jnp.cos(ang), jnp.sin(ang)


def apply_rope(x, cos, sin):
    x1, x2 = jnp.split(x.astype(jnp.float32), 2, axis=-1)
    return jnp.concatenate([x1 * cos - x2 * sin, x2 * cos + x1 * sin], axis=-1).astype(x.dtype)


def rwkv7_scan(r, w, k, v, kk, a):
    bsz, _, n_h, n = r.shape

    def step(state, inp):
        r_t, w_t, k_t, v_t, kk_t, a_t = inp
        sa = jnp.einsum('bhvk,bhk->bhv', state, -kk_t)
        state = (state * w_t[:, :, None, :]
                 + sa[..., None] * (kk_t * a_t)[:, :, None, :]
                 + v_t[..., None] * k_t[:, :, None, :])
        return state, jnp.einsum('bhvk,bhk->bhv', state, r_t)

    xs = tuple(jnp.moveaxis(t, 1, 0) for t in (r, w, k, v, kk, a))
    state0 = jnp.zeros((bsz, n_h, n, n), jnp.float32)
    _, y = lax.scan(step, state0, xs)
    return jnp.moveaxis(y, 0, 1)


def rwkv7_branch(r, k, v, wl, al, gl, w0, w_up, a0, a_up, g_up, k_k, k_a, r_k, gn_g, gn_b):
    bsz, seq, _ = r.shape
    f32 = jnp.float32

    def heads(t):
        return t.astype(f32).reshape(bsz, seq, A_HEADS, A_HEAD_DIM)

    w = -jax.nn.softplus(-(w0 + jnp.tanh(wl) @ w_up)) - 0.5
    decay = jnp.exp(-jnp.exp(w.astype(f32)))
    a = jax.nn.sigmoid(a0 + al @ a_up)
    g = jax.nn.sigmoid(gl) @ g_up
    kk = heads(k * k_k)
    kk = kk / jnp.maximum(jnp.sqrt(jnp.sum(kk * kk, axis=-1, keepdims=True)), 1e-12)
    k = k * (1.0 + (a - 1.0) * k_a)
    rh, kh, vh, ah = heads(r), heads(k), heads(v), heads(a)
    y = rwkv7_scan(rh, heads(decay), kh, vh, kk, ah)
    mean = jnp.mean(y, axis=-1, keepdims=True)
    var = jnp.mean(jnp.square(y - mean), axis=-1, keepdims=True)
    y = ((y - mean) * lax.rsqrt(var + A_GN_EPS)).reshape(bsz, seq, A_WIDTH) * gn_g.astype(f32) + gn_b.astype(f32)
    bonus = jnp.sum(rh * kh * r_k.astype(f32), axis=-1, keepdims=True) * vh
    y = y + bonus.reshape(bsz, seq, A_WIDTH)
    return y.astype(r.dtype) * g


def causal_block_attention(q, k, v):
    bsz, seq, n_h, d_qk = q.shape
    n_blk = seq // Q_BLOCK
    scale = d_qk ** -0.5
    q_blocks = jnp.moveaxis(q.reshape(bsz, n_blk, Q_BLOCK, n_h, d_qk), 1, 0)
    k_pos = jnp.arange(seq)
    neg = jnp.finfo(jnp.float32).min

    def one_block(args):
        q_blk, blk = args
        s = jnp.einsum('bqhd,bkhd->bhqk', q_blk, k).astype(jnp.float32) * scale
        q_pos = blk * Q_BLOCK + jnp.arange(Q_BLOCK)
        s = jnp.where(k_pos[None, :] <= q_pos[:, None], s, neg)
        p = jax.nn.softmax(s, axis=-1)
        return jnp.einsum('bhqk,bkhd->bqhd', p.astype(v.dtype), v)

    o = lax.map(one_block, (q_blocks, jnp.arange(n_blk)))
    return jnp.moveaxis(o, 0, 1).reshape(bsz, seq, n_h, v.shape[-1])


def mla_branch(cq, ckv, kr, cos, sin, q_norm_g, w_uq, kv_norm_g, w_ukv):
    bsz, seq, _ = cq.shape
    q = (rms_norm(cq, q_norm_g) @ w_uq).reshape(bsz, seq, B_HEADS, B_NOPE_DIM + B_ROPE_DIM)
    q_nope, q_rope = q[..., :B_NOPE_DIM], q[..., B_NOPE_DIM:]
    q_rope = apply_rope(q_rope, cos[:, :, None], sin[:, :, None])
    kv = (rms_norm(ckv, kv_norm_g) @ w_ukv).reshape(bsz, seq, B_HEADS, B_NOPE_DIM + B_V_DIM)
    k_nope, v = kv[..., :B_NOPE_DIM], kv[..., B_NOPE_DIM:]
    k_rope = apply_rope(kr, cos, sin)[:, :, None, :]
    k = jnp.concatenate([k_nope, jnp.broadcast_to(k_rope, (bsz, seq, B_HEADS, B_ROPE_DIM))], axis=-1)
    q = jnp.concatenate([q_nope, q_rope], axis=-1)
    o = causal_block_attention(q, k, v)
    return o.reshape(bsz, seq, B_WIDTH)


def hgrn2_chunk_scan(q, k, log_f, v):
    bsz, seq, n_h, d_k = q.shape
    d_v = v.shape[-1]
    n_chunk = seq // C_CHUNK

    def to_chunks(t):
        return jnp.moveaxis(t.reshape(bsz, n_chunk, C_CHUNK, *t.shape[2:]), 1, 0)

    causal = jnp.tril(jnp.ones((C_CHUNK, C_CHUNK), dtype=bool))[None, :, :, None, None]

    def step(state, inp):
        qc, kc, lfc, vc = inp
        b = jnp.cumsum(lfc, axis=1)
        diff = b[:, :, None] - b[:, None, :]
        decay = jnp.where(causal, jnp.exp(jnp.where(causal, diff, 0.0)), 0.0)
        attn = jnp.einsum('bthk,btshk,bshk->btsh', qc, decay, kc)
        o = (jnp.einsum('btsh,bshv->bthv', attn, vc)
             + jnp.einsum('bthk,bhkv->bthv', qc * jnp.exp(b), state))
        b_last = b[:, -1]
        state = (jnp.exp(b_last)[..., None] * state
                 + jnp.einsum('bshk,bshv->bhkv', kc * jnp.exp(b_last[:, None] - b), vc))
        return state, o

    state0 = jnp.zeros((bsz, n_h, d_k, d_v), jnp.float32)
    _, o = lax.scan(step, state0, tuple(to_chunks(t) for t in (q, k, log_f, v)))
    return jnp.moveaxis(o, 0, 1).reshape(bsz, seq, n_h, d_v)


def hgrn2_branch(cq, cf, ci, cg, lb, norm_g):
    bsz, seq, _ = cq.shape
    f32 = jnp.float32
    fz = cf.astype(f32)
    lb = lb.astype(f32)
    f = lb + (1.0 - lb) * jax.nn.sigmoid(fz)
    log_f = jnp.log(jnp.maximum(f, C_MIN_FORGET))
    k = (1.0 - lb) * jax.nn.sigmoid(-fz)
    q = jax.nn.silu(cq.astype(f32))

    def fheads(t):
        return t.reshape(bsz, seq, C_HEADS, C_EXPAND)

    o = hgrn2_chunk_scan(fheads(q), fheads(k), fheads(log_f),
                         ci.astype(f32).reshape(bsz, seq, C_HEADS, C_HEAD_DIM))
    o = rms_norm(o, norm_g) * jax.nn.silu(cg.astype(f32)).reshape(bsz, seq, C_HEADS, C_HEAD_DIM)
    return o.reshape(bsz, seq, C_WIDTH).astype(cq.dtype)


def setup_inputs(seed: int = 0) -> dict:
    key = jax.random.key(seed)
    it = iter(list(jax.random.split(key, 48)))
    f32 = jnp.float32
    L = DEPTH

    def dense(shape):
        return jax.random.normal(next(it), shape, f32) * (shape[-2] ** -0.5)

    def gain(shape):
        return 1.0 + 0.05 * jax.random.normal(next(it), shape, f32)

    def noise(shape, scale, offset=0.0):
        return offset + scale * jax.random.normal(next(it), shape, f32)

    def mix_coef(shape):
        return jax.random.uniform(next(it), shape, f32, 0.05, 0.95)

    return {
        'x': jax.random.normal(next(it), (BATCH, SEQ, D_MODEL), f32),
        'positions': jnp.broadcast_to(jnp.arange(SEQ, dtype=jnp.int32), (BATCH, SEQ)),
        'ffn1_pre_g': gain((L, D_MODEL)),
        'ffn1_post_g': gain((L, D_MODEL)),
        'ffn1_w_gate': dense((L, D_MODEL, D_FF)),
        'ffn1_w_up': dense((L, D_MODEL, D_FF)),
        'ffn1_w_down': dense((L, D_FF, D_MODEL)),
        'mix_pre_g': gain((L, D_MODEL)),
        'mix_post_g': gain((L, D_MODEL)),
        'w_in': dense((L, D_MODEL, N_IN)),
        'rwkv_mu': mix_coef((L, A_COLS)),
        'rwkv_w0': noise((L, A_WIDTH), 0.5),
        'rwkv_w_up': dense((L, A_DECAY_LORA, A_WIDTH)),
        'rwkv_a0': noise((L, A_WIDTH), 0.1),
        'rwkv_a_up': dense((L, A_ICLR_LORA, A_WIDTH)),
        'rwkv_g_up': dense((L, A_GATE_LORA, A_WIDTH)),
        'rwkv_k_k': noise((L, A_WIDTH), 0.05, 0.85),
        'rwkv_k_a': gain((L, A_WIDTH)),
        'rwkv_r_k': noise((L, A_HEADS, A_HEAD_DIM), 0.1),
        'rwkv_gn_g': gain((L, A_WIDTH)),
        'rwkv_gn_b': noise((L, A_WIDTH), 0.02),
        'rwkv_vres_down': dense((L - 1, D_MODEL, A_VRES_LORA)),
        'rwkv_vres_mu': mix_coef((L - 1, A_VRES_LORA)),
        'rwkv_vres_up': dense((L - 1, A_VRES_LORA, A_WIDTH)),
        'rwkv_v0': noise((L - 1, A_WIDTH), 0.1, 1.0),
        'rwkv_out': dense((L, A_WIDTH, D_MODEL)),
        'mla_q_norm_g': gain((L, B_Q_RANK)),
        'mla_w_uq': dense((L, B_Q_RANK, B_HEADS * (B_NOPE_DIM + B_ROPE_DIM))),
        'mla_kv_norm_g': gain((L, B_KV_RANK)),
        'mla_w_ukv': dense((L, B_KV_RANK, B_HEADS * (B_NOPE_DIM + B_V_DIM))),
        'mla_out': dense((L, B_WIDTH, D_MODEL)),
        'hgrn_lower_bounds': gain((L, C_FDIM)),
        'hgrn_norm_g': gain((L, C_HEAD_DIM)),
        'hgrn_out': dense((L, C_WIDTH, D_MODEL)),
        'w_o': dense((L, D_MODEL, D_MODEL)),
        'ffn2_pre_g': gain((L, D_MODEL)),
        'ffn2_post_g': gain((L, D_MODEL)),
        'ffn2_w_gate': dense((L, D_MODEL, D_FF)),
        'ffn2_w_up': dense((L, D_MODEL, D_FF)),
        'ffn2_w_down': dense((L, D_FF, D_MODEL)),
    }


def reference(x, positions, ffn1_pre_g, ffn1_post_g, ffn1_w_gate, ffn1_w_up, ffn1_w_down,
              mix_pre_g, mix_post_g, w_in,
              rwkv_mu, rwkv_w0, rwkv_w_up, rwkv_a0, rwkv_a_up, rwkv_g_up, rwkv_k_k, rwkv_k_a, rwkv_r_k,
              rwkv_gn_g, rwkv_gn_b, rwkv_vres_down, rwkv_vres_mu, rwkv_vres_up, rwkv_v0, rwkv_out,
              mla_q_norm_g, mla_w_uq, mla_kv_norm_g, mla_w_ukv, mla_out,
              hgrn_lower_bounds, hgrn_norm_g, hgrn_out,
              w_o, ffn2_pre_g, ffn2_post_g, ffn2_w_gate, ffn2_w_up, ffn2_w_down):
    bsz, seq, _ = x.shape
    cos, sin = rope_tables(positions)
    lb_p = jax.nn.softmax(hgrn_lower_bounds.astype(jnp.float32), axis=0)
    lower_bounds = jnp.cumsum(lb_p, axis=0) - lb_p[0]
    h = x
    v_first = None
    for l in range(DEPTH):
        y = swiglu(rms_norm(h, ffn1_pre_g[l]), ffn1_w_gate[l], ffn1_w_up[l], ffn1_w_down[l])
        h = h + MACARON_WEIGHT * rms_norm(y, ffn1_post_g[l])

        u = rms_norm(h, mix_pre_g[l])
        if l == 0:
            proj = u @ w_in[l]
        else:
            proj = u @ jnp.concatenate([w_in[l], rwkv_vres_down[l - 1]], axis=1)
        a_r, a_k, a_v, a_wl, a_al, a_gl = split_cols(shift_mix(proj[..., :A_COLS], rwkv_mu[l]), A_SPLITS)
        b_cq, b_ckv, b_kr, c_q, c_f, c_i, c_g, gate_logits = split_cols(proj[..., A_COLS:N_IN], REST_SPLITS)
        if l == 0:
            v_first = a_v
        else:
            a_vl = shift_mix(proj[..., N_IN:], rwkv_vres_mu[l - 1])
            a_v = a_v + (v_first - a_v) * jax.nn.sigmoid(rwkv_v0[l - 1] + a_vl @ rwkv_vres_up[l - 1])

        y_a = rwkv7_branch(a_r, a_k, a_v, a_wl, a_al, a_gl, rwkv_w0[l], rwkv_w_up[l], rwkv_a0[l],
                           rwkv_a_up[l], rwkv_g_up[l], rwkv_k_k[l], rwkv_k_a[l], rwkv_r_k[l],
                           rwkv_gn_g[l], rwkv_gn_b[l])
        y_b = mla_branch(b_cq, b_ckv, b_kr, cos, sin, mla_q_norm_g[l], mla_w_uq[l],
                         mla_kv_norm_g[l], mla_w_ukv[l])
        y_c = hgrn2_branch(c_q, c_f, c_i, c_g, lower_bounds[l], hgrn_norm_g[l])

        gates = jax.nn.sigmoid(gate_logits).reshape(bsz, seq, N_BRANCH, D_MODEL)
        merged = (gates[:, :, 0] * (y_a @ rwkv_out[l])
                  + gates[:, :, 1] * (y_b @ mla_out[l])
                  + gates[:, :, 2] * (y_c @ hgrn_out[l]))
        h = h + rms_norm(merged @ w_o[l], mix_post_g[l])

        y = swiglu(rms_norm(h, ffn2_pre_g[l]), ffn2_w_gate[l], ffn2_w_up[l], ffn2_w_down[l])
        h = h + MACARON_WEIGHT * rms_norm(y, ffn2_post_g[l])
    return h
```

```python
import numpy as np
from contextlib import ExitStack
import ml_dtypes
import concourse.bass as bass
import concourse.mybir as mybir
from concourse.bass_utils import run_bass_kernel_spmd

F32 = mybir.dt.float32
BF16 = mybir.dt.bfloat16
I32 = mybir.dt.int32
AF = mybir.ActivationFunctionType
ALU = mybir.AluOpType
AX = mybir.AxisListType

NCORES = 8
D = 1024
DFF = 2816
NF = DFF // 128
NK = D // 128
EPS = 1e-6
A_COLS = 1792
N_IN = 7584
OFF_CQ, OFF_CKV, OFF_KR = 1792, 2176, 2432
OFF_HQ, OFF_HF, OFF_HI, OFF_HG, OFF_GATE = 2464, 2976, 3488, 4000, 4512
TT = 512


class T:
    __slots__ = ("ap", "keys")

    def __init__(self, ap, keys):
        self.ap = ap
        self.keys = (keys,) if isinstance(keys, str) else tuple(keys)

    def __getitem__(self, idx):
        return T(self.ap[idx], self.keys)

    def k(self, *keys):
        return T(self.ap, keys)

    def rearrange(self, pat, **kw):
        return T(self.ap.rearrange(pat, **kw), self.keys)

    def bcast(self, shape):
        return T(self.ap.to_broadcast(list(shape)), self.keys)


def _keys(*xs):
    ks = []
    for x in xs:
        if isinstance(x, T):
            ks.extend(x.keys)
    return ks


def _ap(x):
    return x.ap if isinstance(x, T) else x


import os as _os
import sys
MAXOPS = int(_os.environ.get("CUTOPS", "100000000"))
EPOCH_KEY = "__epoch__"
SEM_EPOCH = 30000
NDSEM = 6


class Sched:
    COMPUTE = ("pe", "act", "dve", "pool")

    def __init__(self, nc):
        self.nc = nc
        self.ops = []
        self.lines = []

    mute = False

    def add(self, eng, fn, reads, writes, dma=False):
        if self.mute or len(self.ops) >= MAXOPS:
            return
        self.lines.append(sys._getframe(2).f_lineno)
        self.ops.append((eng, fn, tuple(reads) + (EPOCH_KEY,), tuple(writes), dma))

    def barrier(self):
        o = self.bar_tile.ap
        self.lines.append(0)
        self.ops.append(("dve", lambda e: e.memset(o, 0.0), (), (EPOCH_KEY,) + self.bar_tile.keys, False))

    def mm(self, out, lhsT, rhs, start=True, stop=True, **kw):
        o, l, r = out.ap, lhsT.ap, rhs.ap
        self.add("pe", lambda e: e.matmul(o, lhsT=l, rhs=r, start=start, stop=stop, **kw),
                 _keys(lhsT, rhs), _keys(out))

    def transpose(self, out, in_, ident):
        self.mm(out, in_, ident)

    def act(self, out, in_, func, bias=None, scale=None, accum=None):
        o, i = out.ap, in_.ap
        kw = {}
        if bias is not None:
            kw["bias"] = _ap(bias)
        if scale is not None:
            kw["scale"] = _ap(scale)
        if accum is not None:
            kw["accum_out"] = _ap(accum)
        self.add("act", lambda e: e.activation(out=o, in_=i, func=func, **kw),
                 _keys(in_, bias, scale), _keys(out, accum))

    def tt(self, eng, out, a, b, op):
        o, x, y = out.ap, a.ap, b.ap
        self.add(eng, lambda e: e.tensor_tensor(out=o, in0=x, in1=y, op=op), _keys(a, b), _keys(out))

    def ts(self, eng, out, a, s1, op0, s2=None, op1=None):
        o, x = out.ap, a.ap
        s1a, s2a = _ap(s1), _ap(s2)
        if op1 is None:
            self.add(eng, lambda e: e.tensor_scalar(out=o, in0=x, scalar1=s1a, scalar2=None, op0=op0),
                     _keys(a, s1), _keys(out))
        else:
            self.add(eng, lambda e: e.tensor_scalar(out=o, in0=x, scalar1=s1a, scalar2=s2a, op0=op0, op1=op1),
                     _keys(a, s1, s2), _keys(out))

    def stt(self, eng, out, a, scalar, b, op0, op1):
        o, x, y, s = out.ap, a.ap, b.ap, _ap(scalar)
        eng = "dve"
        self.add(eng, lambda e: e.scalar_tensor_tensor(out=o, in0=x, scalar=s, in1=y, op0=op0, op1=op1),
                 _keys(a, scalar, b), _keys(out))

    def copy(self, eng, out, in_):
        o, i = out.ap, in_.ap
        if eng == "act":
            self.add("act", lambda e: e.activation(out=o, in_=i, func=AF.Copy), _keys(in_), _keys(out))
        else:
            self.add(eng, lambda e: e.tensor_copy(out=o, in_=i), _keys(in_), _keys(out))

    def memset(self, eng, out, val):
        o = out.ap
        self.add(eng, lambda e: e.memset(o, val), (), _keys(out))

    def scan(self, out, d0, d1, init, op0, op1):
        o, a, b = out.ap, d0.ap, d1.ap
        self.add("dve", lambda e: e.tensor_tensor_scan(out=o, data0=a, data1=b, initial=init, op0=op0, op1=op1),
                 _keys(d0, d1), _keys(out))

    def reduce(self, eng, out, in_, op, axis=AX.X):
        o, i = out.ap, in_.ap
        self.add(eng, lambda e: e.tensor_reduce(out=o, in_=i, axis=axis, op=op), _keys(in_), _keys(out))

    def recip(self, out, in_):
        o, i = out.ap, in_.ap
        self.add("dve", lambda e: e.reciprocal(out=o, in_=i), _keys(in_), _keys(out))

    def dma(self, q, out, in_):
        o, i = out.ap, in_.ap
        self.add(q, lambda e: e.dma_start(out=o, in_=i), _keys(in_), _keys(out), dma=True)

    def emit(self):
        nc = self.nc
        ops = self.ops
        n = len(ops)
        last_w = {}
        rd_eng = {}
        rd_dma = {}
        deps = [None] * n
        signal = [False] * n
        for i, (eng, fn, rd, wr, dma) in enumerate(ops):
            cand = []
            for k in rd:
                j = last_w.get(k)
                if j is not None:
                    cand.append((j, 0))
            for k in wr:
                j = last_w.get(k)
                if j is not None:
                    cand.append((j, 1))
                for j in rd_eng.get(k, {}).values():
                    cand.append((j, 2))
                for j in rd_dma.get(k, ()):
                    cand.append((j, 2))
            best = {}
            dl = set()
            for j, kind in cand:
                if j == i:
                    continue
                ej, _, _, _, dj = ops[j]
                if dj:
                    dl.add(j)
                    continue
                if (not dma) and ej == eng:
                    if eng == "pe" or kind != 0:
                        continue
                if best.get(ej, -1) < j:
                    best[ej] = j
            dd = set(best.values()) | dl
            deps[i] = dd
            for j in dd:
                signal[j] = True
            for k in rd:
                if dma:
                    rd_dma.setdefault(k, []).append(i)
                else:
                    rd_eng.setdefault(k, {})[eng] = i
            for k in wr:
                last_w[k] = i
                rd_eng[k] = {}
                rd_dma[k] = []
        cnt = {e: 0 for e in self.COMPUTE}
        sigval = [None] * n
        dcnt = {}
        for i, (eng, fn, rd, wr, dma) in enumerate(ops):
            if dma:
                q = dcnt.get(eng, 0)
                dcnt[eng] = q + 1
                sigval[i] = ("d", eng, q % NDSEM, 16 * (q // NDSEM + 1), q)
            elif signal[i]:
                cnt[eng] += 1
                sigval[i] = ("c", eng, cnt[eng])
        self.stats = dict(n=n, cnt=dict(cnt), dcnt=dict(dcnt))
        by_eng = {}
        for i, op in enumerate(ops):
            by_eng.setdefault(op[0], []).append(i)
        with ExitStack() as es:
            csems = {}
            for e in self.COMPUTE:
                ne = max(1, (cnt[e] + SEM_EPOCH - 1) // SEM_EPOCH)
                csems[e] = [es.enter_context(nc.semaphore("c_%s_%d" % (e, t))) for t in range(ne)]
            dsems = {}
            for e in dcnt:
                dsems[e] = [es.enter_context(nc.semaphore("d_%s_%d" % (e, t))) for t in range(NDSEM)]
            block = es.enter_context(nc.Block())

            def make(engname):
                mine = by_eng.get(engname, [])

                def body(e):
                    cw = {x: 0 for x in self.COMPUTE}
                    dw = {}
                    for i in mine:
                        eng, fn, rd, wr, dma = ops[i]
                        if dma:
                            _, _, slot, val, q = sigval[i]
                            if q >= NDSEM:
                                key = (eng, slot)
                                if dw.get(key, 0) < val - 16:
                                    e.wait_ge(dsems[eng][slot], val - 16)
                                    dw[key] = val - 16
                        for j in sorted(deps[i]):
                            sv = sigval[j]
                            if sv[0] == "c":
                                _, ej, c = sv
                                if cw[ej] >= c:
                                    continue
                                cw[ej] = c
                                e.wait_ge(csems[ej][(c - 1) // SEM_EPOCH], (c - 1) % SEM_EPOCH + 1)
                            else:
                                _, ej, slot, val, q = sv
                                key = (ej, slot)
                                if dw.get(key, 0) >= val:
                                    continue
                                dw[key] = val
                                e.wait_ge(dsems[ej][slot], val)
                        inst = fn(e)
                        sv = sigval[i]
                        if sv is not None:
                            if sv[0] == "c":
                                c = sv[2]
                                inst.then_inc(csems[eng][(c - 1) // SEM_EPOCH], 1)
                            else:
                                inst.then_inc(dsems[eng][sv[2]], 16)
                    if engname in dcnt:
                        tot = dcnt[engname]
                        for slot in range(NDSEM):
                            uses = (tot - slot + NDSEM - 1) // NDSEM if tot > slot else 0
                            if uses > 0 and dw.get((engname, slot), 0) < 16 * uses:
                                e.wait_ge(dsems[engname][slot], 16 * uses)
                return body

            block.tensor(make("pe"))
            block.scalar(make("act"))
            block.vector(make("dve"))
            block.gpsimd(make("pool"))
            block.sync(make("sp"))


class Arena:
    def __init__(self, nc, es, words):
        self.t = es.enter_context(nc.sbuf_tensor("arena", [128, words], F32))
        self.words = words
        self.off = 0
        self.uid = 0

    def alloc(self, name, free_shape, dtype=F32, nslots=None):
        nel = int(np.prod(free_shape))
        w = (nel + 1) // 2 if dtype == BF16 else nel
        w = (w + 7) // 8 * 8
        assert self.off + w <= self.words, ("SBUF arena overflow", name, self.off, w, self.words)
        ap = self.t[:, self.off:self.off + w]
        if dtype != F32:
            ap = ap.bitcast(dtype)
        ap = ap[:, 0:nel]
        if len(free_shape) == 2:
            ap = ap.rearrange("p (a b) -> p a b", a=free_shape[0])
        elif len(free_shape) == 3:
            ap = ap.rearrange("p (a b c) -> p a b c", a=free_shape[0], b=free_shape[1])
        self.off += w
        self.uid += 1
        return T(ap, "%s#%d" % (name, self.uid))

    def mark(self):
        return self.off

    def release(self, m):
        self.off = m


class Scratch:
    def __init__(self, ar, ngran):
        self.base = ar.off
        self.t = ar.t
        self.ngran = ngran
        ar.off += ngran * 512
        assert ar.off <= ar.words, ("scratch overflow", ar.off, ar.words)
        self.pos = 0
        self.peak = 0

    def reset(self):
        self.pos = 0

    def get(self, free_shape, dtype=F32):
        nel = int(np.prod(free_shape))
        w = (nel + 1) // 2 if dtype == BF16 else nel
        g = (w + 511) // 512
        assert self.pos + g <= self.ngran, ("scratch granules exhausted", self.pos, g, self.ngran)
        o = self.base + self.pos * 512
        ap = self.t[:, o:o + g * 512]
        if dtype != F32:
            ap = ap.bitcast(dtype)
        ap = ap[:, 0:nel]
        if len(free_shape) == 2:
            ap = ap.rearrange("p (a b) -> p a b", a=free_shape[0])
        elif len(free_shape) == 3:
            ap = ap.rearrange("p (a b c) -> p a b c", a=free_shape[0], b=free_shape[1])
        keys = tuple("scr%d" % (self.pos + i) for i in range(g))
        self.pos += g
        self.peak = max(self.peak, self.pos)
        return T(ap, keys)


class Ctx:
    pass


def make_ctx(nc, es, arena_words):
    cx = Ctx()
    cx.nc = nc
    cx.S = Sched(nc)
    cx.ar = Arena(nc, es, arena_words)
    cx.ps = [T(es.enter_context(nc.psum_tensor("psb%d" % i, [128, 512], F32))[:], "psb%d" % i) for i in range(8)]
    cx.S.bar_tile = cx.ar.alloc("bar", [8])
    cx.dbg = None
    return cx


def dram_in(nc, name, shape, dt=F32):
    return T(nc.dram_tensor(name, list(shape), dt, kind="ExternalInput").ap(), "dram:" + name)


def dram_out(nc, name, shape, dt=F32):
    return T(nc.dram_tensor(name, list(shape), dt, kind="ExternalOutput").ap(), "dram:" + name)


def load_consts(cx, cdram):
    S, ar = cx.S, cx.ar
    cx.ident = ar.alloc("ident", [128])
    cx.ones_f = ar.alloc("ones_f", [128])
    cx.ones_b = ar.alloc("ones_b", [128], BF16)
    S.dma("sp", cx.ident, cdram[:, 0:128])
    S.dma("sp", cx.ones_f, cdram[:, 128:256])
    S.copy("dve", cx.ones_b, cx.ones_f)


def emit_rstd(cx, ss_ps, rs, n_feat, rows=128):
    cx.S.act(rs[0:rows], ss_ps[0:rows], AF.Sqrt, bias=float(n_feat * EPS))
    cx.S.recip(rs[0:rows], rs[0:rows])


def emit_ffn(cx, hT, NT, wg_d, wu_d, wd_d, g_pre, g_post):
    S, ar, ps = cx.S, cx.ar, cx.ps
    G = 1
    GW = G * TT
    m0 = ar.mark()
    xn = ar.alloc("xn", [NK, GW], BF16)
    hid = ar.alloc("hid", [NF, GW], BF16)
    yb = ar.alloc("ffy", [NK, GW])
    sq = [ar.alloc("sq%d" % i, [TT], BF16) for i in range(2)]
    rs = [ar.alloc("rs%d" % i, [TT]) for i in range(G)]
    sg = [ar.alloc("sg%d" % i, [TT]) for i in range(2)]
    tmp = [ar.alloc("ftmp%d" % i, [TT]) for i in range(2)]
    NWB = 3
    wgb = [ar.alloc("wgb%d" % i, [NK, 128], BF16) for i in range(NWB)]
    wub = [ar.alloc("wub%d" % i, [NK, 128], BF16) for i in range(NWB)]
    wdb = [ar.alloc("wdb%d" % i, [NF, 128], BF16) for i in range(2)]
    ss_ps = [ps[6], ps[7]]
    nsq = 0
    for g in range(NT // GW):
        for s in range(G):
            c0 = g * GW + s * TT
            for k in range(NK):
                b = sq[nsq % 2]
                nsq += 1
                S.act(b, hT[:, k, c0:c0 + TT], AF.Square)
                S.mm(ss_ps[s], cx.ones_b, b, start=(k == 0), stop=(k == NK - 1))
            emit_rstd(cx, ss_ps[s], rs[s], D)
            for k in range(NK):
                S.stt("dve" if k % 2 == 0 else "pool", xn[:, k, s * TT:(s + 1) * TT], hT[:, k, c0:c0 + TT],
                      g_pre[:, k:k + 1], rs[s], ALU.mult, ALU.mult)
        it = 0

        def ld_gu(f):
            S.dma("pool", wgb[f % NWB], wg_d[f].rearrange("p (k m) -> p k m", k=NK))
            S.dma("pool", wub[f % NWB], wu_d[f].rearrange("p (k m) -> p k m", k=NK))

        def ld_d(d):
            S.dma("pool", wdb[d % 2], wd_d[d].rearrange("p (f m) -> p f m", f=NF))

        for f in range(NWB - 1):
            ld_gu(f)
        for f in range(NF):
            wgt, wut = wgb[f % NWB], wub[f % NWB]
            if f + NWB - 1 < NF:
                ld_gu(f + NWB - 1)
            if f == NF - 2:
                ld_d(0)
            for s in range(G):
                gp, up = ps[(it % 2) * 2], ps[(it % 2) * 2 + 1]
                sgt = sg[it % 2]
                it += 1
                for k in range(NK):
                    S.mm(gp, wgt[:, k, :], xn[:, k, s * TT:(s + 1) * TT], start=(k == 0), stop=(k == NK - 1))
                for k in range(NK):
                    S.mm(up, wut[:, k, :], xn[:, k, s * TT:(s + 1) * TT], start=(k == 0), stop=(k == NK - 1))
                S.act(sgt, gp, AF.Silu)
                S.tt("dve", hid[:, f, s * TT:(s + 1) * TT], sgt, up, ALU.mult)
        it = 0
        for d in range(NK):
            wdt = wdb[d % 2]
            if d + 1 < NK:
                ld_d(d + 1)
            for s in range(G):
                yp = ps[4 + it % 2]
                it += 1
                for f in range(NF):
                    S.mm(yp, wdt[:, f, :], hid[:, f, s * TT:(s + 1) * TT], start=(f == 0), stop=(f == NF - 1))
                S.act(yb[:, d, s * TT:(s + 1) * TT], yp, AF.Copy)
                b = sq[nsq % 2]
                nsq += 1
                S.act(b, yp, AF.Square)
                S.mm(ss_ps[s], cx.ones_b, b, start=(d == 0), stop=(d == NK - 1))
        for s in range(G):
            c0 = g * GW + s * TT
            emit_rstd(cx, ss_ps[s], rs[s], D)
            for d in range(NK):
                t = tmp[d % 2]
                S.stt("dve", t, yb[:, d, s * TT:(s + 1) * TT], g_post[:, d:d + 1], rs[s], ALU.mult, ALU.mult)
                S.tt("pool", hT[:, d, c0:c0 + TT], hT[:, d, c0:c0 + TT], t, ALU.add)
    S.barrier()
    ar.release(m0)


def emit_norm_to(cx, hT, NT, g32, out_bf, rows_feat=D):
    S, ar, ps = cx.S, cx.ar, cx.ps
    m0 = ar.mark()
    sq = [ar.alloc("nsq%d" % i, [TT], BF16) for i in range(2)]
    rs = ar.alloc("nrs", [TT])
    n = 0
    for t in range(NT // TT):
        c0 = t * TT
        for k in range(NK):
            b = sq[n % 2]
            n += 1
            S.act(b, hT[:, k, c0:c0 + TT], AF.Square)
            S.mm(ps[6], cx.ones_b, b, start=(k == 0), stop=(k == NK - 1))
        emit_rstd(cx, ps[6], rs, D)
        for k in range(NK):
            S.stt("dve" if k % 2 == 0 else "pool", out_bf[:, k, c0:c0 + TT], hT[:, k, c0:c0 + TT],
                  g32[:, k:k + 1], rs, ALU.mult, ALU.mult)
    S.barrier()
    ar.release(m0)


def lay_vec(v):
    v = np.asarray(v, np.float32)
    return np.ascontiguousarray(v.reshape(-1, 128).T)


def lay_w_gu(w):
    w = np.asarray(w, np.float32)
    return np.ascontiguousarray(w.reshape(NK, 128, NF, 128).transpose(2, 1, 0, 3).reshape(NF, 128, NK * 128))


def lay_w_d(w):
    w = np.asarray(w, np.float32)
    return np.ascontiguousarray(w.reshape(NF, 128, NK, 128).transpose(2, 1, 0, 3).reshape(NK, 128, NF * 128))


def lay_act_T(x):
    x = np.asarray(x)
    nt = x.shape[0]
    return np.ascontiguousarray(x.reshape(nt, -1, 128).transpose(2, 1, 0))


def unlay_act_T(xT):
    p, n, nt = xT.shape
    return np.ascontiguousarray(xT.transpose(2, 1, 0).reshape(nt, n * 128))


def make_consts():
    c = np.zeros((128, 384), np.float32)
    c[:, 0:128] = np.eye(128, dtype=np.float32)
    c[:, 128:256] = 1.0
    return c


ARENA_WORDS = 53000


def load_vecs(cx, vd, ncol):
    v = cx.ar.alloc("vecs", [ncol])
    cx.S.dma("sp", v, vd)
    return v


def build_TA(NT):
    nc = bass.Bass("TRN2", target_bir_lowering=False)
    with ExitStack() as es:
        cx = make_ctx(nc, es, ARENA_WORDS)
        S, ar = cx.S, cx.ar
        h_in = dram_in(nc, "h_in", [128, NK, NT])
        cd = dram_in(nc, "consts", [128, 384])
        vd = dram_in(nc, "vec", [128, 24])
        wg = dram_in(nc, "wg", [NF, 128, NK * 128])
        wu = dram_in(nc, "wu", [NF, 128, NK * 128])
        wd = dram_in(nc, "wd", [NK, 128, NF * 128])
        h_out = dram_out(nc, "h_out", [128, NK, NT])
        u_out = dram_out(nc, "u_out", [128, NK, NT], BF16)
        load_consts(cx, cd)
        vec = load_vecs(cx, vd, 24)
        hT = ar.alloc("hT", [NK, NT])
        for k in range(NK):
            S.dma("sp", hT[:, k, :], h_in[:, k, :])
        g = ar.alloc("gains", [24])
        S.ts("dve", g[:, 0:8], vec[:, 0:8], 32.0, ALU.mult)
        S.ts("dve", g[:, 8:16], vec[:, 8:16], 16.0, ALU.mult)
        S.ts("dve", g[:, 16:24], vec[:, 16:24], 32.0, ALU.mult)
        emit_ffn(cx, hT, NT, wg, wu, wd, g[:, 0:8], g[:, 8:16])
        uT = ar.alloc("uT", [NK, NT], BF16)
        emit_norm_to(cx, hT, NT, g[:, 16:24], uT)
        for k in range(NK):
            S.dma("sp", h_out[:, k, :], hT[:, k, :])
            S.dma("sp", u_out[:, k, :], uT[:, k, :])
        S.emit()
    return nc


CB = dict(r=0, k=64, v=128, wl=192, al=256, gl=320, vl=448, cq=512, ckv=896, kr=1152, krs=1248,
          hq=1344, hf=1472, hi=1600, hg=1728)
NCOLB = 1856
SM = dict(w_up=0, a_up=64, g_up=128, vres_up=192, uq=256, uqs=544, ukv=832)
NSM = 1088
VB = dict(mu_r=0, mu_k=1, mu_v=2, mu_wl=3, mu_al=4, mu_gl=5, mu_vl=6, w0=7, a0=8, k_k=9, k_a=10, r_k=11,
          gn_g=12, gn_b=13, v0=14, qg=15, kvg=18, lb=20, hng=24)
NVB = 25
CC = dict(maskAT=0, maskN=512, I64=1024, triH=1536, triA=1664, reset64=1792, reset32=2304, invf=2816, sgn=2817,
          sel96=2818)
NCB = 2920
C0 = float(np.exp(-0.5))
A_GN_EPS = 64e-5
TWO_PI = float(2 * np.pi)
PI = float(np.pi)


def make_constsB():
    c = np.zeros((128, NCB), np.float32)
    m = np.zeros((128, 128), np.float32)
    il = np.arange(64)[:, None]
    tl = np.arange(64)[None, :]
    for half in range(2):
        m[half * 64:(half + 1) * 64, 0:64] = (il < tl)
        m[half * 64:(half + 1) * 64, 64:128] = (il <= tl)
    c[:, 0:512] = np.tile(m, (1, 4))
    mn = (np.arange(64)[:, None] > np.arange(64)[None, :]).astype(np.float32)
    c[0:64, 512:1024] = np.tile(mn, (1, 8))
    c[0:64, 1024:1536] = np.tile(np.eye(64, dtype=np.float32), (1, 8))
    th = (np.arange(32)[:, None] <= np.arange(32)[None, :]).astype(np.float32)
    c[:, 1536:1664] = np.tile(np.tile(th, (4, 1)), (1, 4))
    c[:, 1664:1792] = (np.arange(128)[:, None] <= np.arange(128)[None, :])
    r64 = np.ones(512, np.float32)
    r64[::64] = 0
    r32 = np.ones(512, np.float32)
    r32[::32] = 0
    c[:, 1792:2304] = r64[None, :]
    c[:, 2304:2816] = r32[None, :]
    invf = (10000.0 ** (-np.arange(0, 32, 2, dtype=np.float32) / 32)).astype(np.float32)
    c[64:80, 2816] = invf
    c[80:96, 2816] = invf
    c[64:80, 2817] = -1.0
    c[80:96, 2817] = 1.0
    c[0:96, 2818 + 96] = 1.0
    return c


import os
PHASES = os.environ.get("MIX_PHASES", "rmh")


def emit_cumsum(S, src, bufA, bufB, rows, nch, C):
    v = lambda t: t[rows].rearrange("p (c t) -> p c t", c=nch)
    cur, bufs, i, s = src, [bufA, bufB], 0, 1
    while s < C:
        nxt = bufs[i % 2]
        S.copy("pool", v(nxt)[:, :, 0:s], v(cur)[:, :, 0:s])
        S.tt("dve", v(nxt)[:, :, s:C], v(cur)[:, :, s:C], v(cur)[:, :, 0:C - s], ALU.add)
        cur = nxt
        i += 1
        s *= 2
    return cur, bufs[i % 2]


def emit_mixer(cx, layer, SL, u_d, wB_d, wsm_d, vec_d, pos_d, cB_d, vf_in_d, vf_out_d, y_d):
    S, ar, ps = cx.S, cx.ar, cx.ps
    L0 = (layer == 0)
    ntile = SL // TT
    nblk = SL // 128
    m_all = ar.mark()
    SCALE = float(96 ** -0.5)
    wB = ar.alloc("wB", [NK, NCOLB], BF16)
    for k in range(NK):
        S.dma("pool", wB[:, k, :], wB_d[:, k, :])
    wsm = ar.alloc("wsm", [NSM])
    S.dma("sp", wsm, wsm_d)
    wuq = ar.alloc("wuq", [3, 96], BF16)
    wuqs = ar.alloc("wuqs", [3, 96], BF16)
    wukv = ar.alloc("wukv", [2, 128], BF16)
    S.copy("dve", wuq, wsm[:, SM["uq"]:SM["uq"] + 288].rearrange("p (a b) -> p a b", a=3))
    S.copy("dve", wuqs, wsm[:, SM["uqs"]:SM["uqs"] + 288].rearrange("p (a b) -> p a b", a=3))
    S.copy("dve", wukv, wsm[:, SM["ukv"]:SM["ukv"] + 256].rearrange("p (a b) -> p a b", a=2))
    w_up = wsm[0:64, SM["w_up"]:SM["w_up"] + 64]
    a_up = wsm[0:64, SM["a_up"]:SM["a_up"] + 64]
    g_up = wsm[:, SM["g_up"]:SM["g_up"] + 64]
    vres_up = wsm[0:32, SM["vres_up"]:SM["vres_up"] + 64]
    vec = ar.alloc("vecB", [NVB])
    S.dma("sp", vec, vec_d)
    cB = ar.alloc("cB", [NCB])
    S.dma("sp", cB, cB_d)
    triA = ar.alloc("triA", [128], BF16)
    S.copy("dve", triA, cB[:, CC["triA"]:CC["triA"] + 128])
    sel96 = ar.alloc("sel96", [97], BF16)
    S.copy("dve", sel96, cB[:, CC["sel96"]:CC["sel96"] + 97])
    maskAT = cB[:, CC["maskAT"]:CC["maskAT"] + 512]
    maskN = cB[0:64, CC["maskN"]:CC["maskN"] + 512]
    I64 = cB[0:64, CC["I64"]:CC["I64"] + 512]
    triH = cB[:, CC["triH"]:CC["triH"] + 128].rearrange("p (b t) -> p b t", b=4)
    reset64 = cB[:, CC["reset64"]:CC["reset64"] + 512]
    reset32 = cB[:, CC["reset32"]:CC["reset32"] + 512]
    invf = cB[:, CC["invf"]:CC["invf"] + 1]
    sgn = cB[:, CC["sgn"]:CC["sgn"] + 1]
    ones64 = cx.ones_f[0:64, 0:64]
    id64 = cx.ident[0:64, 0:64]

    def V(name, rows=128):
        return vec[0:rows, VB[name]:VB[name] + 1]

    dv = ar.alloc("dvec", [16])
    S.ts("dve", dv[:, 0:7], vec[:, 0:7], -1.0, ALU.mult, 1.0, ALU.add)
    S.ts("dve", dv[:, 7:8], vec[:, VB["k_a"]:VB["k_a"] + 1], -1.0, ALU.mult, 1.0, ALU.add)
    S.ts("dve", dv[:, 8:11], vec[:, VB["qg"]:VB["qg"] + 3], float(np.sqrt(384.0)), ALU.mult)
    S.ts("dve", dv[:, 11:13], vec[:, VB["kvg"]:VB["kvg"] + 2], 16.0, ALU.mult)
    S.ts("dve", dv[:, 15:16], vec[:, VB["hng"]:VB["hng"] + 1], float(np.sqrt(128.0)), ALU.mult)
    if L0:
        S.memset("dve", dv[:, 13:14], 0.0)
    else:
        le = ar.alloc("lbe", [8])
        S.act(le[:, 0:4], vec[:, VB["lb"]:VB["lb"] + 4], AF.Exp)
        S.reduce("dve", le[:, 4:5], le[:, 0:4], ALU.add)
        S.recip(le[:, 4:5], le[:, 4:5])
        S.reduce("dve", le[:, 5:6], le[:, 1:layer + 1], ALU.add)
        S.tt("dve", dv[:, 13:14], le[:, 5:6], le[:, 4:5], ALU.mult)
    S.ts("dve", dv[:, 14:15], dv[:, 13:14], -1.0, ALU.mult, 1.0, ALU.add)
    mpi = ar.alloc("mpi", [8])
    S.memset("dve", mpi, -PI)

    KT = ar.alloc("KT", [SL], BF16)
    Vaug = ar.alloc("Vaug", [nblk, 65], BF16)
    S.memset("pool", KT[96:97, :], 1.0)
    S.memset("pool", Vaug[:, :, 64:65], 1.0)
    kmax2 = ar.alloc("kmax2", [8])
    S.memset("dve", kmax2[96:97, :], 0.0)
    SV = ar.alloc("SV", [9, 64])
    S.memset("dve", SV[0:64, 0, :], 0.0)
    Sh = ar.alloc("Sh", [17, 128])
    S.memset("pool", Sh[:, 0, :], 0.0)
    shifted = [("r", 64), ("k", 64), ("v", 64), ("wl", 64), ("al", 64), ("gl", 128)] + ([] if L0 else [("vl", 32)])
    halo = ar.alloc("halo", [8])
    S.memset("dve", halo, 0.0)
    ut = [ar.alloc("ut%d" % i, [NK, TT], BF16) for i in range(2)]
    ya_o = ar.alloc("ya_o", [TT], BF16)
    yb_o = ar.alloc("yb_o", [TT], BF16)
    yc_o = ar.alloc("yc_o", [TT], BF16)
    gC = ar.alloc("gC", [16])
    gCh = ar.alloc("gCh", [16])
    sc = Scratch(ar, (ar.words - ar.off) // 512)

    rot = [0]

    def bank():
        b = ps[rot[0] % 5]
        rot[0] += 1
        return b

    def proj(off, M, utile):
        pb = bank()
        for k in range(NK):
            S.mm(pb[0:M, :], wB[:, k, off:off + M], utile[:, k, :], start=(k == 0), stop=(k == NK - 1))
        return pb

    S.dma("sp", ut[0], u_d[:, :, 0:TT])
    for it in range(ntile):
        c0 = it * TT
        u_t = ut[it % 2]
        S.mute = False
        if it + 1 < ntile:
            S.dma("sp", ut[(it + 1) % 2], u_d[:, :, c0 + TT:c0 + 2 * TT])
        S.mute = "r" not in PHASES
        sc.reset()
        A = lambda n=TT, dt=F32: sc.get([n], dt)
        rws = [A(), A()]
        tmpm = A()
        xr, xk, xwl, xal, xgl = A(), A(), A(), A(), A()
        xvf = A(TT + 64)
        xv = xvf[:, 64:64 + TT]
        xvl = A()
        cs, a_t, g_t, kk, bb, bon = A(), A(), A(), A(), A(), A()
        E1, E2, E3, E4 = A(), A(), A(), A()
        t1, t2 = A(), A()
        WL = sc.get([8, 64])
        RA = sc.get([8, 64])
        ARB = sc.get([8, 64])
        BK = sc.get([8, 2, 64])
        BKh = sc.get([8, 2, 64])
        Nm = [sc.get([8, 64]) for i in range(2)]
        Mm = [sc.get([8, 64]) for i in range(2)]
        Pm = [sc.get([8, 64]) for i in range(2)]
        BKhT = sc.get([8, 64])
        UV = sc.get([8, 64])
        Wsb = A(64)
        yT = xal
        vft = xwl
        mixed = {"r": xr, "k": xk, "v": xv, "wl": xwl, "al": xal, "gl": xgl, "vl": xvl}
        for gi, (nm, M) in enumerate(shifted):
            pb = proj(CB[nm], M, u_t)
            rw = rws[gi % 2]
            S.copy("act", rw[0:M], pb[0:M, :])
            mu = vec[0:M, gi:gi + 1]
            omu = dv[0:M, gi:gi + 1]
            S.ts("dve", tmpm[0:M, 1:TT], rw[0:M, 0:TT - 1], mu, ALU.mult)
            S.ts("dve", tmpm[0:M, 0:1], halo[0:M, gi:gi + 1], mu, ALU.mult)
            S.stt("dve", mixed[nm][0:M], rw[0:M], omu, tmpm[0:M], ALU.mult, ALU.add)
            S.copy("pool", halo[0:M, gi:gi + 1], rw[0:M, TT - 1:TT])
        S.memset("pool", xvf[0:64, 0:64], 0.0)
        R = slice(0, 64)
        S.act(xwl[R], xwl[R], AF.Tanh)
        pb = bank()
        S.mm(pb[R, :], w_up, xwl[R])
        S.act(t1[R], pb[R, :], AF.Sigmoid, bias=V("w0", 64))
        cs, t2 = emit_cumsum(S, t1, cs, t2, R, 8, 64)
        S.tt("dve", t2[R], cs[R], t1[R], ALU.subtract)
        S.act(E1[R], cs[R], AF.Exp, scale=-C0)
        S.act(E2[R], cs[R], AF.Exp, scale=C0)
        S.act(E3[R], t2[R], AF.Exp, scale=-C0)
        cs3 = cs[R].rearrange("p (c t) -> p c t", c=8)
        S.tt("dve", t2[R].rearrange("p (c t) -> p c t", c=8), cs3[:, :, 63:64].bcast([64, 8, 64]), cs3, ALU.subtract)
        S.act(E4[R], t2[R], AF.Exp, scale=-C0)
        S.act(gC[R, 0:8], cs3[:, :, 63:64].rearrange("p c o -> p (c o)"), AF.Exp, scale=-C0)
        pb = bank()
        S.mm(pb[R, :], a_up, xal[R])
        S.act(a_t[R], pb[R, :], AF.Sigmoid, bias=V("a0", 64))
        S.act(xgl, xgl, AF.Sigmoid)
        pb = bank()
        S.mm(pb[R, :], g_up, xgl)
        S.copy("act", g_t[R], pb[R, :])
        if L0:
            S.dma("sp", vf_out_d[:, c0:c0 + TT], xv[R])
        else:
            S.dma("sp", vft[R], vf_in_d[:, c0:c0 + TT])
            pb = bank()
            S.mm(pb[R, :], vres_up, xvl[0:32])
            S.act(t1[R], pb[R, :], AF.Sigmoid, bias=V("v0", 64))
            S.tt("dve", vft[R], vft[R], xv[R], ALU.subtract)
            S.tt("dve", vft[R], vft[R], t1[R], ALU.mult)
            S.tt("dve", xv[R], xv[R], vft[R], ALU.add)
        S.ts("dve", kk[R], xk[R], V("k_k", 64), ALU.mult)
        S.tt("dve", t1[R], kk[R], kk[R], ALU.mult)
        pb = bank()
        S.mm(pb[R, :], ones64, t1[R])
        S.act(t1[R], pb[R, :], AF.Sqrt)
        S.ts("dve", t1[R], t1[R], 1e-12, ALU.max)
        S.recip(t1[R], t1[R])
        S.tt("dve", kk[R], kk[R], t1[R], ALU.mult)
        S.ts("dve", t1[R], a_t[R], V("k_a", 64), ALU.mult, dv[R, 7:8], ALU.add)
        S.tt("dve", xk[R], xk[R], t1[R], ALU.mult)
        S.stt("dve", t1[R], xr[R], V("r_k", 64), xk[R], ALU.mult, ALU.mult)
        pb = bank()
        S.mm(pb[R, :], ones64, t1[R])
        S.tt("dve", bon[R], pb[R, :], xv[R], ALU.mult)
        S.tt("dve", bb[R], kk[R], a_t[R], ALU.mult)
        v3 = lambda t: t[R].rearrange("p (c t) -> p c t", c=8)
        S.stt("dve", WL[R], v3(kk), -1.0, v3(E3), ALU.mult, ALU.mult)
        S.tt("dve", RA[R], v3(xr), v3(E1), ALU.mult)
        S.tt("dve", BK[R, :, 0, :], v3(bb), v3(E2), ALU.mult)
        S.tt("dve", BK[R, :, 1, :], v3(xk), v3(E2), ALU.mult)
        S.tt("dve", BKh[R, :, 0, :], v3(bb), v3(E4), ALU.mult)
        S.tt("dve", BKh[R, :, 1, :], v3(xk), v3(E4), ALU.mult)
        f2 = lambda t, c: t[R, c].rearrange("p a b -> p (a b)")
        m4 = maskAT.rearrange("p (c h t) -> p c h t", c=4, h=2)
        for half in range(2):
            pb = bank()
            for cc in range(4):
                c = half * 4 + cc
                S.mm(pb[:, cc * 128:cc * 128 + 64], f2(BK, c), WL[R, c, :])
                S.mm(pb[:, cc * 128 + 64:(cc + 1) * 128], f2(BK, c), RA[R, c, :])
            p4 = pb.rearrange("p (c h t) -> p c h t", c=4, h=2)
            cs_ = slice(half * 4, half * 4 + 4)
            S.tt("dve", Mm[0][R, cs_, :], p4[R, :, 0, :], m4[R, :, 0, :], ALU.mult)
            S.tt("dve", ARB[R, cs_, :], p4[R, :, 1, :], m4[R, :, 1, :], ALU.mult)
            S.tt("dve", WL[64:128, cs_, :], p4[64:128, :, 0, :], m4[64:128, :, 0, :], ALU.mult)
            S.tt("dve", RA[64:128, cs_, :], p4[64:128, :, 1, :], m4[64:128, :, 1, :], ALU.mult)
        pb = bank()
        for c in range(8):
            S.mm(pb[R, c * 64:(c + 1) * 64], WL[R, c, :], BK[R, c, 0, :])
        S.tt("dve", Nm[0][R].rearrange("p a b -> p (a b)"), pb[R, :], maskN, ALU.mult)
        S.tt("pool", Pm[0][R].rearrange("p a b -> p (a b)"), Mm[0][R].rearrange("p a b -> p (a b)"), I64, ALU.add)
        cur = 0
        for rnd in range(5):
            Mc, Nc, Pc = Mm[cur], Nm[cur], Pm[cur]
            Mn, Nn, Pn = Mm[1 - cur], Nm[1 - cur], Pm[1 - cur]
            pbm = bank()
            pbn = bank()
            for c in range(8):
                S.mm(pbm[R, c * 64:(c + 1) * 64], Nc[R, c, :], Mc[R, c, :])
            for c in range(8):
                S.mm(pbn[R, c * 64:(c + 1) * 64], Mc[R, c, :], Nc[R, c, :])
            S.copy("act", Mn[R].rearrange("p a b -> p (a b)"), pbm[R, :])
            S.copy("dve", Nn[R].rearrange("p a b -> p (a b)"), pbn[R, :])
            pbp = bank()
            for c in range(8):
                S.mm(pbp[R, c * 64:(c + 1) * 64], Nn[R, c, :], Pc[R, c, :])
            S.tt("dve", Pn[R].rearrange("p a b -> p (a b)"), pbp[R, :], Pc[R].rearrange("p a b -> p (a b)"), ALU.add)
            cur = 1 - cur
        Pf = Pm[cur]
        pb = bank()
        for c in range(8):
            S.transpose(pb[:, c * 64:(c + 1) * 64], f2(BKh, c), id64)
        S.copy("act", BKhT.rearrange("p a b -> p (a b)"), pb)
        pb = bank()
        for c in range(8):
            S.transpose(pb[:, c * 64:(c + 1) * 64], xvf[R, c * 64:c * 64 + 128], id64)
        S.copy("dve", UV[64:128].rearrange("p a b -> p (a b)"), pb[64:128, :])
        S.copy("dve", SV[64:128, 0:8, :].rearrange("p a b -> p (a b)"), pb[64:128, :])
        ypb = ps[5]
        for c in range(8):
            pw = bank()
            S.mm(pw[R, 0:64], WL[:, c, :], SV[:, c, :])
            S.copy("act", Wsb[R], pw[R, 0:64])
            pu = bank()
            S.mm(pu[R, 0:64], Pf[R, c, :], Wsb[R])
            S.copy("dve", UV[R, c, :], pu[R, 0:64])
            S.mm(ypb[R, c * 64:(c + 1) * 64], SV[:, c, :], RA[:, c, :], start=True, stop=False)
            S.mm(ypb[R, c * 64:(c + 1) * 64], UV[R, c, :], ARB[R, c, :], start=False, stop=True)
            pn = bank()
            S.mm(pn[R, 0:64], BKhT[:, c, :], UV[:, c, :])
            S.stt("dve", SV[R, c + 1, :], SV[R, c, :], gC[R, c:c + 1], pn[R, 0:64], ALU.mult, ALU.add)
        S.copy("pool", SV[R, 0, :], SV[R, 8, :])
        S.copy("act", yT[R], ypb[R, :])
        pb = bank()
        S.mm(pb[R, :], ones64, yT[R])
        S.stt("dve", yT[R], pb[R, :], -1.0 / 64, yT[R], ALU.mult, ALU.add)
        S.act(t1[R], yT[R], AF.Square)
        pb = bank()
        S.mm(pb[R, :], ones64, t1[R])
        S.act(t1[R], pb[R, :], AF.Sqrt, bias=A_GN_EPS, scale=1.0 / 64)
        S.recip(t1[R], t1[R])
        S.tt("dve", yT[R], yT[R], t1[R], ALU.mult)
        S.ts("dve", yT[R], yT[R], V("gn_g", 64), ALU.mult, V("gn_b", 64), ALU.add)
        S.tt("dve", yT[R], yT[R], bon[R], ALU.add)
        S.tt("dve", ya_o[R], yT[R], g_t[R], ALU.mult)
        S.dma("sp", y_d[0:64, c0:c0 + TT], ya_o[R])

        S.mute = "m" not in PHASES
        sc.reset()
        A = lambda n=TT, dt=F32: sc.get([n], dt)
        t1, t2 = A(), A()
        cq_sb = sc.get([3, TT])
        sqb = [A(TT, BF16) for i in range(2)]
        rsq = A()
        cqn = sc.get([3, TT], BF16)
        ckvn = sc.get([2, TT], BF16)
        posi = sc.get([TT], I32)
        cos2, sin2 = A(), A()
        QT = A(TT, BF16)
        qr = A()
        qsq = A(TT, BF16)
        ksq = A(TT, BF16)
        Pb = [A(TT, BF16) for i in range(2)]
        rd = A()
        ob = A()
        def latent_norm(off, nch, dst, gcol, nfeat):
            ssb = ps[6]
            for j in range(nch):
                pb = proj(off + j * 128, 128, u_t)
                S.copy("act", cq_sb[:, j, :], pb)
                b = sqb[j % 2]
                S.act(b, pb, AF.Square)
                S.mm(ssb, cx.ones_b, b, start=(j == 0), stop=(j == nch - 1))
            S.act(rsq, ssb, AF.Sqrt, bias=float(nfeat * EPS))
            S.recip(rsq, rsq)
            for j in range(nch):
                S.stt("dve", dst[:, j, :], cq_sb[:, j, :], dv[:, gcol + j:gcol + j + 1], rsq, ALU.mult, ALU.mult)
        latent_norm(CB["cq"], 3, cqn, 8, 384)
        pq = bank()
        for j in range(3):
            S.mm(pq[0:96, :], wuq[:, j, :], cqn[:, j, :], start=(j == 0), stop=(j == 2))
        pqs = bank()
        for j in range(3):
            S.mm(pqs[0:96, :], wuqs[:, j, :], cqn[:, j, :], start=(j == 0), stop=(j == 2))
        RR = slice(64, 96)
        S.dma("sp", posi[RR], pos_d[:, c0:c0 + TT])
        S.copy("dve", t1[RR], posi[RR])
        S.ts("dve", t1[RR], t1[RR], invf[RR], ALU.mult)
        def sincos(dst, shift):
            S.ts("dve", t2[RR], t1[RR], 1.0 / TWO_PI, ALU.mult, shift, ALU.add)
            S.copy("dve", posi[RR], t2[RR])
            S.copy("dve", rd[RR], posi[RR])
            S.tt("dve", t2[RR], t2[RR], rd[RR], ALU.subtract)
            S.ts("dve", rd[RR], t2[RR], 0.0, ALU.is_lt)
            S.tt("dve", t2[RR], t2[RR], rd[RR], ALU.add)
            S.act(dst[RR], t2[RR], AF.Sin, bias=mpi[RR, 0:1], scale=TWO_PI)
        sincos(sin2, 0.5)
        S.ts("dve", sin2[RR], sin2[RR], sgn[RR], ALU.mult)
        sincos(cos2, 0.75)
        S.act(QT[0:64], pq[0:64, :], AF.Copy, scale=SCALE)
        S.tt("dve", qr[RR], pq[RR, :], cos2[RR], ALU.mult)
        S.tt("dve", t1[RR], pqs[RR, :], sin2[RR], ALU.mult)
        S.tt("dve", qr[RR], qr[RR], t1[RR], ALU.add)
        S.act(QT[RR], qr[RR], AF.Copy, scale=SCALE)
        S.act(qsq[0:64], pq[0:64, :], AF.Square, scale=SCALE)
        S.act(qsq[RR], qr[RR], AF.Square, scale=SCALE)
        latent_norm(CB["ckv"], 2, ckvn, 11, 256)
        pkv = bank()
        for j in range(2):
            S.mm(pkv, wukv[:, j, :], ckvn[:, j, :], start=(j == 0), stop=(j == 1))
        S.copy("act", KT[0:64, c0:c0 + TT], pkv[0:64, :])
        pkr = proj(CB["kr"], 96, u_t)
        pkrs = proj(CB["krs"], 96, u_t)
        S.tt("dve", t1[RR], pkr[RR, :], cos2[RR], ALU.mult)
        S.tt("dve", t2[RR], pkrs[RR, :], sin2[RR], ALU.mult)
        S.tt("dve", KT[RR, c0:c0 + TT], t1[RR], t2[RR], ALU.add)
        S.act(ksq[0:96], KT[0:96, c0:c0 + TT], AF.Square)
        pb = bank()
        S.mm(pb[0:97, :], sel96[0:96, :], ksq[0:96])
        S.reduce("dve", kmax2[96:97, 1:2], pb[96:97, :], ALU.max)
        S.tt("dve", kmax2[96:97, 0:1], kmax2[96:97, 0:1], kmax2[96:97, 1:2], ALU.max)
        pb = bank()
        S.mm(pb[0:97, :], sel96[0:96, :], qsq[0:96])
        S.ts("dve", t1[96:97], pb[96:97, :], kmax2[96:97, 0:1], ALU.mult)
        S.act(t1[96:97], t1[96:97], AF.Sqrt)
        S.ts("dve", QT[96:97], t1[96:97], -1.0, ALU.mult)
        pb = bank()
        for blk in range(4):
            for j in range(2):
                S.mm(pb[:, blk * 64:(blk + 1) * 64], ckvn[:, j, blk * 128:(blk + 1) * 128], wukv[:, j, 64:128],
                     start=(j == 0), stop=(j == 1))
        S.copy("act", Vaug[:, 4 * it:4 * it + 4, 0:64], pb[:, 0:256].rearrange("p (a b) -> p a b", a=4))
        ob_ps = ps[7]
        nkb = 4 * it + 4
        for kb in range(nkb):
            d = kb - 4 * it
            qlo = max(d, 0) * 128
            sp_ = ps[5 + kb % 2]
            pt = Pb[kb % 2]
            S.mm(sp_[:, qlo:TT], KT[0:97, kb * 128:(kb + 1) * 128], QT[0:97, qlo:TT])
            S.act(pt[:, qlo:TT], sp_[:, qlo:TT], AF.Exp)
            if d >= 0:
                S.tt("pool", pt[:, qlo:qlo + 128], pt[:, qlo:qlo + 128], triA, ALU.mult)
            S.mm(ob_ps[0:65, qlo:TT], Vaug[:, kb, :], pt[:, qlo:TT], start=(kb == 0), stop=(kb == nkb - 1))
        S.recip(rd[0:1], ob_ps[64:65, :])
        S.copy("act", ob[R], ob_ps[R, :])
        pb = bank()
        S.mm(pb[R, :], cx.ones_f[0:1, 0:64], rd[0:1])
        S.tt("dve", yb_o[R], ob[R], pb[R, :], ALU.mult)
        S.dma("sp", y_d[64:128, c0:c0 + TT], yb_o[R])

        S.mute = "h" not in PHASES
        sc.reset()
        A = lambda n=TT, dt=F32: sc.get([n], dt)
        t1, t2 = A(), A()
        qh, kx, clh, qb, gh = A(), A(), A(), A(), A()
        qtl, ktl, khat = A(), A(), A()
        Vh = sc.get([16, 128])
        KhT = sc.get([16, 128])
        KV = sc.get([16, 128])
        ATh = sc.get([16, 32])
        oh = A()
        sqb = [A(TT, BF16)]
        rsq = A()
        pb = proj(CB["hq"], 128, u_t)
        S.act(qh, pb, AF.Silu)
        pb = proj(CB["hf"], 128, u_t)
        S.act(t1, pb, AF.Sigmoid)
        S.ts("dve", t1, t1, dv[:, 14:15], ALU.mult, dv[:, 13:14], ALU.add)
        S.ts("dve", kx, t1, -1.0, ALU.mult, 1.0, ALU.add)
        S.ts("dve", t1, t1, 1e-6, ALU.max)
        S.act(t1, t1, AF.Ln)
        clh, t2 = emit_cumsum(S, t1, clh, t2, slice(0, 128), 16, 32)
        c16 = lambda t: t.rearrange("p (c t) -> p c t", c=16)
        S.act(t2, clh, AF.Exp)
        S.tt("dve", qb, qh, t2, ALU.mult)
        S.tt("dve", c16(t1), c16(clh), c16(clh)[:, :, 15:16].bcast([128, 16, 32]), ALU.subtract)
        S.act(t2, t1, AF.Exp)
        S.tt("dve", qtl, qh, t2, ALU.mult)
        S.act(t2, t1, AF.Exp, scale=-1.0)
        S.tt("dve", ktl, kx, t2, ALU.mult)
        S.tt("dve", c16(t1), c16(clh), c16(clh)[:, :, 31:32].bcast([128, 16, 32]), ALU.subtract)
        S.act(t2, t1, AF.Exp, scale=-1.0)
        S.tt("dve", khat, kx, t2, ALU.mult)
        S.act(gCh[:, 0:16], c16(clh)[:, :, 31:32].rearrange("p c o -> p (c o)"), AF.Exp)
        Q = slice(0, 32)
        for blk in range(4):
            pb = bank()
            for j in range(4):
                c = blk * 4 + j
                for k in range(NK):
                    S.mm(pb[Q, j * 128:(j + 1) * 128], u_t[:, k, c * 32:(c + 1) * 32],
                         wB[:, k, CB["hi"]:CB["hi"] + 128], start=(k == 0), stop=(k == NK - 1))
            S.copy("act", Vh[Q, blk * 4:blk * 4 + 4, :].rearrange("p a b -> p (a b)"), pb[Q, :])
        pb = proj(CB["hg"], 128, u_t)
        S.act(gh, pb, AF.Silu)
        for blk in range(4):
            pb = bank()
            for j in range(4):
                c = blk * 4 + j
                S.transpose(pb[Q, j * 128:(j + 1) * 128], khat[:, c * 32:(c + 1) * 32], cx.ident)
            S.copy("dve", KhT[Q, blk * 4:blk * 4 + 4, :].rearrange("p a b -> p (a b)"), pb[Q, :])
        for blk in range(4):
            pb = bank()
            for j in range(4):
                c = blk * 4 + j
                S.mm(pb[:, j * 128:(j + 1) * 128], KhT[Q, c, :], Vh[Q, c, :])
            S.copy("act" if blk % 2 == 0 else "dve", KV[:, 4 * blk:4 * blk + 4, :].rearrange("p a b -> p (a b)"), pb)
        pb = bank()
        for c in range(16):
            S.mm(pb[Q, c * 32:(c + 1) * 32], ktl[:, c * 32:(c + 1) * 32], qtl[:, c * 32:(c + 1) * 32])
        S.tt("dve", ATh[Q], pb[Q, :].rearrange("p (c t) -> p c t", c=16),
             triH[Q, 0:1, :].bcast([32, 16, 32]), ALU.mult)
        for c in range(16):
            S.stt("dve", Sh[:, c + 1, :], Sh[:, c, :], gCh[:, c:c + 1], KV[:, c, :], ALU.mult, ALU.add)
        ohp = ps[6]
        for c in range(16):
            S.mm(ohp[:, c * 32:(c + 1) * 32], Sh[:, c, :], qb[:, c * 32:(c + 1) * 32], start=True, stop=False)
            S.mm(ohp[:, c * 32:(c + 1) * 32], Vh[Q, c, :], ATh[Q, c, :], start=False, stop=True)
        S.copy("pool", Sh[:, 0, :], Sh[:, 16, :])
        S.copy("act", oh, ohp)
        S.act(sqb[0], ohp, AF.Square)
        pb = bank()
        S.mm(pb, cx.ones_b, sqb[0])
        S.act(rsq, pb, AF.Sqrt, bias=float(128 * EPS))
        S.recip(rsq, rsq)
        S.stt("dve", oh, oh, dv[:, 15:16], rsq, ALU.mult, ALU.mult)
        S.tt("dve", yc_o, oh, gh, ALU.mult)
        S.dma("sp", y_d[128:192, c0:c0 + TT], yc_o[0:64])
    S.mute = False
    if cx.dbg is not None:
        for nm, t in (("QT", QT), ("cos2", cos2), ("sin2", sin2), ("rd", rd), ("ob", ob), ("qr", qr)):
            cx.dbg[nm] = t
        cx.dbg["KT"] = KT
        cx.dbg["Vaug"] = Vaug
    S.barrier()
    cx.scr_peak = sc.peak
    ar.release(m_all)


DEBUG_B = bool(int(os.environ.get("DEBUG_B", "0")))


def build_B(layer, SL):
    nc = bass.Bass("TRN2", target_bir_lowering=False)
    with ExitStack() as es:
        cx = make_ctx(nc, es, ARENA_WORDS)
        u_d = dram_in(nc, "u_full", [128, NK, SL], BF16)
        cd = dram_in(nc, "consts", [128, 384])
        wB_d = dram_in(nc, "wB", [128, NK, NCOLB])
        wsm_d = dram_in(nc, "wsm", [128, NSM])
        vec_d = dram_in(nc, "vecB", [128, NVB])
        pos_d = dram_in(nc, "pos", [32, SL], I32)
        cB_d = dram_in(nc, "cB", [128, NCB])
        y_d = dram_out(nc, "y_out", [192, SL], BF16)
        if layer == 0:
            vf_in, vf_out = None, dram_out(nc, "vf_out", [64, SL])
        else:
            vf_in, vf_out = dram_in(nc, "vf_in", [64, SL]), None
        load_consts(cx, cd)
        if DEBUG_B:
            cx.dbg = {}
        emit_mixer(cx, layer, SL, u_d, wB_d, wsm_d, vec_d, pos_d, cB_d, vf_in, vf_out, y_d)
        if DEBUG_B:
            for nm, t in cx.dbg.items():
                shp = [128] + list(t.ap.shape[1:])
                od = dram_out(nc, "dbg_" + nm, shp, t.ap.dtype)
                cx.S.dma("sp", od, t)
        cx.S.emit()
        print("B stats", cx.S.stats, "scratch peak", cx.scr_peak)
    return nc


def lay_rows(w, nch):
    w = np.asarray(w, np.float32)
    return np.ascontiguousarray(w.reshape(nch, 128, -1).transpose(1, 0, 2))


def prep_B_core(inp, l, c, SL):
    f32 = np.float32
    w_in = np.asarray(inp["w_in"][l], f32)
    hd, hf_ = c // 2, c % 2
    perm = np.concatenate([np.arange(hf_ * 64, hf_ * 64 + 64), np.arange((1 - hf_) * 64, (1 - hf_) * 64 + 64)])
    W = np.zeros((D, NCOLB), f32)

    def put(name, cols):
        W[:, CB[name]:CB[name] + cols.shape[1]] = cols
    put("r", w_in[:, c * 64:(c + 1) * 64])
    put("k", w_in[:, 512 + c * 64:512 + (c + 1) * 64])
    put("v", w_in[:, 1024 + c * 64:1024 + (c + 1) * 64])
    put("wl", w_in[:, 1536:1600])
    put("al", w_in[:, 1600:1664])
    put("gl", w_in[:, 1664:1792])
    if l > 0:
        put("vl", np.asarray(inp["rwkv_vres_down"][l - 1], f32))
    put("cq", w_in[:, OFF_CQ:OFF_CQ + 384])
    put("ckv", w_in[:, OFF_CKV:OFF_CKV + 256])
    kr = w_in[:, OFF_KR:OFF_KR + 32]
    W[:, CB["kr"] + 64:CB["kr"] + 96] = kr
    W[:, CB["krs"] + 64:CB["krs"] + 80] = kr[:, 16:32]
    W[:, CB["krs"] + 80:CB["krs"] + 96] = kr[:, 0:16]
    put("hq", w_in[:, OFF_HQ + hd * 128:OFF_HQ + (hd + 1) * 128])
    put("hf", w_in[:, OFF_HF + hd * 128:OFF_HF + (hd + 1) * 128])
    put("hi", w_in[:, OFF_HI + hd * 128:OFF_HI + (hd + 1) * 128][:, perm])
    put("hg", w_in[:, OFF_HG + hd * 128:OFF_HG + (hd + 1) * 128][:, perm])
    wB = lay_rows(W, NK)
    wsm = np.zeros((128, NSM), f32)
    hs = slice(c * 64, (c + 1) * 64)
    wsm[0:64, SM["w_up"]:SM["w_up"] + 64] = np.asarray(inp["rwkv_w_up"][l], f32)[:, hs]
    wsm[0:64, SM["a_up"]:SM["a_up"] + 64] = np.asarray(inp["rwkv_a_up"][l], f32)[:, hs]
    wsm[:, SM["g_up"]:SM["g_up"] + 64] = np.asarray(inp["rwkv_g_up"][l], f32)[:, hs]
    if l > 0:
        wsm[0:32, SM["vres_up"]:SM["vres_up"] + 64] = np.asarray(inp["rwkv_vres_up"][l - 1], f32)[:, hs]
    uq = np.asarray(inp["mla_w_uq"][l], f32)[:, c * 96:(c + 1) * 96]
    uqs = np.concatenate([uq[:, 0:64], uq[:, 80:96], uq[:, 64:80]], axis=1)
    wsm[:, SM["uq"]:SM["uq"] + 288] = lay_rows(uq, 3).reshape(128, 288)
    wsm[:, SM["uqs"]:SM["uqs"] + 288] = lay_rows(uqs, 3).reshape(128, 288)
    ukv = np.asarray(inp["mla_w_ukv"][l], f32)[:, c * 128:(c + 1) * 128]
    wsm[:, SM["ukv"]:SM["ukv"] + 256] = lay_rows(ukv, 2).reshape(128, 256)
    vec = np.zeros((128, NVB), f32)
    mu = np.asarray(inp["rwkv_mu"][l], f32)
    vec[0:64, VB["mu_r"]] = mu[c * 64:(c + 1) * 64]
    vec[0:64, VB["mu_k"]] = mu[512 + c * 64:512 + (c + 1) * 64]
    vec[0:64, VB["mu_v"]] = mu[1024 + c * 64:1024 + (c + 1) * 64]
    vec[0:64, VB["mu_wl"]] = mu[1536:1600]
    vec[0:64, VB["mu_al"]] = mu[1600:1664]
    vec[:, VB["mu_gl"]] = mu[1664:1792]
    if l > 0:
        vec[0:32, VB["mu_vl"]] = np.asarray(inp["rwkv_vres_mu"][l - 1], f32)
        vec[0:64, VB["v0"]] = np.asarray(inp["rwkv_v0"][l - 1], f32)[hs]
    for nm, key in (("w0", "rwkv_w0"), ("a0", "rwkv_a0"), ("k_k", "rwkv_k_k"), ("k_a", "rwkv_k_a"),
                    ("gn_g", "rwkv_gn_g"), ("gn_b", "rwkv_gn_b")):
        vec[0:64, VB[nm]] = np.asarray(inp[key][l], f32)[hs]
    vec[0:64, VB["r_k"]] = np.asarray(inp["rwkv_r_k"][l], f32)[c]
    vec[:, VB["qg"]:VB["qg"] + 3] = lay_vec(inp["mla_q_norm_g"][l])
    vec[:, VB["kvg"]:VB["kvg"] + 2] = lay_vec(inp["mla_kv_norm_g"][l])
    vec[:, VB["lb"]:VB["lb"] + 4] = np.asarray(inp["hgrn_lower_bounds"], f32)[:, hd * 128:(hd + 1) * 128].T
    vec[:, VB["hng"]] = np.asarray(inp["hgrn_norm_g"][l], f32)[perm]
    pos = np.ascontiguousarray(np.broadcast_to(np.asarray(inp["positions"]).reshape(1, -1)[:, :SL], (32, SL))).astype(np.int32)
    return dict(wB=wB, wsm=wsm, vecB=vec, pos=pos)


def emit_merge(cx, hT, uT, NT, y_d, wgate_d, wouts_d, wo_d, g_post):
    S, ar, ps = cx.S, cx.ar, cx.ps
    m0 = ar.mark()
    wouts = ar.alloc("wouts", [12, D], BF16)
    for j in range(12):
        S.dma("pool", wouts[:, j, :], wouts_d[:, j, :])
    wo = ar.alloc("wo", [NK, D], BF16)
    for k in range(NK):
        S.dma("pool", wo[:, k, :], wo_d[:, k, :])
    wgb = [ar.alloc("wgateb%d" % i, [NK, 384], BF16) for i in range(2)]
    yt = ar.alloc("yt", [12, TT], BF16)
    merged = ar.alloc("merged", [NK, TT], BF16)
    sig = [ar.alloc("sig%d" % i, [TT]) for i in range(3)]
    mt = [ar.alloc("mt%d" % i, [TT]) for i in range(2)]
    z = ar.alloc("z", [NK, TT])
    sq = [ar.alloc("msq%d" % i, [TT], BF16) for i in range(2)]
    rs = ar.alloc("mrs", [TT])
    tmp = [ar.alloc("mtmp%d" % i, [TT]) for i in range(2)]
    ntile = NT // TT
    nld = [0]

    def ld_gate(d):
        S.dma("pool", wgb[nld[0] % 2], wgate_d[d].rearrange("p (k m) -> p k m", k=NK))
        nld[0] += 1

    ld_gate(0)
    nsq = 0
    for t in range(ntile):
        c0 = t * TT
        for j in range(12):
            S.dma("sp", yt[:, j, :], y_d[:, j, c0:c0 + TT])
        for d in range(NK):
            wgt = wgb[(t * NK + d) % 2]
            if t * NK + d + 1 < ntile * NK:
                ld_gate((d + 1) % NK)
            for j in range(3):
                pj = ps[j]
                for k in range(4):
                    S.mm(pj, wouts[:, 4 * j + k, d * 128:(d + 1) * 128], yt[:, 4 * j + k, :], start=(k == 0), stop=(k == 3))
                gj = ps[3 + j]
                for k in range(NK):
                    S.mm(gj, wgt[:, k, j * 128:(j + 1) * 128], uT[:, k, c0:c0 + TT], start=(k == 0), stop=(k == NK - 1))
                S.act(sig[j], gj, AF.Sigmoid)
            S.tt("dve", mt[0], sig[0], ps[0], ALU.mult)
            S.tt("dve", mt[1], sig[1], ps[1], ALU.mult)
            S.tt("pool", mt[0], mt[0], mt[1], ALU.add)
            S.tt("dve", mt[1], sig[2], ps[2], ALU.mult)
            S.tt("pool", merged[:, d, :], mt[0], mt[1], ALU.add)
        for d in range(NK):
            zp = ps[6]
            for k in range(NK):
                S.mm(zp, wo[:, k, d * 128:(d + 1) * 128], merged[:, k, :], start=(k == 0), stop=(k == NK - 1))
            S.act(z[:, d, :], zp, AF.Copy)
            b = sq[nsq % 2]
            nsq += 1
            S.act(b, zp, AF.Square)
            S.mm(ps[7], cx.ones_b, b, start=(d == 0), stop=(d == NK - 1))
        emit_rstd(cx, ps[7], rs, D)
        for d in range(NK):
            tq = tmp[d % 2]
            S.stt("dve", tq, z[:, d, :], g_post[:, d:d + 1], rs, ALU.mult, ALU.mult)
            S.tt("pool", hT[:, d, c0:c0 + TT], hT[:, d, c0:c0 + TT], tq, ALU.add)
    S.barrier()
    ar.release(m0)


def lay_gate(w_in_l):
    g = np.asarray(w_in_l, np.float32)[:, OFF_GATE:OFF_GATE + 3 * D]
    g = g.reshape(NK, 128, 3, NK, 128)
    return np.ascontiguousarray(g.transpose(3, 1, 0, 2, 4).reshape(NK, 128, NK * 384))


def build_T(NT, merge, nxt):
    nc = bass.Bass("TRN2", target_bir_lowering=False)
    with ExitStack() as es:
        cx = make_ctx(nc, es, ARENA_WORDS)
        S, ar = cx.S, cx.ar
        nv = 24 * (int(merge) + int(nxt))
        h_in = dram_in(nc, "h_in", [128, NK, NT])
        cd = dram_in(nc, "consts", [128, 384])
        vd = dram_in(nc, "vec", [128, nv])
        load_consts(cx, cd)
        vec = load_vecs(cx, vd, nv)
        g = ar.alloc("gains", [nv])
        hT = ar.alloc("hT", [NK, NT])
        for k in range(NK):
            S.dma("sp", hT[:, k, :], h_in[:, k, :])
        uT = ar.alloc("uT", [NK, NT], BF16)
        o = 0
        if merge:
            u_in = dram_in(nc, "u_in", [128, NK, NT], BF16)
            y_in = dram_in(nc, "y_in", [128, 12, NT], BF16)
            wgate = dram_in(nc, "wgate", [NK, 128, NK * 384])
            wouts = dram_in(nc, "wouts", [128, 12, D])
            wo = dram_in(nc, "wo", [128, NK, D])
            wg2 = dram_in(nc, "wg2", [NF, 128, NK * 128])
            wu2 = dram_in(nc, "wu2", [NF, 128, NK * 128])
            wd2 = dram_in(nc, "wd2", [NK, 128, NF * 128])
            for k in range(NK):
                S.dma("sp", uT[:, k, :], u_in[:, k, :])
            S.ts("dve", g[:, 0:8], vec[:, 0:8], 32.0, ALU.mult)
            S.ts("dve", g[:, 8:16], vec[:, 8:16], 32.0, ALU.mult)
            S.ts("dve", g[:, 16:24], vec[:, 16:24], 16.0, ALU.mult)
            emit_merge(cx, hT, uT, NT, y_in, wgate, wouts, wo, g[:, 0:8])
            emit_ffn(cx, hT, NT, wg2, wu2, wd2, g[:, 8:16], g[:, 16:24])
            o = 24
        h_out = dram_out(nc, "h_out", [128, NK, NT])
        if nxt:
            wg1 = dram_in(nc, "wg1", [NF, 128, NK * 128])
            wu1 = dram_in(nc, "wu1", [NF, 128, NK * 128])
            wd1 = dram_in(nc, "wd1", [NK, 128, NF * 128])
            u_out = dram_out(nc, "u_out", [128, NK, NT], BF16)
            S.ts("dve", g[:, o:o + 8], vec[:, o:o + 8], 32.0, ALU.mult)
            S.ts("dve", g[:, o + 8:o + 16], vec[:, o + 8:o + 16], 16.0, ALU.mult)
            S.ts("dve", g[:, o + 16:o + 24], vec[:, o + 16:o + 24], 32.0, ALU.mult)
            emit_ffn(cx, hT, NT, wg1, wu1, wd1, g[:, o:o + 8], g[:, o + 8:o + 16])
            emit_norm_to(cx, hT, NT, g[:, o + 16:o + 24], uT)
            for k in range(NK):
                S.dma("sp", u_out[:, k, :], uT[:, k, :])
        for k in range(NK):
            S.dma("sp", h_out[:, k, :], hT[:, k, :])
        S.emit()
    return nc


_PROG = {}


def _prog(key, fn):
    if key not in _PROG:
        _PROG[key] = fn()
    return _PROG[key]


def _run(nc, ins):
    res = run_bass_kernel_spmd(nc, ins, core_ids=list(range(NCORES)))
    return res.results


def kernel_impl(inp, SL, depth):
    NT = SL // NCORES
    consts = make_consts()
    cB = make_constsB()
    x = np.asarray(inp["x"], np.float32).reshape(SL, D)
    hs = [lay_act_T(x[c * NT:(c + 1) * NT]) for c in range(NCORES)]
    us = None
    vfirst = None

    def ffn_w(prefix, l, tag):
        return {"wg" + tag: lay_w_gu(inp[prefix + "_w_gate"][l]), "wu" + tag: lay_w_gu(inp[prefix + "_w_up"][l]),
                "wd" + tag: lay_w_d(inp[prefix + "_w_down"][l])}

    def vecs(names_l):
        return np.concatenate([lay_vec(inp[n][l]) for n, l in names_l], axis=1)

    nc = _prog(("T", NT, False, True), lambda: build_T(NT, False, True))
    com = dict(consts=consts, vec=vecs([("ffn1_pre_g", 0), ("ffn1_post_g", 0), ("mix_pre_g", 0)]))
    com.update(ffn_w("ffn1", 0, "1"))
    res = _run(nc, [dict(com, h_in=hs[c]) for c in range(NCORES)])
    hs = [np.asarray(r["h_out"]) for r in res]
    us = [np.asarray(r["u_out"]) for r in res]
    for l in range(depth):
        u_full = np.ascontiguousarray(np.concatenate(us, axis=2))
        ncb = _prog(("B", l if l < 2 else l, SL), lambda: build_B(l, SL))
        ins = []
        for c in range(NCORES):
            d = prep_B_core(inp, l, c, SL)
            d.update(u_full=u_full, consts=consts, cB=cB)
            if l > 0:
                d["vf_in"] = vfirst[c]
            ins.append(d)
        res = _run(ncb, ins)
        if l == 0:
            vfirst = [np.asarray(r["vf_out"]) for r in res]
        ys = [np.asarray(r["y_out"]) for r in res]
        y_ins = []
        for j in range(NCORES):
            yi = np.zeros((128, 12, NT), dtype=ys[0].dtype)
            for br in range(3):
                for q in range(4):
                    for half in range(2):
                        yi[half * 64:(half + 1) * 64, br * 4 + q, :] = ys[2 * q + half][br * 64:(br + 1) * 64, j * NT:(j + 1) * NT]
            y_ins.append(yi)
        last = (l == depth - 1)
        nct = _prog(("T", NT, True, not last), lambda: build_T(NT, True, not last))
        names = [("mix_post_g", l), ("ffn2_pre_g", l), ("ffn2_post_g", l)]
        if not last:
            names += [("ffn1_pre_g", l + 1), ("ffn1_post_g", l + 1), ("mix_pre_g", l + 1)]
        com = dict(consts=consts, vec=vecs(names), wgate=lay_gate(inp["w_in"][l]),
                   wouts=np.concatenate([lay_rows(inp["rwkv_out"][l], 4), lay_rows(inp["mla_out"][l], 4),
                                         lay_rows(inp["hgrn_out"][l], 4)], axis=1),
                   wo=lay_rows(inp["w_o"][l], NK))
        com.update(ffn_w("ffn2", l, "2"))
        if not last:
            com.update(ffn_w("ffn1", l + 1, "1"))
        res = _run(nct, [dict(com, h_in=hs[c], u_in=us[c], y_in=y_ins[c]) for c in range(NCORES)])
        hs = [np.asarray(r["h_out"]) for r in res]
        if not last:
            us = [np.asarray(r["u_out"]) for r in res]
    out = np.concatenate([unlay_act_T(h) for h in hs], axis=0)
    return out.reshape(1, SL, D).astype(np.float32)


def kernel(**inputs):
    return kernel_impl(inputs, 16384, 4)
```

```python
import numpy as np
from contextlib import ExitStack
import ml_dtypes
import concourse.bass as bass
import concourse.mybir as mybir
from concourse.bass_utils import run_bass_kernel_spmd

F32 = mybir.dt.float32
BF16 = mybir.dt.bfloat16
I32 = mybir.dt.int32
AF = mybir.ActivationFunctionType
ALU = mybir.AluOpType
AX = mybir.AxisListType

NCORES = 8
D = 1024
DFF = 2816
NF = DFF // 128
NK = D // 128
EPS = 1e-6
A_COLS = 1792
N_IN = 7584
OFF_CQ, OFF_CKV, OFF_KR = 1792, 2176, 2432
OFF_HQ, OFF_HF, OFF_HI, OFF_HG, OFF_GATE = 2464, 2976, 3488, 4000, 4512
TT = 512


class T:
    __slots__ = ("ap", "keys")

    def __init__(self, ap, keys):
        self.ap = ap
        self.keys = (keys,) if isinstance(keys, str) else tuple(keys)

    def __getitem__(self, idx):
        return T(self.ap[idx], self.keys)

    def k(self, *keys):
        return T(self.ap, keys)

    def rearrange(self, pat, **kw):
        return T(self.ap.rearrange(pat, **kw), self.keys)

    def bcast(self, shape):
        return T(self.ap.to_broadcast(list(shape)), self.keys)


def _keys(*xs):
    ks = []
    for x in xs:
        if isinstance(x, T):
            ks.extend(x.keys)
    return ks


def _ap(x):
    return x.ap if isinstance(x, T) else x


import os as _os
import sys
MAXOPS = int(_os.environ.get("CUTOPS", "100000000"))
EPOCH_KEY = "__epoch__"
SEM_EPOCH = 30000
NDSEM = 6


class Sched:
    COMPUTE = ("pe", "act", "dve", "pool")

    def __init__(self, nc):
        self.nc = nc
        self.ops = []
        self.lines = []

    mute = False

    def add(self, eng, fn, reads, writes, dma=False):
        if self.mute or len(self.ops) >= MAXOPS:
            return
        self.lines.append(sys._getframe(2).f_lineno)
        self.ops.append((eng, fn, tuple(reads) + (EPOCH_KEY,), tuple(writes), dma))

    def barrier(self):
        o = self.bar_tile.ap
        self.lines.append(0)
        self.ops.append(("dve", lambda e: e.memset(o, 0.0), (), (EPOCH_KEY,) + self.bar_tile.keys, False))

    def mm(self, out, lhsT, rhs, start=True, stop=True, **kw):
        o, l, r = out.ap, lhsT.ap, rhs.ap
        self.add("pe", lambda e: e.matmul(o, lhsT=l, rhs=r, start=start, stop=stop, **kw),
                 _keys(lhsT, rhs), _keys(out))

    def transpose(self, out, in_, ident):
        self.mm(out, in_, ident)

    def act(self, out, in_, func, bias=None, scale=None, accum=None):
        o, i = out.ap, in_.ap
        kw = {}
        if bias is not None:
            kw["bias"] = _ap(bias)
        if scale is not None:
            kw["scale"] = _ap(scale)
        if accum is not None:
            kw["accum_out"] = _ap(accum)
        self.add("act", lambda e: e.activation(out=o, in_=i, func=func, **kw),
                 _keys(in_, bias, scale), _keys(out, accum))

    def tt(self, eng, out, a, b, op):
        o, x, y = out.ap, a.ap, b.ap
        self.add(eng, lambda e: e.tensor_tensor(out=o, in0=x, in1=y, op=op), _keys(a, b), _keys(out))

    def ts(self, eng, out, a, s1, op0, s2=None, op1=None):
        o, x = out.ap, a.ap
        s1a, s2a = _ap(s1), _ap(s2)
        if op1 is None:
            self.add(eng, lambda e: e.tensor_scalar(out=o, in0=x, scalar1=s1a, scalar2=None, op0=op0),
                     _keys(a, s1), _keys(out))
        else:
            self.add(eng, lambda e: e.tensor_scalar(out=o, in0=x, scalar1=s1a, scalar2=s2a, op0=op0, op1=op1),
                     _keys(a, s1, s2), _keys(out))

    def stt(self, eng, out, a, scalar, b, op0, op1):
        o, x, y, s = out.ap, a.ap, b.ap, _ap(scalar)
        eng = "dve"
        self.add(eng, lambda e: e.scalar_tensor_tensor(out=o, in0=x, scalar=s, in1=y, op0=op0, op1=op1),
                 _keys(a, scalar, b), _keys(out))

    def copy(self, eng, out, in_):
        o, i = out.ap, in_.ap
        if eng == "act":
            self.add("act", lambda e: e.activation(out=o, in_=i, func=AF.Copy), _keys(in_), _keys(out))
        else:
            self.add(eng, lambda e: e.tensor_copy(out=o, in_=i), _keys(in_), _keys(out))

    def memset(self, eng, out, val):
        o = out.ap
        self.add(eng, lambda e: e.memset(o, val), (), _keys(out))

    def scan(self, out, d0, d1, init, op0, op1):
        o, a, b = out.ap, d0.ap, d1.ap
        self.add("dve", lambda e: e.tensor_tensor_scan(out=o, data0=a, data1=b, initial=init, op0=op0, op1=op1),
                 _keys(d0, d1), _keys(out))

    def reduce(self, eng, out, in_, op, axis=AX.X):
        o, i = out.ap, in_.ap
        self.add(eng, lambda e: e.tensor_reduce(out=o, in_=i, axis=axis, op=op), _keys(in_), _keys(out))

    def recip(self, out, in_):
        o, i = out.ap, in_.ap
        self.add("dve", lambda e: e.reciprocal(out=o, in_=i), _keys(in_), _keys(out))

    def dma(self, q, out, in_):
        o, i = out.ap, in_.ap
        self.add(q, lambda e: e.dma_start(out=o, in_=i), _keys(in_), _keys(out), dma=True)

    def emit(self):
        nc = self.nc
        ops = self.ops
        n = len(ops)
        last_w = {}
        rd_eng = {}
        rd_dma = {}
        deps = [None] * n
        signal = [False] * n
        for i, (eng, fn, rd, wr, dma) in enumerate(ops):
            cand = []
            for k in rd:
                j = last_w.get(k)
                if j is not None:
                    cand.append((j, 0))
            for k in wr:
                j = last_w.get(k)
                if j is not None:
                    cand.append((j, 1))
                for j in rd_eng.get(k, {}).values():
                    cand.append((j, 2))
                for j in rd_dma.get(k, ()):
                    cand.append((j, 2))
            best = {}
            dl = set()
            for j, kind in cand:
                if j == i:
                    continue
                ej, _, _, _, dj = ops[j]
                if dj:
                    dl.add(j)
                    continue
                if (not dma) and ej == eng:
                    if eng == "pe" or kind != 0:
                        continue
                if best.get(ej, -1) < j:
                    best[ej] = j
            dd = set(best.values()) | dl
            deps[i] = dd
            for j in dd:
                signal[j] = True
            for k in rd:
                if dma:
                    rd_dma.setdefault(k, []).append(i)
                else:
                    rd_eng.setdefault(k, {})[eng] = i
            for k in wr:
                last_w[k] = i
                rd_eng[k] = {}
                rd_dma[k] = []
        cnt = {e: 0 for e in self.COMPUTE}
        sigval = [None] * n
        dcnt = {}
        for i, (eng, fn, rd, wr, dma) in enumerate(ops):
            if dma:
                q = dcnt.get(eng, 0)
                dcnt[eng] = q + 1
                sigval[i] = ("d", eng, q % NDSEM, 16 * (q // NDSEM + 1), q)
            elif signal[i]:
                cnt[eng] += 1
                sigval[i] = ("c", eng, cnt[eng])
        self.stats = dict(n=n, cnt=dict(cnt), dcnt=dict(dcnt))
        by_eng = {}
        for i, op in enumerate(ops):
            by_eng.setdefault(op[0], []).append(i)
        with ExitStack() as es:
            csems = {}
            for e in self.COMPUTE:
                ne = max(1, (cnt[e] + SEM_EPOCH - 1) // SEM_EPOCH)
                csems[e] = [es.enter_context(nc.semaphore("c_%s_%d" % (e, t))) for t in range(ne)]
            dsems = {}
            for e in dcnt:
                dsems[e] = [es.enter_context(nc.semaphore("d_%s_%d" % (e, t))) for t in range(NDSEM)]
            block = es.enter_context(nc.Block())

            def make(engname):
                mine = by_eng.get(engname, [])

                def body(e):
                    cw = {x: 0 for x in self.COMPUTE}
                    dw = {}
                    for i in mine:
                        eng, fn, rd, wr, dma = ops[i]
                        if dma:
                            _, _, slot, val, q = sigval[i]
                            if q >= NDSEM:
                                key = (eng, slot)
                                if dw.get(key, 0) < val - 16:
                                    e.wait_ge(dsems[eng][slot], val - 16)
                                    dw[key] = val - 16
                        for j in sorted(deps[i]):
                            sv = sigval[j]
                            if sv[0] == "c":
                                _, ej, c = sv
                                if cw[ej] >= c:
                                    continue
                                cw[ej] = c
                                e.wait_ge(csems[ej][(c - 1) // SEM_EPOCH], (c - 1) % SEM_EPOCH + 1)
                            else:
                                _, ej, slot, val, q = sv
                                key = (ej, slot)
                                if dw.get(key, 0) >= val:
                                    continue
                                dw[key] = val
                                e.wait_ge(dsems[ej][slot], val)
                        inst = fn(e)
                        sv = sigval[i]
                        if sv is not None:
                            if sv[0] == "c":
                                c = sv[2]
                                inst.then_inc(csems[eng][(c - 1) // SEM_EPOCH], 1)
                            else:
                                inst.then_inc(dsems[eng][sv[2]], 16)
                    if engname in dcnt:
                        tot = dcnt[engname]
                        for slot in range(NDSEM):
                            uses = (tot - slot + NDSEM - 1) // NDSEM if tot > slot else 0
                            if uses > 0 and dw.get((engname, slot), 0) < 16 * uses:
                                e.wait_ge(dsems[engname][slot], 16 * uses)
                return body

            block.tensor(make("pe"))
            block.scalar(make("act"))
            block.vector(make("dve"))
            block.gpsimd(make("pool"))
            block.sync(make("sp"))


class Arena:
    def __init__(self, nc, es, words):
        self.t = es.enter_context(nc.sbuf_tensor("arena", [128, words], F32))
        self.words = words
        self.off = 0
        self.uid = 0

    def alloc(self, name, free_shape, dtype=F32, nslots=None):
        nel = int(np.prod(free_shape))
        w = (nel + 1) // 2 if dtype == BF16 else nel
        w = (w + 7) // 8 * 8
        assert self.off + w <= self.words, ("SBUF arena overflow", name, self.off, w, self.words)
        ap = self.t[:, self.off:self.off + w]
        if dtype != F32:
            ap = ap.bitcast(dtype)
        ap = ap[:, 0:nel]
        if len(free_shape) == 2:
            ap = ap.rearrange("p (a b) -> p a b", a=free_shape[0])
        elif len(free_shape) == 3:
            ap = ap.rearrange("p (a b c) -> p a b c", a=free_shape[0], b=free_shape[1])
        self.off += w
        self.uid += 1
        return T(ap, "%s#%d" % (name, self.uid))

    def mark(self):
        return self.off

    def release(self, m):
        self.off = m


class Scratch:
    def __init__(self, ar, ngran):
        self.base = ar.off
        self.t = ar.t
        self.ngran = ngran
        ar.off += ngran * 512
        assert ar.off <= ar.words, ("scratch overflow", ar.off, ar.words)
        self.pos = 0
        self.peak = 0

    def reset(self):
        self.pos = 0

    def get(self, free_shape, dtype=F32):
        nel = int(np.prod(free_shape))
        w = (nel + 1) // 2 if dtype == BF16 else nel
        g = (w + 511) // 512
        assert self.pos + g <= self.ngran, ("scratch granules exhausted", self.pos, g, self.ngran)
        o = self.base + self.pos * 512
        ap = self.t[:, o:o + g * 512]
        if dtype != F32:
            ap = ap.bitcast(dtype)
        ap = ap[:, 0:nel]
        if len(free_shape) == 2:
            ap = ap.rearrange("p (a b) -> p a b", a=free_shape[0])
        elif len(free_shape) == 3:
            ap = ap.rearrange("p (a b c) -> p a b c", a=free_shape[0], b=free_shape[1])
        keys = tuple("scr%d" % (self.pos + i) for i in range(g))
        self.pos += g
        self.peak = max(self.peak, self.pos)
        return T(ap, keys)


class Ctx:
    pass


def make_ctx(nc, es, arena_words):
    cx = Ctx()
    cx.nc = nc
    cx.S = Sched(nc)
    cx.ar = Arena(nc, es, arena_words)
    cx.ps = [T(es.enter_context(nc.psum_tensor("psb%d" % i, [128, 512], F32))[:], "psb%d" % i) for i in range(8)]
    cx.S.bar_tile = cx.ar.alloc("bar", [8])
    cx.dbg = None
    return cx


def dram_in(nc, name, shape, dt=F32):
    return T(nc.dram_tensor(name, list(shape), dt, kind="ExternalInput").ap(), "dram:" + name)


def dram_out(nc, name, shape, dt=F32):
    return T(nc.dram_tensor(name, list(shape), dt, kind="ExternalOutput").ap(), "dram:" + name)


def load_consts(cx, cdram):
    S, ar = cx.S, cx.ar
    cx.ident = ar.alloc("ident", [128])
    cx.ones_f = ar.alloc("ones_f", [128])
    cx.ones_b = ar.alloc("ones_b", [128], BF16)
    S.dma("sp", cx.ident, cdram[:, 0:128])
    S.dma("sp", cx.ones_f, cdram[:, 128:256])
    S.copy("dve", cx.ones_b, cx.ones_f)


def emit_rstd(cx, ss_ps, rs, n_feat, rows=128):
    cx.S.act(rs[0:rows], ss_ps[0:rows], AF.Sqrt, bias=float(n_feat * EPS))
    cx.S.recip(rs[0:rows], rs[0:rows])


def emit_ffn(cx, hT, NT, wg_d, wu_d, wd_d, g_pre, g_post, G=1):
    S, ar, ps = cx.S, cx.ar, cx.ps
    if NT % (G * TT) != 0:
        G = 1
    GW = G * TT
    m0 = ar.mark()
    xn = ar.alloc("xn", [NK, GW], BF16)
    hid = ar.alloc("hid", [NF, GW], BF16)
    yb = ar.alloc("ffy", [NK, GW])
    sq = [ar.alloc("sq%d" % i, [TT], BF16) for i in range(2)]
    rs = [ar.alloc("rs%d" % i, [TT]) for i in range(G)]
    sg = [ar.alloc("sg%d" % i, [TT]) for i in range(2)]
    tmp = [ar.alloc("ftmp%d" % i, [TT]) for i in range(2)]
    NWB = 3
    wgb = [ar.alloc("wgb%d" % i, [NK, 128], BF16) for i in range(NWB)]
    wub = [ar.alloc("wub%d" % i, [NK, 128], BF16) for i in range(NWB)]
    wdb = [ar.alloc("wdb%d" % i, [NF, 128], BF16) for i in range(2)]
    ss_ps = [ps[6], ps[7]]
    nsq = 0
    for g in range(NT // GW):
        for s in range(G):
            c0 = g * GW + s * TT
            for k in range(NK):
                b = sq[nsq % 2]
                nsq += 1
                S.act(b, hT[:, k, c0:c0 + TT], AF.Square)
                S.mm(ss_ps[s], cx.ones_b, b, start=(k == 0), stop=(k == NK - 1))
            emit_rstd(cx, ss_ps[s], rs[s], D)
            for k in range(NK):
                S.stt("dve" if k % 2 == 0 else "pool", xn[:, k, s * TT:(s + 1) * TT], hT[:, k, c0:c0 + TT],
                      g_pre[:, k:k + 1], rs[s], ALU.mult, ALU.mult)
        it = 0

        def ld_gu(f):
            S.dma("pool", wgb[f % NWB], wg_d[f].rearrange("p (k m) -> p k m", k=NK))
            S.dma("pool", wub[f % NWB], wu_d[f].rearrange("p (k m) -> p k m", k=NK))

        def ld_d(d):
            S.dma("pool", wdb[d % 2], wd_d[d].rearrange("p (f m) -> p f m", f=NF))

        for f in range(NWB - 1):
            ld_gu(f)
        for f in range(NF):
            wgt, wut = wgb[f % NWB], wub[f % NWB]
            if f + NWB - 1 < NF:
                ld_gu(f + NWB - 1)
            if f == NF - 2:
                ld_d(0)
            for s in range(G):
                gp, up = ps[(it % 2) * 2], ps[(it % 2) * 2 + 1]
                sgt = sg[it % 2]
                it += 1
                for k in range(NK):
                    S.mm(gp, wgt[:, k, :], xn[:, k, s * TT:(s + 1) * TT], start=(k == 0), stop=(k == NK - 1))
                for k in range(NK):
                    S.mm(up, wut[:, k, :], xn[:, k, s * TT:(s + 1) * TT], start=(k == 0), stop=(k == NK - 1))
                S.act(sgt, gp, AF.Silu)
                S.tt("dve", hid[:, f, s * TT:(s + 1) * TT], sgt, up, ALU.mult)
        it = 0
        for d in range(NK):
            wdt = wdb[d % 2]
            if d + 1 < NK:
                ld_d(d + 1)
            for s in range(G):
                yp = ps[4 + it % 2]
                it += 1
                for f in range(NF):
                    S.mm(yp, wdt[:, f, :], hid[:, f, s * TT:(s + 1) * TT], start=(f == 0), stop=(f == NF - 1))
                S.act(yb[:, d, s * TT:(s + 1) * TT], yp, AF.Copy)
                b = sq[nsq % 2]
                nsq += 1
                S.act(b, yp, AF.Square)
                S.mm(ss_ps[s], cx.ones_b, b, start=(d == 0), stop=(d == NK - 1))
        for s in range(G):
            c0 = g * GW + s * TT
            emit_rstd(cx, ss_ps[s], rs[s], D)
            for d in range(NK):
                t = tmp[d % 2]
                S.stt("dve", t, yb[:, d, s * TT:(s + 1) * TT], g_post[:, d:d + 1], rs[s], ALU.mult, ALU.mult)
                S.tt("pool", hT[:, d, c0:c0 + TT], hT[:, d, c0:c0 + TT], t, ALU.add)
    S.barrier()
    ar.release(m0)


def emit_norm_to(cx, hT, NT, g32, out_bf, rows_feat=D):
    S, ar, ps = cx.S, cx.ar, cx.ps
    m0 = ar.mark()
    sq = [ar.alloc("nsq%d" % i, [TT], BF16) for i in range(2)]
    rs = ar.alloc("nrs", [TT])
    n = 0
    for t in range(NT // TT):
        c0 = t * TT
        for k in range(NK):
            b = sq[n % 2]
            n += 1
            S.act(b, hT[:, k, c0:c0 + TT], AF.Square)
            S.mm(ps[6], cx.ones_b, b, start=(k == 0), stop=(k == NK - 1))
        emit_rstd(cx, ps[6], rs, D)
        for k in range(NK):
            S.stt("dve" if k % 2 == 0 else "pool", out_bf[:, k, c0:c0 + TT], hT[:, k, c0:c0 + TT],
                  g32[:, k:k + 1], rs, ALU.mult, ALU.mult)
    S.barrier()
    ar.release(m0)


def lay_vec(v):
    v = np.asarray(v, np.float32)
    return np.ascontiguousarray(v.reshape(-1, 128).T)


def lay_w_gu(w):
    w = np.asarray(w, np.float32)
    return np.ascontiguousarray(w.reshape(NK, 128, NF, 128).transpose(2, 1, 0, 3).reshape(NF, 128, NK * 128))


def lay_w_d(w):
    w = np.asarray(w, np.float32)
    return np.ascontiguousarray(w.reshape(NF, 128, NK, 128).transpose(2, 1, 0, 3).reshape(NK, 128, NF * 128))


def lay_act_T(x):
    x = np.asarray(x)
    nt = x.shape[0]
    return np.ascontiguousarray(x.reshape(nt, -1, 128).transpose(2, 1, 0))


def unlay_act_T(xT):
    p, n, nt = xT.shape
    return np.ascontiguousarray(xT.transpose(2, 1, 0).reshape(nt, n * 128))


def make_consts():
    c = np.zeros((128, 384), np.float32)
    c[:, 0:128] = np.eye(128, dtype=np.float32)
    c[:, 128:256] = 1.0
    return c


ARENA_WORDS = 53000


def load_vecs(cx, vd, ncol):
    v = cx.ar.alloc("vecs", [ncol])
    cx.S.dma("sp", v, vd)
    return v


def build_TA(NT):
    nc = bass.Bass("TRN2", target_bir_lowering=False)
    with ExitStack() as es:
        cx = make_ctx(nc, es, ARENA_WORDS)
        S, ar = cx.S, cx.ar
        h_in = dram_in(nc, "h_in", [128, NK, NT])
        cd = dram_in(nc, "consts", [128, 384])
        vd = dram_in(nc, "vec", [128, 24])
        wg = dram_in(nc, "wg", [NF, 128, NK * 128])
        wu = dram_in(nc, "wu", [NF, 128, NK * 128])
        wd = dram_in(nc, "wd", [NK, 128, NF * 128])
        h_out = dram_out(nc, "h_out", [128, NK, NT])
        u_out = dram_out(nc, "u_out", [128, NK, NT], BF16)
        load_consts(cx, cd)
        vec = load_vecs(cx, vd, 24)
        hT = ar.alloc("hT", [NK, NT])
        for k in range(NK):
            S.dma("sp", hT[:, k, :], h_in[:, k, :])
        g = ar.alloc("gains", [24])
        S.ts("dve", g[:, 0:8], vec[:, 0:8], 32.0, ALU.mult)
        S.ts("dve", g[:, 8:16], vec[:, 8:16], 16.0, ALU.mult)
        S.ts("dve", g[:, 16:24], vec[:, 16:24], 32.0, ALU.mult)
        emit_ffn(cx, hT, NT, wg, wu, wd, g[:, 0:8], g[:, 8:16])
        uT = ar.alloc("uT", [NK, NT], BF16)
        emit_norm_to(cx, hT, NT, g[:, 16:24], uT)
        for k in range(NK):
            S.dma("sp", h_out[:, k, :], hT[:, k, :])
            S.dma("sp", u_out[:, k, :], uT[:, k, :])
        S.emit()
    return nc


CB = dict(r=0, k=64, v=128, wl=192, al=256, gl=320, vl=448, cq=512, ckv=896, kr=1152, krs=1248,
          hq=1344, hf=1472, hi=1600, hg=1728)
NCOLB = 1856
SM = dict(w_up=0, a_up=64, g_up=128, vres_up=192, uq=256, uqs=544, ukv=832)
NSM = 1088
VB = dict(mu_r=0, mu_k=1, mu_v=2, mu_wl=3, mu_al=4, mu_gl=5, mu_vl=6, w0=7, a0=8, k_k=9, k_a=10, r_k=11,
          gn_g=12, gn_b=13, v0=14, qg=15, kvg=18, lb=20, hng=24)
NVB = 25
CC = dict(maskAT=0, maskN=512, I64=1024, triH=1536, triA=1664, reset64=1792, reset32=2304, invf=2816, sgn=2817,
          sel96=2818)
NCB = 2920
C0 = float(np.exp(-0.5))
A_GN_EPS = 64e-5
TWO_PI = float(2 * np.pi)
PI = float(np.pi)


def make_constsB():
    c = np.zeros((128, NCB), np.float32)
    m = np.zeros((128, 128), np.float32)
    il = np.arange(64)[:, None]
    tl = np.arange(64)[None, :]
    for half in range(2):
        m[half * 64:(half + 1) * 64, 0:64] = (il < tl)
        m[half * 64:(half + 1) * 64, 64:128] = (il <= tl)
    c[:, 0:512] = np.tile(m, (1, 4))
    mn = (np.arange(64)[:, None] > np.arange(64)[None, :]).astype(np.float32)
    c[0:64, 512:1024] = np.tile(mn, (1, 8))
    c[0:64, 1024:1536] = np.tile(np.eye(64, dtype=np.float32), (1, 8))
    th = (np.arange(32)[:, None] <= np.arange(32)[None, :]).astype(np.float32)
    c[:, 1536:1664] = np.tile(np.tile(th, (4, 1)), (1, 4))
    c[:, 1664:1792] = (np.arange(128)[:, None] <= np.arange(128)[None, :])
    r64 = np.ones(512, np.float32)
    r64[::64] = 0
    r32 = np.ones(512, np.float32)
    r32[::32] = 0
    c[:, 1792:2304] = r64[None, :]
    c[:, 2304:2816] = r32[None, :]
    invf = (10000.0 ** (-np.arange(0, 32, 2, dtype=np.float32) / 32)).astype(np.float32)
    c[64:80, 2816] = invf
    c[80:96, 2816] = invf
    c[64:80, 2817] = -1.0
    c[80:96, 2817] = 1.0
    c[0:96, 2818 + 96] = 1.0
    return c


import os
PHASES = os.environ.get("MIX_PHASES", "rmh")


def emit_cumsum(S, src, bufA, bufB, rows, nch, C):
    v = lambda t: t[rows].rearrange("p (c t) -> p c t", c=nch)
    cur, bufs, i, s = src, [bufA, bufB], 0, 1
    while s < C:
        nxt = bufs[i % 2]
        S.copy("pool", v(nxt)[:, :, 0:s], v(cur)[:, :, 0:s])
        S.tt("dve", v(nxt)[:, :, s:C], v(cur)[:, :, s:C], v(cur)[:, :, 0:C - s], ALU.add)
        cur = nxt
        i += 1
        s *= 2
    return cur, bufs[i % 2]


def emit_mixer(cx, layer, SL, u_d, wB_d, wsm_d, vec_d, pos_d, cB_d, vf_in_d, vf_out_d, y_d):
    S, ar, ps = cx.S, cx.ar, cx.ps
    L0 = (layer == 0)
    ntile = SL // TT
    nblk = SL // 128
    m_all = ar.mark()
    SCALE = float(96 ** -0.5)
    wB = ar.alloc("wB", [NK, NCOLB], BF16)
    for k in range(NK):
        S.dma("pool", wB[:, k, :], wB_d[:, k, :])
    wsm = ar.alloc("wsm", [NSM])
    S.dma("sp", wsm, wsm_d)
    wuq = ar.alloc("wuq", [3, 96], BF16)
    wuqs = ar.alloc("wuqs", [3, 96], BF16)
    wukv = ar.alloc("wukv", [2, 128], BF16)
    S.copy("dve", wuq, wsm[:, SM["uq"]:SM["uq"] + 288].rearrange("p (a b) -> p a b", a=3))
    S.copy("dve", wuqs, wsm[:, SM["uqs"]:SM["uqs"] + 288].rearrange("p (a b) -> p a b", a=3))
    S.copy("dve", wukv, wsm[:, SM["ukv"]:SM["ukv"] + 256].rearrange("p (a b) -> p a b", a=2))
    w_up = wsm[0:64, SM["w_up"]:SM["w_up"] + 64]
    a_up = wsm[0:64, SM["a_up"]:SM["a_up"] + 64]
    g_up = wsm[:, SM["g_up"]:SM["g_up"] + 64]
    vres_up = wsm[0:32, SM["vres_up"]:SM["vres_up"] + 64]
    vec = ar.alloc("vecB", [NVB])
    S.dma("sp", vec, vec_d)
    cB = ar.alloc("cB", [NCB])
    S.dma("sp", cB, cB_d)
    triA = ar.alloc("triA", [128], BF16)
    S.copy("dve", triA, cB[:, CC["triA"]:CC["triA"] + 128])
    sel96 = ar.alloc("sel96", [97], BF16)
    S.copy("dve", sel96, cB[:, CC["sel96"]:CC["sel96"] + 97])
    maskAT = cB[:, CC["maskAT"]:CC["maskAT"] + 512]
    maskN = cB[0:64, CC["maskN"]:CC["maskN"] + 512]
    I64 = cB[0:64, CC["I64"]:CC["I64"] + 512]
    triH = cB[:, CC["triH"]:CC["triH"] + 128].rearrange("p (b t) -> p b t", b=4)
    reset64 = cB[:, CC["reset64"]:CC["reset64"] + 512]
    reset32 = cB[:, CC["reset32"]:CC["reset32"] + 512]
    invf = cB[:, CC["invf"]:CC["invf"] + 1]
    sgn = cB[:, CC["sgn"]:CC["sgn"] + 1]
    ones64 = cx.ones_f[0:64, 0:64]
    id64 = cx.ident[0:64, 0:64]

    def V(name, rows=128):
        return vec[0:rows, VB[name]:VB[name] + 1]

    dv = ar.alloc("dvec", [16])
    S.ts("dve", dv[:, 0:7], vec[:, 0:7], -1.0, ALU.mult, 1.0, ALU.add)
    S.ts("dve", dv[:, 7:8], vec[:, VB["k_a"]:VB["k_a"] + 1], -1.0, ALU.mult, 1.0, ALU.add)
    S.ts("dve", dv[:, 8:11], vec[:, VB["qg"]:VB["qg"] + 3], float(np.sqrt(384.0)), ALU.mult)
    S.ts("dve", dv[:, 11:13], vec[:, VB["kvg"]:VB["kvg"] + 2], 16.0, ALU.mult)
    S.ts("dve", dv[:, 15:16], vec[:, VB["hng"]:VB["hng"] + 1], float(np.sqrt(128.0)), ALU.mult)
    if L0:
        S.memset("dve", dv[:, 13:14], 0.0)
    else:
        le = ar.alloc("lbe", [8])
        S.act(le[:, 0:4], vec[:, VB["lb"]:VB["lb"] + 4], AF.Exp)
        S.reduce("dve", le[:, 4:5], le[:, 0:4], ALU.add)
        S.recip(le[:, 4:5], le[:, 4:5])
        S.reduce("dve", le[:, 5:6], le[:, 1:layer + 1], ALU.add)
        S.tt("dve", dv[:, 13:14], le[:, 5:6], le[:, 4:5], ALU.mult)
    S.ts("dve", dv[:, 14:15], dv[:, 13:14], -1.0, ALU.mult, 1.0, ALU.add)
    mpi = ar.alloc("mpi", [8])
    S.memset("dve", mpi, -PI)

    KT = ar.alloc("KT", [SL], BF16)
    Vaug = ar.alloc("Vaug", [nblk, 65], BF16)
    S.memset("pool", KT[96:97, :], 1.0)
    S.memset("pool", Vaug[:, :, 64:65], 1.0)
    kmax2 = ar.alloc("kmax2", [8])
    S.memset("dve", kmax2[96:97, :], 0.0)
    SV = ar.alloc("SV", [9, 64])
    S.memset("dve", SV[0:64, 0, :], 0.0)
    Sh = ar.alloc("Sh", [17, 128])
    S.memset("pool", Sh[:, 0, :], 0.0)
    shifted = [("r", 64), ("k", 64), ("v", 64), ("wl", 64), ("al", 64), ("gl", 128)] + ([] if L0 else [("vl", 32)])
    halo = ar.alloc("halo", [8])
    S.memset("dve", halo, 0.0)
    ut = [ar.alloc("ut%d" % i, [NK, TT], BF16) for i in range(2)]
    ya_o = ar.alloc("ya_o", [TT], BF16)
    yb_o = ar.alloc("yb_o", [TT], BF16)
    yc_o = ar.alloc("yc_o", [TT], BF16)
    gC = ar.alloc("gC", [16])
    gCh = ar.alloc("gCh", [16])
    QT = ar.alloc("QT", [TT], BF16)
    Pb = [ar.alloc("Pb%d" % i, [TT], BF16) for i in range(2)]
    rd = ar.alloc("rd", [TT])
    ob = ar.alloc("ob", [TT])
    sc = Scratch(ar, (ar.words - ar.off) // 512)

    rot = [0]

    def bank():
        b = ps[rot[0] % 4]
        rot[0] += 1
        return b

    def proj(off, M, utile):
        pb = bank()
        for k in range(NK):
            S.mm(pb[0:M, :], wB[:, k, off:off + M], utile[:, k, :], start=(k == 0), stop=(k == NK - 1))
        return pb

    S.dma("sp", ut[0], u_d[:, :, 0:TT])
    for it in range(ntile):
        c0 = it * TT
        u_t = ut[it % 2]
        S.mute = False
        if it + 1 < ntile:
            S.dma("sp", ut[(it + 1) % 2], u_d[:, :, c0 + TT:c0 + 2 * TT])
        R = slice(0, 64)
        RR = slice(64, 96)

        def do_rwkv():
            S.mute = "r" not in PHASES
            sc.reset()
            A = lambda n=TT, dt=F32: sc.get([n], dt)
            rws = [A(), A()]
            tmpm = A()
            xr, xk, xwl, xal, xgl = A(), A(), A(), A(), A()
            xvf = A(TT + 64)
            xv = xvf[:, 64:64 + TT]
            xvl = A()
            cs, a_t, g_t, kk, bb, bon = A(), A(), A(), A(), A(), A()
            E1, E2, E3, E4 = rws[0], rws[1], tmpm, A()
            t1, t2 = A(), A()
            WL = sc.get([8, 64])
            RA = sc.get([8, 64])
            ARB = sc.get([8, 64])
            BK = sc.get([8, 2, 64])
            BKh = sc.get([8, 2, 64])
            Nm = [sc.get([8, 64]) for i in range(2)]
            Mm = [sc.get([8, 64]) for i in range(2)]
            Pm = [sc.get([8, 64]) for i in range(2)]
            BKhT = sc.get([8, 64])
            UV = sc.get([8, 64])
            Wsb = A(64)
            yT = xal
            vft = xwl
            mixed = {"r": xr, "k": xk, "v": xv, "wl": xwl, "al": xal, "gl": xgl, "vl": xvl}
            for gi, (nm, M) in enumerate(shifted):
                pb = proj(CB[nm], M, u_t)
                rw = rws[gi % 2]
                S.copy("act", rw[0:M], pb[0:M, :])
                mu = vec[0:M, gi:gi + 1]
                omu = dv[0:M, gi:gi + 1]
                S.ts("dve", tmpm[0:M, 1:TT], rw[0:M, 0:TT - 1], mu, ALU.mult)
                S.ts("dve", tmpm[0:M, 0:1], halo[0:M, gi:gi + 1], mu, ALU.mult)
                S.stt("dve", mixed[nm][0:M], rw[0:M], omu, tmpm[0:M], ALU.mult, ALU.add)
                S.copy("pool", halo[0:M, gi:gi + 1], rw[0:M, TT - 1:TT])
            S.memset("pool", xvf[0:64, 0:64], 0.0)
            R = slice(0, 64)
            S.act(xwl[R], xwl[R], AF.Tanh)
            pb = bank()
            S.mm(pb[R, :], w_up, xwl[R])
            S.act(t1[R], pb[R, :], AF.Sigmoid, bias=V("w0", 64))
            cs, t2 = emit_cumsum(S, t1, cs, t2, R, 8, 64)
            S.tt("dve", t2[R], cs[R], t1[R], ALU.subtract)
            S.act(E1[R], cs[R], AF.Exp, scale=-C0)
            S.act(E2[R], cs[R], AF.Exp, scale=C0)
            S.act(E3[R], t2[R], AF.Exp, scale=-C0)
            cs3 = cs[R].rearrange("p (c t) -> p c t", c=8)
            S.tt("dve", t2[R].rearrange("p (c t) -> p c t", c=8), cs3[:, :, 63:64].bcast([64, 8, 64]), cs3, ALU.subtract)
            S.act(E4[R], t2[R], AF.Exp, scale=-C0)
            S.act(gC[R, 0:8], cs3[:, :, 63:64].rearrange("p c o -> p (c o)"), AF.Exp, scale=-C0)
            pb = bank()
            S.mm(pb[R, :], a_up, xal[R])
            S.act(a_t[R], pb[R, :], AF.Sigmoid, bias=V("a0", 64))
            S.act(xgl, xgl, AF.Sigmoid)
            pb = bank()
            S.mm(pb[R, :], g_up, xgl)
            S.copy("act", g_t[R], pb[R, :])
            if L0:
                S.dma("sp", vf_out_d[:, c0:c0 + TT], xv[R])
            else:
                S.dma("sp", vft[R], vf_in_d[:, c0:c0 + TT])
                pb = bank()
                S.mm(pb[R, :], vres_up, xvl[0:32])
                S.act(t1[R], pb[R, :], AF.Sigmoid, bias=V("v0", 64))
                S.tt("dve", vft[R], vft[R], xv[R], ALU.subtract)
                S.tt("dve", vft[R], vft[R], t1[R], ALU.mult)
                S.tt("dve", xv[R], xv[R], vft[R], ALU.add)
            S.ts("dve", kk[R], xk[R], V("k_k", 64), ALU.mult)
            S.tt("dve", t1[R], kk[R], kk[R], ALU.mult)
            pb = bank()
            S.mm(pb[R, :], ones64, t1[R])
            S.act(t1[R], pb[R, :], AF.Sqrt)
            S.ts("dve", t1[R], t1[R], 1e-12, ALU.max)
            S.recip(t1[R], t1[R])
            S.tt("dve", kk[R], kk[R], t1[R], ALU.mult)
            S.ts("dve", t1[R], a_t[R], V("k_a", 64), ALU.mult, dv[R, 7:8], ALU.add)
            S.tt("dve", xk[R], xk[R], t1[R], ALU.mult)
            S.stt("dve", t1[R], xr[R], V("r_k", 64), xk[R], ALU.mult, ALU.mult)
            pb = bank()
            S.mm(pb[R, :], ones64, t1[R])
            S.tt("dve", bon[R], pb[R, :], xv[R], ALU.mult)
            S.tt("dve", bb[R], kk[R], a_t[R], ALU.mult)
            v3 = lambda t: t[R].rearrange("p (c t) -> p c t", c=8)
            S.stt("dve", WL[R], v3(kk), -1.0, v3(E3), ALU.mult, ALU.mult)
            S.tt("dve", RA[R], v3(xr), v3(E1), ALU.mult)
            S.tt("dve", BK[R, :, 0, :], v3(bb), v3(E2), ALU.mult)
            S.tt("dve", BK[R, :, 1, :], v3(xk), v3(E2), ALU.mult)
            S.tt("dve", BKh[R, :, 0, :], v3(bb), v3(E4), ALU.mult)
            S.tt("dve", BKh[R, :, 1, :], v3(xk), v3(E4), ALU.mult)
            f2 = lambda t, c: t[R, c].rearrange("p a b -> p (a b)")
            m4 = maskAT.rearrange("p (c h t) -> p c h t", c=4, h=2)
            for half in range(2):
                pb = bank()
                for cc in range(4):
                    c = half * 4 + cc
                    S.mm(pb[:, cc * 128:cc * 128 + 64], f2(BK, c), WL[R, c, :])
                    S.mm(pb[:, cc * 128 + 64:(cc + 1) * 128], f2(BK, c), RA[R, c, :])
                p4 = pb.rearrange("p (c h t) -> p c h t", c=4, h=2)
                cs_ = slice(half * 4, half * 4 + 4)
                S.tt("dve", Mm[0][R, cs_, :], p4[R, :, 0, :], m4[R, :, 0, :], ALU.mult)
                S.tt("dve", ARB[R, cs_, :], p4[R, :, 1, :], m4[R, :, 1, :], ALU.mult)
                S.tt("dve", WL[64:128, cs_, :], p4[64:128, :, 0, :], m4[64:128, :, 0, :], ALU.mult)
                S.tt("dve", RA[64:128, cs_, :], p4[64:128, :, 1, :], m4[64:128, :, 1, :], ALU.mult)
            pb = bank()
            for c in range(8):
                S.mm(pb[R, c * 64:(c + 1) * 64], WL[R, c, :], BK[R, c, 0, :])
            S.tt("dve", Nm[0][R].rearrange("p a b -> p (a b)"), pb[R, :], maskN, ALU.mult)
            S.tt("pool", Pm[0][R].rearrange("p a b -> p (a b)"), Mm[0][R].rearrange("p a b -> p (a b)"), I64, ALU.add)
            cur = 0
            for rnd in range(5):
                Mc, Nc, Pc = Mm[cur], Nm[cur], Pm[cur]
                Mn, Nn, Pn = Mm[1 - cur], Nm[1 - cur], Pm[1 - cur]
                pbm = bank()
                pbn = bank()
                for c in range(8):
                    S.mm(pbm[R, c * 64:(c + 1) * 64], Nc[R, c, :], Mc[R, c, :])
                for c in range(8):
                    S.mm(pbn[R, c * 64:(c + 1) * 64], Mc[R, c, :], Nc[R, c, :])
                S.copy("act", Mn[R].rearrange("p a b -> p (a b)"), pbm[R, :])
                S.copy("dve", Nn[R].rearrange("p a b -> p (a b)"), pbn[R, :])
                pbp = bank()
                for c in range(8):
                    S.mm(pbp[R, c * 64:(c + 1) * 64], Nn[R, c, :], Pc[R, c, :])
                S.tt("dve", Pn[R].rearrange("p a b -> p (a b)"), pbp[R, :], Pc[R].rearrange("p a b -> p (a b)"), ALU.add)
                cur = 1 - cur
            Pf = Pm[cur]
            pb = bank()
            for c in range(8):
                S.transpose(pb[:, c * 64:(c + 1) * 64], f2(BKh, c), id64)
            S.copy("act", BKhT.rearrange("p a b -> p (a b)"), pb)
            pb = bank()
            for c in range(8):
                S.transpose(pb[:, c * 64:(c + 1) * 64], xvf[R, c * 64:c * 64 + 128], id64)
            S.copy("dve", UV[64:128].rearrange("p a b -> p (a b)"), pb[64:128, :])
            S.copy("dve", SV[64:128, 0:8, :].rearrange("p a b -> p (a b)"), pb[64:128, :])
            ypb = ps[4]
            for c in range(8):
                pw = bank()
                S.mm(pw[R, 0:64], WL[:, c, :], SV[:, c, :])
                S.copy("act", Wsb[R], pw[R, 0:64])
                pu = bank()
                S.mm(pu[R, 0:64], Pf[R, c, :], Wsb[R])
                S.copy("dve", UV[R, c, :], pu[R, 0:64])
                S.mm(ypb[R, c * 64:(c + 1) * 64], SV[:, c, :], RA[:, c, :], start=True, stop=False)
                S.mm(ypb[R, c * 64:(c + 1) * 64], UV[R, c, :], ARB[R, c, :], start=False, stop=True)
                pn = bank()
                S.mm(pn[R, 0:64], BKhT[:, c, :], UV[:, c, :])
                S.stt("dve", SV[R, c + 1, :], SV[R, c, :], gC[R, c:c + 1], pn[R, 0:64], ALU.mult, ALU.add)
            S.copy("pool", SV[R, 0, :], SV[R, 8, :])
            S.copy("act", yT[R], ypb[R, :])
            pb = bank()
            S.mm(pb[R, :], ones64, yT[R])
            S.stt("dve", yT[R], pb[R, :], -1.0 / 64, yT[R], ALU.mult, ALU.add)
            S.act(t1[R], yT[R], AF.Square)
            pb = bank()
            S.mm(pb[R, :], ones64, t1[R])
            S.act(t1[R], pb[R, :], AF.Sqrt, bias=A_GN_EPS, scale=1.0 / 64)
            S.recip(t1[R], t1[R])
            S.tt("dve", yT[R], yT[R], t1[R], ALU.mult)
            S.ts("dve", yT[R], yT[R], V("gn_g", 64), ALU.mult, V("gn_b", 64), ALU.add)
            S.tt("dve", yT[R], yT[R], bon[R], ALU.add)
            S.tt("dve", ya_o[R], yT[R], g_t[R], ALU.mult)
            S.dma("sp", y_d[0:64, c0:c0 + TT], ya_o[R])

        def do_mla_prep():
            S.mute = "m" not in PHASES
            sc.reset()
            A = lambda n=TT, dt=F32: sc.get([n], dt)
            t1, t2 = A(), A()
            cq_sb = sc.get([3, TT])
            sqb = [A(TT, BF16) for i in range(2)]
            rsq = A()
            cqn = sc.get([3, TT], BF16)
            ckvn = sc.get([2, TT], BF16)
            posi = sc.get([TT], I32)
            cos2, sin2 = A(), A()
            qr = A()
            qsq = A(TT, BF16)
            ksq = A(TT, BF16)
            rdm = A()
            def latent_norm(off, nch, dst, gcol, nfeat):
                ssb = ps[4]
                for j in range(nch):
                    pb = proj(off + j * 128, 128, u_t)
                    S.copy("act", cq_sb[:, j, :], pb)
                    b = sqb[j % 2]
                    S.act(b, pb, AF.Square)
                    S.mm(ssb, cx.ones_b, b, start=(j == 0), stop=(j == nch - 1))
                S.act(rsq, ssb, AF.Sqrt, bias=float(nfeat * EPS))
                S.recip(rsq, rsq)
                for j in range(nch):
                    S.stt("dve", dst[:, j, :], cq_sb[:, j, :], dv[:, gcol + j:gcol + j + 1], rsq, ALU.mult, ALU.mult)
            latent_norm(CB["cq"], 3, cqn, 8, 384)
            pq = bank()
            for j in range(3):
                S.mm(pq[0:96, :], wuq[:, j, :], cqn[:, j, :], start=(j == 0), stop=(j == 2))
            pqs = bank()
            for j in range(3):
                S.mm(pqs[0:96, :], wuqs[:, j, :], cqn[:, j, :], start=(j == 0), stop=(j == 2))
            RR = slice(64, 96)
            S.dma("sp", posi[RR], pos_d[:, c0:c0 + TT])
            S.copy("dve", t1[RR], posi[RR])
            S.ts("dve", t1[RR], t1[RR], invf[RR], ALU.mult)
            def sincos(dst, shift):
                S.ts("dve", t2[RR], t1[RR], 1.0 / TWO_PI, ALU.mult, shift, ALU.add)
                S.copy("dve", posi[RR], t2[RR])
                S.copy("dve", rdm[RR], posi[RR])
                S.tt("dve", t2[RR], t2[RR], rdm[RR], ALU.subtract)
                S.ts("dve", rdm[RR], t2[RR], 0.0, ALU.is_lt)
                S.tt("dve", t2[RR], t2[RR], rdm[RR], ALU.add)
                S.act(dst[RR], t2[RR], AF.Sin, bias=mpi[RR, 0:1], scale=TWO_PI)
            sincos(sin2, 0.5)
            S.ts("dve", sin2[RR], sin2[RR], sgn[RR], ALU.mult)
            sincos(cos2, 0.75)
            S.act(QT[0:64], pq[0:64, :], AF.Copy, scale=SCALE)
            S.tt("dve", qr[RR], pq[RR, :], cos2[RR], ALU.mult)
            S.tt("dve", t1[RR], pqs[RR, :], sin2[RR], ALU.mult)
            S.tt("dve", qr[RR], qr[RR], t1[RR], ALU.add)
            S.act(QT[RR], qr[RR], AF.Copy, scale=SCALE)
            S.act(qsq[0:64], pq[0:64, :], AF.Square, scale=SCALE)
            S.act(qsq[RR], qr[RR], AF.Square, scale=SCALE)
            latent_norm(CB["ckv"], 2, ckvn, 11, 256)
            pkv = bank()
            for j in range(2):
                S.mm(pkv, wukv[:, j, :], ckvn[:, j, :], start=(j == 0), stop=(j == 1))
            S.copy("act", KT[0:64, c0:c0 + TT], pkv[0:64, :])
            pkr = proj(CB["kr"], 96, u_t)
            pkrs = proj(CB["krs"], 96, u_t)
            S.tt("dve", t1[RR], pkr[RR, :], cos2[RR], ALU.mult)
            S.tt("dve", t2[RR], pkrs[RR, :], sin2[RR], ALU.mult)
            S.tt("dve", KT[RR, c0:c0 + TT], t1[RR], t2[RR], ALU.add)
            S.act(ksq[0:96], KT[0:96, c0:c0 + TT], AF.Square)
            pb = bank()
            S.mm(pb[0:97, :], sel96[0:96, :], ksq[0:96])
            S.reduce("dve", kmax2[96:97, 1:2], pb[96:97, :], ALU.max)
            S.tt("dve", kmax2[96:97, 0:1], kmax2[96:97, 0:1], kmax2[96:97, 1:2], ALU.max)
            pb = bank()
            S.mm(pb[0:97, :], sel96[0:96, :], qsq[0:96])
            S.ts("dve", t1[96:97], pb[96:97, :], kmax2[96:97, 0:1], ALU.mult)
            S.act(t1[96:97], t1[96:97], AF.Sqrt)
            S.ts("dve", QT[96:97], t1[96:97], -1.0, ALU.mult)
            pb = bank()
            for blk in range(4):
                for j in range(2):
                    S.mm(pb[:, blk * 64:(blk + 1) * 64], ckvn[:, j, blk * 128:(blk + 1) * 128], wukv[:, j, 64:128],
                         start=(j == 0), stop=(j == 1))
            S.copy("act", Vaug[:, 4 * it:4 * it + 4, 0:64], pb[:, 0:256].rearrange("p (a b) -> p a b", a=4))
        def do_attn():
            ob_ps = ps[7]
            nkb = 4 * it + 4
            for kb in range(nkb):
                d = kb - 4 * it
                qlo = max(d, 0) * 128
                sp_ = ps[5 + kb % 2]
                pt = Pb[kb % 2]
                S.mm(sp_[:, qlo:TT], KT[0:97, kb * 128:(kb + 1) * 128], QT[0:97, qlo:TT])
                S.act(pt[:, qlo:TT], sp_[:, qlo:TT], AF.Exp)
                if d >= 0:
                    S.tt("pool", pt[:, qlo:qlo + 128], pt[:, qlo:qlo + 128], triA, ALU.mult)
                S.mm(ob_ps[0:65, qlo:TT], Vaug[:, kb, :], pt[:, qlo:TT], start=(kb == 0), stop=(kb == nkb - 1))
            S.recip(rd[0:1], ob_ps[64:65, :])
            S.copy("act", ob[R], ob_ps[R, :])
            pb = ps[5]
            S.mm(pb[R, :], cx.ones_f[0:1, 0:64], rd[0:1])
            S.tt("dve", yb_o[R], ob[R], pb[R, :], ALU.mult)
            S.dma("sp", y_d[64:128, c0:c0 + TT], yb_o[R])

        def do_hgrn():
            S.mute = "h" not in PHASES
            sc.reset()
            A = lambda n=TT, dt=F32: sc.get([n], dt)
            t1, t2 = A(), A()
            qh, kx, clh, qb, gh = A(), A(), A(), A(), A()
            qtl, ktl, khat = A(), A(), A()
            Vh = sc.get([16, 128])
            KhT = sc.get([16, 128])
            KV = sc.get([16, 128])
            ATh = sc.get([16, 32])
            oh = A()
            sqb = [A(TT, BF16)]
            rsq = A()
            pb = proj(CB["hq"], 128, u_t)
            S.act(qh, pb, AF.Silu)
            pb = proj(CB["hf"], 128, u_t)
            S.act(t1, pb, AF.Sigmoid)
            S.ts("dve", t1, t1, dv[:, 14:15], ALU.mult, dv[:, 13:14], ALU.add)
            S.ts("dve", kx, t1, -1.0, ALU.mult, 1.0, ALU.add)
            S.ts("dve", t1, t1, 1e-6, ALU.max)
            S.act(t1, t1, AF.Ln)
            clh, t2 = emit_cumsum(S, t1, clh, t2, slice(0, 128), 16, 32)
            c16 = lambda t: t.rearrange("p (c t) -> p c t", c=16)
            S.act(t2, clh, AF.Exp)
            S.tt("dve", qb, qh, t2, ALU.mult)
            S.tt("dve", c16(t1), c16(clh), c16(clh)[:, :, 15:16].bcast([128, 16, 32]), ALU.subtract)
            S.act(t2, t1, AF.Exp)
            S.tt("dve", qtl, qh, t2, ALU.mult)
            S.act(t2, t1, AF.Exp, scale=-1.0)
            S.tt("dve", ktl, kx, t2, ALU.mult)
            S.tt("dve", c16(t1), c16(clh), c16(clh)[:, :, 31:32].bcast([128, 16, 32]), ALU.subtract)
            S.act(t2, t1, AF.Exp, scale=-1.0)
            S.tt("dve", khat, kx, t2, ALU.mult)
            S.act(gCh[:, 0:16], c16(clh)[:, :, 31:32].rearrange("p c o -> p (c o)"), AF.Exp)
            Q = slice(0, 32)
            for blk in range(4):
                pb = bank()
                for j in range(4):
                    c = blk * 4 + j
                    for k in range(NK):
                        S.mm(pb[Q, j * 128:(j + 1) * 128], u_t[:, k, c * 32:(c + 1) * 32],
                             wB[:, k, CB["hi"]:CB["hi"] + 128], start=(k == 0), stop=(k == NK - 1))
                S.copy("act", Vh[Q, blk * 4:blk * 4 + 4, :].rearrange("p a b -> p (a b)"), pb[Q, :])
            pb = proj(CB["hg"], 128, u_t)
            S.act(gh, pb, AF.Silu)
            for blk in range(4):
                pb = bank()
                for j in range(4):
                    c = blk * 4 + j
                    S.transpose(pb[Q, j * 128:(j + 1) * 128], khat[:, c * 32:(c + 1) * 32], cx.ident)
                S.copy("dve", KhT[Q, blk * 4:blk * 4 + 4, :].rearrange("p a b -> p (a b)"), pb[Q, :])
            for blk in range(4):
                pb = bank()
                for j in range(4):
                    c = blk * 4 + j
                    S.mm(pb[:, j * 128:(j + 1) * 128], KhT[Q, c, :], Vh[Q, c, :])
                S.copy("act" if blk % 2 == 0 else "dve", KV[:, 4 * blk:4 * blk + 4, :].rearrange("p a b -> p (a b)"), pb)
            pb = bank()
            for c in range(16):
                S.mm(pb[Q, c * 32:(c + 1) * 32], ktl[:, c * 32:(c + 1) * 32], qtl[:, c * 32:(c + 1) * 32])
            S.tt("dve", ATh[Q], pb[Q, :].rearrange("p (c t) -> p c t", c=16),
                 triH[Q, 0:1, :].bcast([32, 16, 32]), ALU.mult)
            for c in range(16):
                S.stt("dve", Sh[:, c + 1, :], Sh[:, c, :], gCh[:, c:c + 1], KV[:, c, :], ALU.mult, ALU.add)
            ohp = ps[4]
            for c in range(16):
                S.mm(ohp[:, c * 32:(c + 1) * 32], Sh[:, c, :], qb[:, c * 32:(c + 1) * 32], start=True, stop=False)
                S.mm(ohp[:, c * 32:(c + 1) * 32], Vh[Q, c, :], ATh[Q, c, :], start=False, stop=True)
            S.copy("pool", Sh[:, 0, :], Sh[:, 16, :])
            S.copy("act", oh, ohp)
            S.act(sqb[0], ohp, AF.Square)
            pb = bank()
            S.mm(pb, cx.ones_b, sqb[0])
            S.act(rsq, pb, AF.Sqrt, bias=float(128 * EPS))
            S.recip(rsq, rsq)
            S.stt("dve", oh, oh, dv[:, 15:16], rsq, ALU.mult, ALU.mult)
            S.tt("dve", yc_o, oh, gh, ALU.mult)
            S.dma("sp", y_d[128:192, c0:c0 + TT], yc_o[0:64])
        do_mla_prep()
        S.mute = False
        main_ops, main_lines = S.ops, S.lines
        S.ops, S.lines = [], []
        do_rwkv()
        do_hgrn()
        S.mute = False
        a_ops, a_lines = S.ops, S.lines
        S.ops, S.lines = [], []
        S.mute = "m" not in PHASES
        do_attn()
        S.mute = False
        b_ops, b_lines = S.ops, S.lines
        ia = ib = 0
        na, nb = len(a_ops), len(b_ops)
        while ia < na or ib < nb:
            if ib >= nb or (ia < na and ia * nb <= ib * na):
                main_ops.append(a_ops[ia]); main_lines.append(a_lines[ia]); ia += 1
            else:
                main_ops.append(b_ops[ib]); main_lines.append(b_lines[ib]); ib += 1
        S.ops, S.lines = main_ops, main_lines
    S.mute = False
    if cx.dbg is not None:
        for nm, t in (("QT", QT), ("cos2", cos2), ("sin2", sin2), ("rd", rd), ("ob", ob), ("qr", qr)):
            cx.dbg[nm] = t
        cx.dbg["KT"] = KT
        cx.dbg["Vaug"] = Vaug
    S.barrier()
    cx.scr_peak = sc.peak
    ar.release(m_all)


DEBUG_B = bool(int(os.environ.get("DEBUG_B", "0")))


def build_B(layer, SL):
    nc = bass.Bass("TRN2", target_bir_lowering=False)
    with ExitStack() as es:
        cx = make_ctx(nc, es, ARENA_WORDS)
        u_d = dram_in(nc, "u_full", [128, NK, SL], BF16)
        cd = dram_in(nc, "consts", [128, 384])
        wB_d = dram_in(nc, "wB", [128, NK, NCOLB])
        wsm_d = dram_in(nc, "wsm", [128, NSM])
        vec_d = dram_in(nc, "vecB", [128, NVB])
        pos_d = dram_in(nc, "pos", [32, SL], I32)
        cB_d = dram_in(nc, "cB", [128, NCB])
        y_d = dram_out(nc, "y_out", [192, SL], BF16)
        if layer == 0:
            vf_in, vf_out = None, dram_out(nc, "vf_out", [64, SL])
        else:
            vf_in, vf_out = dram_in(nc, "vf_in", [64, SL]), None
        load_consts(cx, cd)
        if DEBUG_B:
            cx.dbg = {}
        emit_mixer(cx, layer, SL, u_d, wB_d, wsm_d, vec_d, pos_d, cB_d, vf_in, vf_out, y_d)
        if DEBUG_B:
            for nm, t in cx.dbg.items():
                shp = [128] + list(t.ap.shape[1:])
                od = dram_out(nc, "dbg_" + nm, shp, t.ap.dtype)
                cx.S.dma("sp", od, t)
        cx.S.emit()
        print("B stats", cx.S.stats, "scratch peak", cx.scr_peak)
    return nc


def lay_rows(w, nch):
    w = np.asarray(w, np.float32)
    return np.ascontiguousarray(w.reshape(nch, 128, -1).transpose(1, 0, 2))


def prep_B_core(inp, l, c, SL):
    f32 = np.float32
    w_in = np.asarray(inp["w_in"][l], f32)
    hd, hf_ = c // 2, c % 2
    perm = np.concatenate([np.arange(hf_ * 64, hf_ * 64 + 64), np.arange((1 - hf_) * 64, (1 - hf_) * 64 + 64)])
    W = np.zeros((D, NCOLB), f32)

    def put(name, cols):
        W[:, CB[name]:CB[name] + cols.shape[1]] = cols
    put("r", w_in[:, c * 64:(c + 1) * 64])
    put("k", w_in[:, 512 + c * 64:512 + (c + 1) * 64])
    put("v", w_in[:, 1024 + c * 64:1024 + (c + 1) * 64])
    put("wl", w_in[:, 1536:1600])
    put("al", w_in[:, 1600:1664])
    put("gl", w_in[:, 1664:1792])
    if l > 0:
        put("vl", np.asarray(inp["rwkv_vres_down"][l - 1], f32))
    put("cq", w_in[:, OFF_CQ:OFF_CQ + 384])
    put("ckv", w_in[:, OFF_CKV:OFF_CKV + 256])
    kr = w_in[:, OFF_KR:OFF_KR + 32]
    W[:, CB["kr"] + 64:CB["kr"] + 96] = kr
    W[:, CB["krs"] + 64:CB["krs"] + 80] = kr[:, 16:32]
    W[:, CB["krs"] + 80:CB["krs"] + 96] = kr[:, 0:16]
    put("hq", w_in[:, OFF_HQ + hd * 128:OFF_HQ + (hd + 1) * 128])
    put("hf", w_in[:, OFF_HF + hd * 128:OFF_HF + (hd + 1) * 128])
    put("hi", w_in[:, OFF_HI + hd * 128:OFF_HI + (hd + 1) * 128][:, perm])
    put("hg", w_in[:, OFF_HG + hd * 128:OFF_HG + (hd + 1) * 128][:, perm])
    wB = lay_rows(W, NK)
    wsm = np.zeros((128, NSM), f32)
    hs = slice(c * 64, (c + 1) * 64)
    wsm[0:64, SM["w_up"]:SM["w_up"] + 64] = np.asarray(inp["rwkv_w_up"][l], f32)[:, hs]
    wsm[0:64, SM["a_up"]:SM["a_up"] + 64] = np.asarray(inp["rwkv_a_up"][l], f32)[:, hs]
    wsm[:, SM["g_up"]:SM["g_up"] + 64] = np.asarray(inp["rwkv_g_up"][l], f32)[:, hs]
    if l > 0:
        wsm[0:32, SM["vres_up"]:SM["vres_up"] + 64] = np.asarray(inp["rwkv_vres_up"][l - 1], f32)[:, hs]
    uq = np.asarray(inp["mla_w_uq"][l], f32)[:, c * 96:(c + 1) * 96]
    uqs = np.concatenate([uq[:, 0:64], uq[:, 80:96], uq[:, 64:80]], axis=1)
    wsm[:, SM["uq"]:SM["uq"] + 288] = lay_rows(uq, 3).reshape(128, 288)
    wsm[:, SM["uqs"]:SM["uqs"] + 288] = lay_rows(uqs, 3).reshape(128, 288)
    ukv = np.asarray(inp["mla_w_ukv"][l], f32)[:, c * 128:(c + 1) * 128]
    wsm[:, SM["ukv"]:SM["ukv"] + 256] = lay_rows(ukv, 2).reshape(128, 256)
    vec = np.zeros((128, NVB), f32)
    mu = np.asarray(inp["rwkv_mu"][l], f32)
    vec[0:64, VB["mu_r"]] = mu[c * 64:(c + 1) * 64]
    vec[0:64, VB["mu_k"]] = mu[512 + c * 64:512 + (c + 1) * 64]
    vec[0:64, VB["mu_v"]] = mu[1024 + c * 64:1024 + (c + 1) * 64]
    vec[0:64, VB["mu_wl"]] = mu[1536:1600]
    vec[0:64, VB["mu_al"]] = mu[1600:1664]
    vec[:, VB["mu_gl"]] = mu[1664:1792]
    if l > 0:
        vec[0:32, VB["mu_vl"]] = np.asarray(inp["rwkv_vres_mu"][l - 1], f32)
        vec[0:64, VB["v0"]] = np.asarray(inp["rwkv_v0"][l - 1], f32)[hs]
    for nm, key in (("w0", "rwkv_w0"), ("a0", "rwkv_a0"), ("k_k", "rwkv_k_k"), ("k_a", "rwkv_k_a"),
                    ("gn_g", "rwkv_gn_g"), ("gn_b", "rwkv_gn_b")):
        vec[0:64, VB[nm]] = np.asarray(inp[key][l], f32)[hs]
    vec[0:64, VB["r_k"]] = np.asarray(inp["rwkv_r_k"][l], f32)[c]
    vec[:, VB["qg"]:VB["qg"] + 3] = lay_vec(inp["mla_q_norm_g"][l])
    vec[:, VB["kvg"]:VB["kvg"] + 2] = lay_vec(inp["mla_kv_norm_g"][l])
    vec[:, VB["lb"]:VB["lb"] + 4] = np.asarray(inp["hgrn_lower_bounds"], f32)[:, hd * 128:(hd + 1) * 128].T
    vec[:, VB["hng"]] = np.asarray(inp["hgrn_norm_g"][l], f32)[perm]
    pos = np.ascontiguousarray(np.broadcast_to(np.asarray(inp["positions"]).reshape(1, -1)[:, :SL], (32, SL))).astype(np.int32)
    return dict(wB=wB, wsm=wsm, vecB=vec, pos=pos)


def emit_merge(cx, hT, uT, NT, y_d, wgate_d, wouts_d, wo_d, g_post):
    S, ar, ps = cx.S, cx.ar, cx.ps
    m0 = ar.mark()
    wouts = ar.alloc("wouts", [12, D], BF16)
    for j in range(12):
        S.dma("pool", wouts[:, j, :], wouts_d[:, j, :])
    wo = ar.alloc("wo", [NK, D], BF16)
    for k in range(NK):
        S.dma("pool", wo[:, k, :], wo_d[:, k, :])
    wgb = [ar.alloc("wgateb%d" % i, [NK, 384], BF16) for i in range(2)]
    yt = ar.alloc("yt", [12, TT], BF16)
    merged = ar.alloc("merged", [NK, TT], BF16)
    sig = [ar.alloc("sig%d" % i, [TT]) for i in range(3)]
    mt = [ar.alloc("mt%d" % i, [TT]) for i in range(2)]
    z = ar.alloc("z", [NK, TT])
    sq = [ar.alloc("msq%d" % i, [TT], BF16) for i in range(2)]
    rs = ar.alloc("mrs", [TT])
    tmp = [ar.alloc("mtmp%d" % i, [TT]) for i in range(2)]
    ntile = NT // TT
    nld = [0]

    def ld_gate(d):
        S.dma("pool", wgb[nld[0] % 2], wgate_d[d].rearrange("p (k m) -> p k m", k=NK))
        nld[0] += 1

    ld_gate(0)
    nsq = 0
    for t in range(ntile):
        c0 = t * TT
        for j in range(12):
            S.dma("sp", yt[:, j, :], y_d[:, j, c0:c0 + TT])
        for d in range(NK):
            wgt = wgb[(t * NK + d) % 2]
            if t * NK + d + 1 < ntile * NK:
                ld_gate((d + 1) % NK)
            for j in range(3):
                pj = ps[j]
                for k in range(4):
                    S.mm(pj, wouts[:, 4 * j + k, d * 128:(d + 1) * 128], yt[:, 4 * j + k, :], start=(k == 0), stop=(k == 3))
                gj = ps[3 + j]
                for k in range(NK):
                    S.mm(gj, wgt[:, k, j * 128:(j + 1) * 128], uT[:, k, c0:c0 + TT], start=(k == 0), stop=(k == NK - 1))
                S.act(sig[j], gj, AF.Sigmoid)
            S.tt("dve", mt[0], sig[0], ps[0], ALU.mult)
            S.tt("dve", mt[1], sig[1], ps[1], ALU.mult)
            S.tt("pool", mt[0], mt[0], mt[1], ALU.add)
            S.tt("dve", mt[1], sig[2], ps[2], ALU.mult)
            S.tt("pool", merged[:, d, :], mt[0], mt[1], ALU.add)
        for d in range(NK):
            zp = ps[6]
            for k in range(NK):
                S.mm(zp, wo[:, k, d * 128:(d + 1) * 128], merged[:, k, :], start=(k == 0), stop=(k == NK - 1))
            S.act(z[:, d, :], zp, AF.Copy)
            b = sq[nsq % 2]
            nsq += 1
            S.act(b, zp, AF.Square)
            S.mm(ps[7], cx.ones_b, b, start=(d == 0), stop=(d == NK - 1))
        emit_rstd(cx, ps[7], rs, D)
        for d in range(NK):
            tq = tmp[d % 2]
            S.stt("dve", tq, z[:, d, :], g_post[:, d:d + 1], rs, ALU.mult, ALU.mult)
            S.tt("pool", hT[:, d, c0:c0 + TT], hT[:, d, c0:c0 + TT], tq, ALU.add)
    S.barrier()
    ar.release(m0)


def lay_gate(w_in_l):
    g = np.asarray(w_in_l, np.float32)[:, OFF_GATE:OFF_GATE + 3 * D]
    g = g.reshape(NK, 128, 3, NK, 128)
    return np.ascontiguousarray(g.transpose(3, 1, 0, 2, 4).reshape(NK, 128, NK * 384))


def build_T(NT, merge, nxt):
    nc = bass.Bass("TRN2", target_bir_lowering=False)
    with ExitStack() as es:
        cx = make_ctx(nc, es, ARENA_WORDS)
        S, ar = cx.S, cx.ar
        nv = 24 * (int(merge) + int(nxt))
        h_in = dram_in(nc, "h_in", [128, NK, NT])
        cd = dram_in(nc, "consts", [128, 384])
        vd = dram_in(nc, "vec", [128, nv])
        load_consts(cx, cd)
        vec = load_vecs(cx, vd, nv)
        g = ar.alloc("gains", [nv])
        hT = ar.alloc("hT", [NK, NT])
        for k in range(NK):
            S.dma("sp", hT[:, k, :], h_in[:, k, :])
        o = 0
        if merge:
            u_in = dram_in(nc, "u_in", [128, NK, NT], BF16)
            y_in = dram_in(nc, "y_in", [128, 12, NT], BF16)
            wgate = dram_in(nc, "wgate", [NK, 128, NK * 384])
            wouts = dram_in(nc, "wouts", [128, 12, D])
            wo = dram_in(nc, "wo", [128, NK, D])
            wg2 = dram_in(nc, "wg2", [NF, 128, NK * 128])
            wu2 = dram_in(nc, "wu2", [NF, 128, NK * 128])
            wd2 = dram_in(nc, "wd2", [NK, 128, NF * 128])
            mk = ar.mark()
            uTm = ar.alloc("uTm", [NK, NT], BF16)
            for k in range(NK):
                S.dma("sp", uTm[:, k, :], u_in[:, k, :])
            S.ts("dve", g[:, 0:8], vec[:, 0:8], 32.0, ALU.mult)
            S.ts("dve", g[:, 8:16], vec[:, 8:16], 32.0, ALU.mult)
            S.ts("dve", g[:, 16:24], vec[:, 16:24], 16.0, ALU.mult)
            emit_merge(cx, hT, uTm, NT, y_in, wgate, wouts, wo, g[:, 0:8])
            ar.release(mk)
            emit_ffn(cx, hT, NT, wg2, wu2, wd2, g[:, 8:16], g[:, 16:24], G=2)
            o = 24
        h_out = dram_out(nc, "h_out", [128, NK, NT])
        if nxt:
            wg1 = dram_in(nc, "wg1", [NF, 128, NK * 128])
            wu1 = dram_in(nc, "wu1", [NF, 128, NK * 128])
            wd1 = dram_in(nc, "wd1", [NK, 128, NF * 128])
            u_out = dram_out(nc, "u_out", [128, NK, NT], BF16)
            S.ts("dve", g[:, o:o + 8], vec[:, o:o + 8], 32.0, ALU.mult)
            S.ts("dve", g[:, o + 8:o + 16], vec[:, o + 8:o + 16], 16.0, ALU.mult)
            S.ts("dve", g[:, o + 16:o + 24], vec[:, o + 16:o + 24], 32.0, ALU.mult)
            emit_ffn(cx, hT, NT, wg1, wu1, wd1, g[:, o:o + 8], g[:, o + 8:o + 16], G=2)
            uT = ar.alloc("uT", [NK, NT], BF16)
            emit_norm_to(cx, hT, NT, g[:, o + 16:o + 24], uT)
            for k in range(NK):
                S.dma("sp", u_out[:, k, :], uT[:, k, :])
        for k in range(NK):
            S.dma("sp", h_out[:, k, :], hT[:, k, :])
        S.emit()
    return nc


_PROG = {}


def _prog(key, fn):
    if key not in _PROG:
        _PROG[key] = fn()
    return _PROG[key]


def _run(nc, ins):
    res = run_bass_kernel_spmd(nc, ins, core_ids=list(range(NCORES)))
    return res.results


def kernel_impl(inp, SL, depth):
    NT = SL // NCORES
    consts = make_consts()
    cB = make_constsB()
    x = np.asarray(inp["x"], np.float32).reshape(SL, D)
    hs = [lay_act_T(x[c * NT:(c + 1) * NT]) for c in range(NCORES)]
    us = None
    vfirst = None

    def ffn_w(prefix, l, tag):
        return {"wg" + tag: lay_w_gu(inp[prefix + "_w_gate"][l]), "wu" + tag: lay_w_gu(inp[prefix + "_w_up"][l]),
                "wd" + tag: lay_w_d(inp[prefix + "_w_down"][l])}

    def vecs(names_l):
        return np.concatenate([lay_vec(inp[n][l]) for n, l in names_l], axis=1)

    nc = _prog(("T", NT, False, True), lambda: build_T(NT, False, True))
    com = dict(consts=consts, vec=vecs([("ffn1_pre_g", 0), ("ffn1_post_g", 0), ("mix_pre_g", 0)]))
    com.update(ffn_w("ffn1", 0, "1"))
    res = _run(nc, [dict(com, h_in=hs[c]) for c in range(NCORES)])
    hs = [np.asarray(r["h_out"]) for r in res]
    us = [np.asarray(r["u_out"]) for r in res]
    for l in range(depth):
        u_full = np.ascontiguousarray(np.concatenate(us, axis=2))
        ncb = _prog(("B", l if l < 2 else l, SL), lambda: build_B(l, SL))
        ins = []
        for c in range(NCORES):
            d = prep_B_core(inp, l, c, SL)
            d.update(u_full=u_full, consts=consts, cB=cB)
            if l > 0:
                d["vf_in"] = vfirst[c]
            ins.append(d)
        res = _run(ncb, ins)
        if l == 0:
            vfirst = [np.asarray(r["vf_out"]) for r in res]
        ys = [np.asarray(r["y_out"]) for r in res]
        y_ins = []
        for j in range(NCORES):
            yi = np.zeros((128, 12, NT), dtype=ys[0].dtype)
            for br in range(3):
                for q in range(4):
                    for half in range(2):
                        yi[half * 64:(half + 1) * 64, br * 4 + q, :] = ys[2 * q + half][br * 64:(br + 1) * 64, j * NT:(j + 1) * NT]
            y_ins.append(yi)
        last = (l == depth - 1)
        nct = _prog(("T", NT, True, not last), lambda: build_T(NT, True, not last))
        names = [("mix_post_g", l), ("ffn2_pre_g", l), ("ffn2_post_g", l)]
        if not last:
            names += [("ffn1_pre_g", l + 1), ("ffn1_post_g", l + 1), ("mix_pre_g", l + 1)]
        com = dict(consts=consts, vec=vecs(names), wgate=lay_gate(inp["w_in"][l]),
                   wouts=np.concatenate([lay_rows(inp["rwkv_out"][l], 4), lay_rows(inp["mla_out"][l], 4),
                                         lay_rows(inp["hgrn_out"][l], 4)], axis=1),
                   wo=lay_rows(inp["w_o"][l], NK))
        com.update(ffn_w("ffn2", l, "2"))
        if not last:
            com.update(ffn_w("ffn1", l + 1, "1"))
        res = _run(nct, [dict(com, h_in=hs[c], u_in=us[c], y_in=y_ins[c]) for c in range(NCORES)])
        hs = [np.asarray(r["h_out"]) for r in res]
        if not last:
            us = [np.asarray(r["u_out"]) for r in res]
    out = np.concatenate([unlay_act_T(h) for h in hs], axis=0)
    return out.reshape(1, SL, D).astype(np.float32)


def kernel(**inputs):
    return kernel_impl(inputs, 16384, 4)
```

```python
import numpy as np
from contextlib import ExitStack
import ml_dtypes
import concourse.bass as bass
import concourse.mybir as mybir
from concourse.bass_utils import run_bass_kernel_spmd

F32 = mybir.dt.float32
BF16 = mybir.dt.bfloat16
I32 = mybir.dt.int32
AF = mybir.ActivationFunctionType
ALU = mybir.AluOpType
AX = mybir.AxisListType

NCORES = 8
D = 1024
DFF = 2816
NF = DFF // 128
NK = D // 128
EPS = 1e-6
A_COLS = 1792
N_IN = 7584
OFF_CQ, OFF_CKV, OFF_KR = 1792, 2176, 2432
OFF_HQ, OFF_HF, OFF_HI, OFF_HG, OFF_GATE = 2464, 2976, 3488, 4000, 4512
TT = 512


class T:
    __slots__ = ("ap", "keys")

    def __init__(self, ap, keys):
        self.ap = ap
        self.keys = (keys,) if isinstance(keys, str) else tuple(keys)

    def __getitem__(self, idx):
        return T(self.ap[idx], self.keys)

    def k(self, *keys):
        return T(self.ap, keys)

    def rearrange(self, pat, **kw):
        return T(self.ap.rearrange(pat, **kw), self.keys)

    def bcast(self, shape):
        return T(self.ap.to_broadcast(list(shape)), self.keys)


def _keys(*xs):
    ks = []
    for x in xs:
        if isinstance(x, T):
            ks.extend(x.keys)
    return ks


def _ap(x):
    return x.ap if isinstance(x, T) else x


import os as _os
import sys
MAXOPS = int(_os.environ.get("CUTOPS", "100000000"))
EPOCH_KEY = "__epoch__"
SEM_EPOCH = 30000
NDSEM = 6


class Sched:
    COMPUTE = ("pe", "act", "dve", "pool")

    def __init__(self, nc):
        self.nc = nc
        self.ops = []
        self.lines = []

    mute = False

    def add(self, eng, fn, reads, writes, dma=False):
        if self.mute or len(self.ops) >= MAXOPS:
            return
        self.lines.append(sys._getframe(2).f_lineno)
        self.ops.append((eng, fn, tuple(reads) + (EPOCH_KEY,), tuple(writes), dma))

    def barrier(self):
        o = self.bar_tile.ap
        self.lines.append(0)
        self.ops.append(("dve", lambda e: e.memset(o, 0.0), (), (EPOCH_KEY,) + self.bar_tile.keys, False))

    def mm(self, out, lhsT, rhs, start=True, stop=True, **kw):
        o, l, r = out.ap, lhsT.ap, rhs.ap
        self.add("pe", lambda e: e.matmul(o, lhsT=l, rhs=r, start=start, stop=stop, **kw),
                 _keys(lhsT, rhs), _keys(out))

    def transpose(self, out, in_, ident):
        self.mm(out, in_, ident)

    def act(self, out, in_, func, bias=None, scale=None, accum=None):
        o, i = out.ap, in_.ap
        kw = {}
        if bias is not None:
            kw["bias"] = _ap(bias)
        if scale is not None:
            kw["scale"] = _ap(scale)
        if accum is not None:
            kw["accum_out"] = _ap(accum)
        self.add("act", lambda e: e.activation(out=o, in_=i, func=func, **kw),
                 _keys(in_, bias, scale), _keys(out, accum))

    def tt(self, eng, out, a, b, op):
        o, x, y = out.ap, a.ap, b.ap
        self.add(eng, lambda e: e.tensor_tensor(out=o, in0=x, in1=y, op=op), _keys(a, b), _keys(out))

    def ts(self, eng, out, a, s1, op0, s2=None, op1=None):
        o, x = out.ap, a.ap
        s1a, s2a = _ap(s1), _ap(s2)
        if op1 is None:
            self.add(eng, lambda e: e.tensor_scalar(out=o, in0=x, scalar1=s1a, scalar2=None, op0=op0),
                     _keys(a, s1), _keys(out))
        else:
            self.add(eng, lambda e: e.tensor_scalar(out=o, in0=x, scalar1=s1a, scalar2=s2a, op0=op0, op1=op1),
                     _keys(a, s1, s2), _keys(out))

    def stt(self, eng, out, a, scalar, b, op0, op1):
        o, x, y, s = out.ap, a.ap, b.ap, _ap(scalar)
        eng = "dve"
        self.add(eng, lambda e: e.scalar_tensor_tensor(out=o, in0=x, scalar=s, in1=y, op0=op0, op1=op1),
                 _keys(a, scalar, b), _keys(out))

    def copy(self, eng, out, in_):
        o, i = out.ap, in_.ap
        if eng == "act":
            self.add("act", lambda e: e.activation(out=o, in_=i, func=AF.Copy), _keys(in_), _keys(out))
        else:
            self.add(eng, lambda e: e.tensor_copy(out=o, in_=i), _keys(in_), _keys(out))

    def memset(self, eng, out, val):
        o = out.ap
        self.add(eng, lambda e: e.memset(o, val), (), _keys(out))

    def scan(self, out, d0, d1, init, op0, op1):
        o, a, b = out.ap, d0.ap, d1.ap
        self.add("dve", lambda e: e.tensor_tensor_scan(out=o, data0=a, data1=b, initial=init, op0=op0, op1=op1),
                 _keys(d0, d1), _keys(out))

    def reduce(self, eng, out, in_, op, axis=AX.X):
        o, i = out.ap, in_.ap
        self.add(eng, lambda e: e.tensor_reduce(out=o, in_=i, axis=axis, op=op), _keys(in_), _keys(out))

    def recip(self, out, in_):
        o, i = out.ap, in_.ap
        self.add("dve", lambda e: e.reciprocal(out=o, in_=i), _keys(in_), _keys(out))

    def dma(self, q, out, in_):
        o, i = out.ap, in_.ap
        self.add(q, lambda e: e.dma_start(out=o, in_=i), _keys(in_), _keys(out), dma=True)

    def emit(self):
        nc = self.nc
        ops = self.ops
        n = len(ops)
        last_w = {}
        rd_eng = {}
        rd_dma = {}
        deps = [None] * n
        signal = [False] * n
        for i, (eng, fn, rd, wr, dma) in enumerate(ops):
            cand = []
            for k in rd:
                j = last_w.get(k)
                if j is not None:
                    cand.append((j, 0))
            for k in wr:
                j = last_w.get(k)
                if j is not None:
                    cand.append((j, 1))
                for j in rd_eng.get(k, {}).values():
                    cand.append((j, 2))
                for j in rd_dma.get(k, ()):
                    cand.append((j, 2))
            best = {}
            dl = set()
            for j, kind in cand:
                if j == i:
                    continue
                ej, _, _, _, dj = ops[j]
                if dj:
                    dl.add(j)
                    continue
                if (not dma) and ej == eng:
                    if eng == "pe" or kind != 0:
                        continue
                if best.get(ej, -1) < j:
                    best[ej] = j
            dd = set(best.values()) | dl
            deps[i] = dd
            for j in dd:
                signal[j] = True
            for k in rd:
                if dma:
                    rd_dma.setdefault(k, []).append(i)
                else:
                    rd_eng.setdefault(k, {})[eng] = i
            for k in wr:
                last_w[k] = i
                rd_eng[k] = {}
                rd_dma[k] = []
        cnt = {e: 0 for e in self.COMPUTE}
        sigval = [None] * n
        dcnt = {}
        for i, (eng, fn, rd, wr, dma) in enumerate(ops):
            if dma:
                q = dcnt.get(eng, 0)
                dcnt[eng] = q + 1
                sigval[i] = ("d", eng, q % NDSEM, 16 * (q // NDSEM + 1), q)
            elif signal[i]:
                cnt[eng] += 1
                sigval[i] = ("c", eng, cnt[eng])
        self.stats = dict(n=n, cnt=dict(cnt), dcnt=dict(dcnt))
        by_eng = {}
        for i, op in enumerate(ops):
            by_eng.setdefault(op[0], []).append(i)
        with ExitStack() as es:
            csems = {}
            for e in self.COMPUTE:
                ne = max(1, (cnt[e] + SEM_EPOCH - 1) // SEM_EPOCH)
                csems[e] = [es.enter_context(nc.semaphore("c_%s_%d" % (e, t))) for t in range(ne)]
            dsems = {}
            for e in dcnt:
                dsems[e] = [es.enter_context(nc.semaphore("d_%s_%d" % (e, t))) for t in range(NDSEM)]
            block = es.enter_context(nc.Block())

            def make(engname):
                mine = by_eng.get(engname, [])

                def body(e):
                    cw = {x: 0 for x in self.COMPUTE}
                    dw = {}
                    for i in mine:
                        eng, fn, rd, wr, dma = ops[i]
                        if dma:
                            _, _, slot, val, q = sigval[i]
                            if q >= NDSEM:
                                key = (eng, slot)
                                if dw.get(key, 0) < val - 16:
                                    e.wait_ge(dsems[eng][slot], val - 16)
                                    dw[key] = val - 16
                        for j in sorted(deps[i]):
                            sv = sigval[j]
                            if sv[0] == "c":
                                _, ej, c = sv
                                if cw[ej] >= c:
                                    continue
                                cw[ej] = c
                                e.wait_ge(csems[ej][(c - 1) // SEM_EPOCH], (c - 1) % SEM_EPOCH + 1)
                            else:
                                _, ej, slot, val, q = sv
                                key = (ej, slot)
                                if dw.get(key, 0) >= val:
                                    continue
                                dw[key] = val
                                e.wait_ge(dsems[ej][slot], val)
                        inst = fn(e)
                        sv = sigval[i]
                        if sv is not None:
                            if sv[0] == "c":
                                c = sv[2]
                                inst.then_inc(csems[eng][(c - 1) // SEM_EPOCH], 1)
                            else:
                                inst.then_inc(dsems[eng][sv[2]], 16)
                    if engname in dcnt:
                        tot = dcnt[engname]
                        for slot in range(NDSEM):
                            uses = (tot - slot + NDSEM - 1) // NDSEM if tot > slot else 0
                            if uses > 0 and dw.get((engname, slot), 0) < 16 * uses:
                                e.wait_ge(dsems[engname][slot], 16 * uses)
                return body

            block.tensor(make("pe"))
            block.scalar(make("act"))
            block.vector(make("dve"))
            block.gpsimd(make("pool"))
            block.sync(make("sp"))


class Arena:
    def __init__(self, nc, es, words):
        self.t = es.enter_context(nc.sbuf_tensor("arena", [128, words], F32))
        self.words = words
        self.off = 0
        self.uid = 0

    def alloc(self, name, free_shape, dtype=F32, nslots=None):
        nel = int(np.prod(free_shape))
        w = (nel + 1) // 2 if dtype == BF16 else nel
        w = (w + 7) // 8 * 8
        assert self.off + w <= self.words, ("SBUF arena overflow", name, self.off, w, self.words)
        ap = self.t[:, self.off:self.off + w]
        if dtype != F32:
            ap = ap.bitcast(dtype)
        ap = ap[:, 0:nel]
        if len(free_shape) == 2:
            ap = ap.rearrange("p (a b) -> p a b", a=free_shape[0])
        elif len(free_shape) == 3:
            ap = ap.rearrange("p (a b c) -> p a b c", a=free_shape[0], b=free_shape[1])
        self.off += w
        self.uid += 1
        return T(ap, "%s#%d" % (name, self.uid))

    def mark(self):
        return self.off

    def release(self, m):
        self.off = m


class Scratch:
    def __init__(self, ar, ngran):
        self.base = ar.off
        self.t = ar.t
        self.ngran = ngran
        ar.off += ngran * 512
        assert ar.off <= ar.words, ("scratch overflow", ar.off, ar.words)
        self.pos = 0
        self.peak = 0

    def reset(self):
        self.pos = 0

    def get(self, free_shape, dtype=F32):
        nel = int(np.prod(free_shape))
        w = (nel + 1) // 2 if dtype == BF16 else nel
        g = (w + 511) // 512
        assert self.pos + g <= self.ngran, ("scratch granules exhausted", self.pos, g, self.ngran)
        o = self.base + self.pos * 512
        ap = self.t[:, o:o + g * 512]
        if dtype != F32:
            ap = ap.bitcast(dtype)
        ap = ap[:, 0:nel]
        if len(free_shape) == 2:
            ap = ap.rearrange("p (a b) -> p a b", a=free_shape[0])
        elif len(free_shape) == 3:
            ap = ap.rearrange("p (a b c) -> p a b c", a=free_shape[0], b=free_shape[1])
        keys = tuple("scr%d" % (self.pos + i) for i in range(g))
        self.pos += g
        self.peak = max(self.peak, self.pos)
        return T(ap, keys)


class Ctx:
    pass


def make_ctx(nc, es, arena_words):
    cx = Ctx()
    cx.nc = nc
    cx.S = Sched(nc)
    cx.ar = Arena(nc, es, arena_words)
    cx.ps = [T(es.enter_context(nc.psum_tensor("psb%d" % i, [128, 512], F32))[:], "psb%d" % i) for i in range(8)]
    cx.S.bar_tile = cx.ar.alloc("bar", [8])
    cx.dbg = None
    return cx


def dram_in(nc, name, shape, dt=F32):
    return T(nc.dram_tensor(name, list(shape), dt, kind="ExternalInput").ap(), "dram:" + name)


def dram_out(nc, name, shape, dt=F32):
    return T(nc.dram_tensor(name, list(shape), dt, kind="ExternalOutput").ap(), "dram:" + name)


def load_consts(cx, cdram):
    S, ar = cx.S, cx.ar
    cx.ident = ar.alloc("ident", [128])
    cx.ones_f = ar.alloc("ones_f", [128])
    cx.ones_b = ar.alloc("ones_b", [128], BF16)
    S.dma("sp", cx.ident, cdram[:, 0:128])
    S.dma("sp", cx.ones_f, cdram[:, 128:256])
    S.copy("dve", cx.ones_b, cx.ones_f)
    cx.ident_b = ar.alloc("ident_b", [128], BF16)
    S.copy("dve", cx.ident_b, cx.ident)


def emit_rstd(cx, ss_ps, rs, n_feat, rows=128):
    cx.S.act(rs[0:rows], ss_ps[0:rows], AF.Sqrt, bias=float(n_feat * EPS))
    cx.S.recip(rs[0:rows], rs[0:rows])


def emit_ffn(cx, hT, NT, wg_d, wu_d, wd_d, g_pre, g_post, G=1):
    S, ar, ps = cx.S, cx.ar, cx.ps
    if NT % (G * TT) != 0:
        G = 1
    GW = G * TT
    m0 = ar.mark()
    xn = ar.alloc("xn", [NK, GW], BF16)
    hid = ar.alloc("hid", [NF, GW], BF16)
    yb = ar.alloc("ffy", [NK, GW])
    sq = [ar.alloc("sq%d" % i, [TT], BF16) for i in range(2)]
    rs = [ar.alloc("rs%d" % i, [TT]) for i in range(G)]
    sg = [ar.alloc("sg%d" % i, [TT]) for i in range(2)]
    tmp = [ar.alloc("ftmp%d" % i, [TT]) for i in range(2)]
    NWB = 3
    wgb = [ar.alloc("wgb%d" % i, [NK, 128], BF16) for i in range(NWB)]
    wub = [ar.alloc("wub%d" % i, [NK, 128], BF16) for i in range(NWB)]
    wdb = [ar.alloc("wdb%d" % i, [NF, 128], BF16) for i in range(2)]
    ss_ps = [ps[6], ps[7]]
    nsq = 0
    for g in range(NT // GW):
        for s in range(G):
            c0 = g * GW + s * TT
            for k in range(NK):
                b = sq[nsq % 2]
                nsq += 1
                S.act(b, hT[:, k, c0:c0 + TT], AF.Square)
                S.mm(ss_ps[s], cx.ones_b, b, start=(k == 0), stop=(k == NK - 1))
            emit_rstd(cx, ss_ps[s], rs[s], D)
            for k in range(NK):
                S.stt("dve" if k % 2 == 0 else "pool", xn[:, k, s * TT:(s + 1) * TT], hT[:, k, c0:c0 + TT],
                      g_pre[:, k:k + 1], rs[s], ALU.mult, ALU.mult)
        it = 0

        def ld_gu(f):
            S.dma("pool", wgb[f % NWB], wg_d[f].rearrange("p (k m) -> p k m", k=NK))
            S.dma("pool", wub[f % NWB], wu_d[f].rearrange("p (k m) -> p k m", k=NK))

        def ld_d(d):
            S.dma("pool", wdb[d % 2], wd_d[d].rearrange("p (f m) -> p f m", f=NF))

        for f in range(NWB - 1):
            ld_gu(f)
        for f in range(NF):
            wgt, wut = wgb[f % NWB], wub[f % NWB]
            if f + NWB - 1 < NF:
                ld_gu(f + NWB - 1)
            if f == NF - 2:
                ld_d(0)
            for s in range(G):
                gp, up = ps[(it % 2) * 2], ps[(it % 2) * 2 + 1]
                sgt = sg[it % 2]
                it += 1
                for k in range(NK):
                    S.mm(gp, wgt[:, k, :], xn[:, k, s * TT:(s + 1) * TT], start=(k == 0), stop=(k == NK - 1))
                for k in range(NK):
                    S.mm(up, wut[:, k, :], xn[:, k, s * TT:(s + 1) * TT], start=(k == 0), stop=(k == NK - 1))
                S.act(sgt, gp, AF.Silu)
                S.tt("dve", hid[:, f, s * TT:(s + 1) * TT], sgt, up, ALU.mult)
        it = 0
        for d in range(NK):
            wdt = wdb[d % 2]
            if d + 1 < NK:
                ld_d(d + 1)
            for s in range(G):
                yp = ps[4 + it % 2]
                it += 1
                for f in range(NF):
                    S.mm(yp, wdt[:, f, :], hid[:, f, s * TT:(s + 1) * TT], start=(f == 0), stop=(f == NF - 1))
                S.act(yb[:, d, s * TT:(s + 1) * TT], yp, AF.Copy)
                b = sq[nsq % 2]
                nsq += 1
                S.act(b, yp, AF.Square)
                S.mm(ss_ps[s], cx.ones_b, b, start=(d == 0), stop=(d == NK - 1))
        for s in range(G):
            c0 = g * GW + s * TT
            emit_rstd(cx, ss_ps[s], rs[s], D)
            for d in range(NK):
                t = tmp[d % 2]
                S.stt("dve", t, yb[:, d, s * TT:(s + 1) * TT], g_post[:, d:d + 1], rs[s], ALU.mult, ALU.mult)
                S.tt("pool", hT[:, d, c0:c0 + TT], hT[:, d, c0:c0 + TT], t, ALU.add)
    S.barrier()
    ar.release(m0)


def emit_norm_to(cx, hT, NT, g32, out_bf, rows_feat=D):
    S, ar, ps = cx.S, cx.ar, cx.ps
    m0 = ar.mark()
    sq = [ar.alloc("nsq%d" % i, [TT], BF16) for i in range(2)]
    rs = ar.alloc("nrs", [TT])
    n = 0
    for t in range(NT // TT):
        c0 = t * TT
        for k in range(NK):
            b = sq[n % 2]
            n += 1
            S.act(b, hT[:, k, c0:c0 + TT], AF.Square)
            S.mm(ps[6], cx.ones_b, b, start=(k == 0), stop=(k == NK - 1))
        emit_rstd(cx, ps[6], rs, D)
        for k in range(NK):
            S.stt("dve" if k % 2 == 0 else "pool", out_bf[:, k, c0:c0 + TT], hT[:, k, c0:c0 + TT],
                  g32[:, k:k + 1], rs, ALU.mult, ALU.mult)
    S.barrier()
    ar.release(m0)


def lay_vec(v):
    v = np.asarray(v, np.float32)
    return np.ascontiguousarray(v.reshape(-1, 128).T)


def lay_w_gu(w):
    w = np.asarray(w, np.float32)
    return np.ascontiguousarray(w.reshape(NK, 128, NF, 128).transpose(2, 1, 0, 3).reshape(NF, 128, NK * 128))


def lay_w_d(w):
    w = np.asarray(w, np.float32)
    return np.ascontiguousarray(w.reshape(NF, 128, NK, 128).transpose(2, 1, 0, 3).reshape(NK, 128, NF * 128))


def lay_act_T(x):
    x = np.asarray(x)
    nt = x.shape[0]
    return np.ascontiguousarray(x.reshape(nt, -1, 128).transpose(2, 1, 0))


def unlay_act_T(xT):
    p, n, nt = xT.shape
    return np.ascontiguousarray(xT.transpose(2, 1, 0).reshape(nt, n * 128))


def make_consts():
    c = np.zeros((128, 384), np.float32)
    c[:, 0:128] = np.eye(128, dtype=np.float32)
    c[:, 128:256] = 1.0
    return c


ARENA_WORDS = 53000


def load_vecs(cx, vd, ncol):
    v = cx.ar.alloc("vecs", [ncol])
    cx.S.dma("sp", v, vd)
    return v


def build_TA(NT):
    nc = bass.Bass("TRN2", target_bir_lowering=False)
    with ExitStack() as es:
        cx = make_ctx(nc, es, ARENA_WORDS)
        S, ar = cx.S, cx.ar
        h_in = dram_in(nc, "h_in", [128, NK, NT])
        cd = dram_in(nc, "consts", [128, 384])
        vd = dram_in(nc, "vec", [128, 24])
        wg = dram_in(nc, "wg", [NF, 128, NK * 128])
        wu = dram_in(nc, "wu", [NF, 128, NK * 128])
        wd = dram_in(nc, "wd", [NK, 128, NF * 128])
        h_out = dram_out(nc, "h_out", [128, NK, NT])
        u_out = dram_out(nc, "u_out", [128, NK, NT], BF16)
        load_consts(cx, cd)
        vec = load_vecs(cx, vd, 24)
        hT = ar.alloc("hT", [NK, NT])
        for k in range(NK):
            S.dma("sp", hT[:, k, :], h_in[:, k, :])
        g = ar.alloc("gains", [24])
        S.ts("dve", g[:, 0:8], vec[:, 0:8], 32.0, ALU.mult)
        S.ts("dve", g[:, 8:16], vec[:, 8:16], 16.0, ALU.mult)
        S.ts("dve", g[:, 16:24], vec[:, 16:24], 32.0, ALU.mult)
        emit_ffn(cx, hT, NT, wg, wu, wd, g[:, 0:8], g[:, 8:16])
        uT = ar.alloc("uT", [NK, NT], BF16)
        emit_norm_to(cx, hT, NT, g[:, 16:24], uT)
        for k in range(NK):
            S.dma("sp", h_out[:, k, :], hT[:, k, :])
            S.dma("sp", u_out[:, k, :], uT[:, k, :])
        S.emit()
    return nc


CB = dict(r=0, k=64, v=128, wl=192, al=256, gl=320, vl=448, cq=512, ckv=896, kr=1152, krs=1248,
          hq=1344, hf=1472, hi=1600, hg=1728)
NCOLB = 1856
SM = dict(w_up=0, a_up=64, g_up=128, vres_up=192, uq=256, uqs=544, ukv=832)
NSM = 1088
VB = dict(mu_r=0, mu_k=1, mu_v=2, mu_wl=3, mu_al=4, mu_gl=5, mu_vl=6, w0=7, a0=8, k_k=9, k_a=10, r_k=11,
          gn_g=12, gn_b=13, v0=14, qg=15, kvg=18, lb=20, hng=24)
NVB = 25
CC = dict(maskAT=0, maskN=512, I64=1024, triH=1536, triA=1664, invf=1792, sgn=1793, sel96=1794)
NCB = 1896
C0 = float(np.exp(-0.5))
A_GN_EPS = 64e-5
TWO_PI = float(2 * np.pi)
PI = float(np.pi)


def make_constsB():
    c = np.zeros((128, NCB), np.float32)
    m = np.zeros((128, 128), np.float32)
    il = np.arange(64)[:, None]
    tl = np.arange(64)[None, :]
    for half in range(2):
        m[half * 64:(half + 1) * 64, 0:64] = (il < tl)
        m[half * 64:(half + 1) * 64, 64:128] = (il <= tl)
    c[:, 0:512] = np.tile(m, (1, 4))
    mn = (np.arange(64)[:, None] > np.arange(64)[None, :]).astype(np.float32)
    c[0:64, 512:1024] = np.tile(mn, (1, 8))
    c[0:64, 1024:1536] = np.tile(np.eye(64, dtype=np.float32), (1, 8))
    th = (np.arange(32)[:, None] <= np.arange(32)[None, :]).astype(np.float32)
    c[:, 1536:1664] = np.tile(np.tile(th, (4, 1)), (1, 4))
    c[:, 1664:1792] = (np.arange(128)[:, None] <= np.arange(128)[None, :])
    invf = (10000.0 ** (-np.arange(0, 32, 2, dtype=np.float32) / 32)).astype(np.float32)
    c[64:80, CC["invf"]] = invf
    c[80:96, CC["invf"]] = invf
    c[64:80, CC["sgn"]] = -1.0
    c[80:96, CC["sgn"]] = 1.0
    c[0:96, CC["sel96"] + 96] = 1.0
    return c


import os
PHASES = os.environ.get("MIX_PHASES", "rmh")


def emit_cumsum(S, src, bufA, bufB, rows, nch, C):
    v = lambda t: t[rows].rearrange("p (c t) -> p c t", c=nch)
    cur, bufs, i, s = src, [bufA, bufB], 0, 1
    while s < C:
        nxt = bufs[i % 2]
        S.copy("pool", v(nxt)[:, :, 0:s], v(cur)[:, :, 0:s])
        S.tt("dve", v(nxt)[:, :, s:C], v(cur)[:, :, s:C], v(cur)[:, :, 0:C - s], ALU.add)
        cur = nxt
        i += 1
        s *= 2
    return cur, bufs[i % 2]


def emit_mixer(cx, layer, SL, u_d, wB_d, wsm_d, vec_d, pos_d, cB_d, vf_in_d, vf_out_d, y_d):
    S, ar, ps = cx.S, cx.ar, cx.ps
    L0 = (layer == 0)
    ntile = SL // TT
    nblk = SL // 128
    m_all = ar.mark()
    SCALE = float(96 ** -0.5)
    wB = ar.alloc("wB", [NK, NCOLB], BF16)
    for k in range(NK):
        S.dma("pool", wB[:, k, :], wB_d[:, k, :])
    wsm = ar.alloc("wsm", [NSM])
    S.dma("sp", wsm, wsm_d)
    wuq = ar.alloc("wuq", [3, 96], BF16)
    wuqs = ar.alloc("wuqs", [3, 96], BF16)
    wukv = ar.alloc("wukv", [2, 128], BF16)
    S.copy("dve", wuq, wsm[:, SM["uq"]:SM["uq"] + 288].rearrange("p (a b) -> p a b", a=3))
    S.copy("dve", wuqs, wsm[:, SM["uqs"]:SM["uqs"] + 288].rearrange("p (a b) -> p a b", a=3))
    S.copy("dve", wukv, wsm[:, SM["ukv"]:SM["ukv"] + 256].rearrange("p (a b) -> p a b", a=2))
    w_up = wsm[0:64, SM["w_up"]:SM["w_up"] + 64]
    a_up = wsm[0:64, SM["a_up"]:SM["a_up"] + 64]
    g_up = wsm[:, SM["g_up"]:SM["g_up"] + 64]
    vres_up = wsm[0:32, SM["vres_up"]:SM["vres_up"] + 64]
    vec = ar.alloc("vecB", [NVB])
    S.dma("sp", vec, vec_d)
    cB = ar.alloc("cB", [NCB])
    S.dma("sp", cB, cB_d)
    triA = ar.alloc("triA", [128], BF16)
    S.copy("dve", triA, cB[:, CC["triA"]:CC["triA"] + 128])
    sel96 = ar.alloc("sel96", [97], BF16)
    S.copy("dve", sel96, cB[:, CC["sel96"]:CC["sel96"] + 97])
    maskAT = cB[:, CC["maskAT"]:CC["maskAT"] + 512]
    maskN = cB[0:64, CC["maskN"]:CC["maskN"] + 512]
    I64 = cB[0:64, CC["I64"]:CC["I64"] + 512]
    triH = cB[:, CC["triH"]:CC["triH"] + 128].rearrange("p (b t) -> p b t", b=4)
    invf = cB[:, CC["invf"]:CC["invf"] + 1]
    sgn = cB[:, CC["sgn"]:CC["sgn"] + 1]
    ones64 = cx.ones_f[0:64, 0:64]
    id64 = cx.ident_b[0:64, 0:64]

    def V(name, rows=128):
        return vec[0:rows, VB[name]:VB[name] + 1]

    dv = ar.alloc("dvec", [16])
    S.ts("dve", dv[:, 0:7], vec[:, 0:7], -1.0, ALU.mult, 1.0, ALU.add)
    S.ts("dve", dv[:, 7:8], vec[:, VB["k_a"]:VB["k_a"] + 1], -1.0, ALU.mult, 1.0, ALU.add)
    S.ts("dve", dv[:, 8:11], vec[:, VB["qg"]:VB["qg"] + 3], float(np.sqrt(384.0)), ALU.mult)
    S.ts("dve", dv[:, 11:13], vec[:, VB["kvg"]:VB["kvg"] + 2], 16.0, ALU.mult)
    S.ts("dve", dv[:, 15:16], vec[:, VB["hng"]:VB["hng"] + 1], float(np.sqrt(128.0)), ALU.mult)
    if L0:
        S.memset("dve", dv[:, 13:14], 0.0)
    else:
        le = ar.alloc("lbe", [8])
        S.act(le[:, 0:4], vec[:, VB["lb"]:VB["lb"] + 4], AF.Exp)
        S.reduce("dve", le[:, 4:5], le[:, 0:4], ALU.add)
        S.recip(le[:, 4:5], le[:, 4:5])
        S.reduce("dve", le[:, 5:6], le[:, 1:layer + 1], ALU.add)
        S.tt("dve", dv[:, 13:14], le[:, 5:6], le[:, 4:5], ALU.mult)
    S.ts("dve", dv[:, 14:15], dv[:, 13:14], -1.0, ALU.mult, 1.0, ALU.add)
    mpi = ar.alloc("mpi", [8])
    S.memset("dve", mpi, -PI)

    KT = ar.alloc("KT", [SL], BF16)
    Vaug = ar.alloc("Vaug", [nblk, 65], BF16)
    S.memset("pool", KT[96:97, :], 1.0)
    S.memset("pool", Vaug[:, :, 64:65], 1.0)
    kmax2 = ar.alloc("kmax2", [8])
    S.memset("dve", kmax2[96:97, :], 0.0)
    SV = ar.alloc("SV", [9, 64], BF16)
    SVf = ar.alloc("SVf", [9, 64])
    S.memset("dve", SV[0:64, 0, :], 0.0)
    S.memset("dve", SVf[0:64, 0, :], 0.0)
    Sh = ar.alloc("Sh", [17, 128])
    S.memset("pool", Sh[:, 0, :], 0.0)
    shifted = [("r", 64), ("k", 64), ("v", 64), ("wl", 64), ("al", 64), ("gl", 128)] + ([] if L0 else [("vl", 32)])
    halo = ar.alloc("halo", [8])
    S.memset("dve", halo, 0.0)
    ut = [ar.alloc("ut%d" % i, [NK, TT], BF16) for i in range(2)]
    ya_o = ar.alloc("ya_o", [TT], BF16)
    yb_o = ar.alloc("yb_o", [TT], BF16)
    yc_o = ar.alloc("yc_o", [TT], BF16)
    gC = ar.alloc("gC", [16])
    gCh = ar.alloc("gCh", [16])
    QT = ar.alloc("QT", [TT], BF16)
    Pb = [ar.alloc("Pb%d" % i, [TT], BF16) for i in range(2)]
    rd = ar.alloc("rd", [TT])
    ob = ar.alloc("ob", [TT])
    sc = Scratch(ar, (ar.words - ar.off) // 512)

    rot = [0]

    def bank():
        b = ps[rot[0] % 4]
        rot[0] += 1
        return b

    def proj(off, M, utile):
        pb = bank()
        for k in range(NK):
            S.mm(pb[0:M, :], wB[:, k, off:off + M], utile[:, k, :], start=(k == 0), stop=(k == NK - 1))
        return pb

    S.dma("sp", ut[0], u_d[:, :, 0:TT])
    for it in range(ntile):
        c0 = it * TT
        u_t = ut[it % 2]
        S.mute = False
        if it + 1 < ntile:
            S.dma("sp", ut[(it + 1) % 2], u_d[:, :, c0 + TT:c0 + 2 * TT])
        R = slice(0, 64)
        RR = slice(64, 96)

        def do_rwkv():
            S.mute = "r" not in PHASES
            sc.reset()
            A = lambda n=TT, dt=F32: sc.get([n], dt)
            rws = [A(), A()]
            tmpm = A()
            xr, xk, xwl, xal, xgl = A(), A(), A(), A(), A()
            xvf = A(TT + 64)
            xv = xvf[:, 64:64 + TT]
            xvl = A()
            cs, a_t, g_t, kk, bb, bon = A(), A(), A(), A(), A(), A()
            E1, E2, E3, E4 = rws[0], rws[1], tmpm, A()
            t1, t2 = A(), A()
            WL = sc.get([8, 64], BF16)
            RA = sc.get([8, 64], BF16)
            ARB = sc.get([8, 64], BF16)
            BK = sc.get([8, 2, 64], BF16)
            BKh = sc.get([8, 2, 64], BF16)
            xvb = sc.get([TT + 64], BF16)
            Nm = [sc.get([8, 64], BF16) for i in range(2)]
            Mm = [sc.get([8, 64], BF16) for i in range(2)]
            Pm = [sc.get([8, 64]) for i in range(2)]
            Pmb = [sc.get([8, 64], BF16) for i in range(2)]
            BKhT = sc.get([8, 64], BF16)
            UV = sc.get([8, 64], BF16)
            Wsb = A(64, BF16)
            yT = xal
            vft = xwl
            mixed = {"r": xr, "k": xk, "v": xv, "wl": xwl, "al": xal, "gl": xgl, "vl": xvl}
            for gi, (nm, M) in enumerate(shifted):
                pb = proj(CB[nm], M, u_t)
                rw = rws[gi % 2]
                S.copy("act", rw[0:M], pb[0:M, :])
                mu = vec[0:M, gi:gi + 1]
                omu = dv[0:M, gi:gi + 1]
                S.ts("dve", tmpm[0:M, 1:TT], rw[0:M, 0:TT - 1], mu, ALU.mult)
                S.ts("dve", tmpm[0:M, 0:1], halo[0:M, gi:gi + 1], mu, ALU.mult)
                S.stt("dve", mixed[nm][0:M], rw[0:M], omu, tmpm[0:M], ALU.mult, ALU.add)
                S.copy("pool", halo[0:M, gi:gi + 1], rw[0:M, TT - 1:TT])
            S.memset("pool", xvf[0:64, 0:64], 0.0)
            R = slice(0, 64)
            S.act(xwl[R], xwl[R], AF.Tanh)
            pb = bank()
            S.mm(pb[R, :], w_up, xwl[R])
            S.act(t1[R], pb[R, :], AF.Sigmoid, bias=V("w0", 64))
            cs, t2 = emit_cumsum(S, t1, cs, t2, R, 8, 64)
            S.tt("dve", t2[R], cs[R], t1[R], ALU.subtract)
            S.act(E1[R], cs[R], AF.Exp, scale=-C0)
            S.act(E2[R], cs[R], AF.Exp, scale=C0)
            S.act(E3[R], t2[R], AF.Exp, scale=-C0)
            cs3 = cs[R].rearrange("p (c t) -> p c t", c=8)
            S.tt("dve", t2[R].rearrange("p (c t) -> p c t", c=8), cs3[:, :, 63:64].bcast([64, 8, 64]), cs3, ALU.subtract)
            S.act(E4[R], t2[R], AF.Exp, scale=-C0)
            S.act(gC[R, 0:8], cs3[:, :, 63:64].rearrange("p c o -> p (c o)"), AF.Exp, scale=-C0)
            pb = bank()
            S.mm(pb[R, :], a_up, xal[R])
            S.act(a_t[R], pb[R, :], AF.Sigmoid, bias=V("a0", 64))
            S.act(xgl, xgl, AF.Sigmoid)
            pb = bank()
            S.mm(pb[R, :], g_up, xgl)
            S.copy("act", g_t[R], pb[R, :])
            if L0:
                S.dma("sp", vf_out_d[:, c0:c0 + TT], xv[R])
            else:
                S.dma("sp", vft[R], vf_in_d[:, c0:c0 + TT])
                pb = bank()
                S.mm(pb[R, :], vres_up, xvl[0:32])
                S.act(t1[R], pb[R, :], AF.Sigmoid, bias=V("v0", 64))
                S.tt("dve", vft[R], vft[R], xv[R], ALU.subtract)
                S.tt("dve", vft[R], vft[R], t1[R], ALU.mult)
                S.tt("dve", xv[R], xv[R], vft[R], ALU.add)
            S.ts("dve", kk[R], xk[R], V("k_k", 64), ALU.mult)
            S.tt("dve", t1[R], kk[R], kk[R], ALU.mult)
            pb = bank()
            S.mm(pb[R, :], ones64, t1[R])
            S.act(t1[R], pb[R, :], AF.Sqrt)
            S.ts("dve", t1[R], t1[R], 1e-12, ALU.max)
            S.recip(t1[R], t1[R])
            S.tt("dve", kk[R], kk[R], t1[R], ALU.mult)
            S.ts("dve", t1[R], a_t[R], V("k_a", 64), ALU.mult, dv[R, 7:8], ALU.add)
            S.tt("dve", xk[R], xk[R], t1[R], ALU.mult)
            S.stt("dve", t1[R], xr[R], V("r_k", 64), xk[R], ALU.mult, ALU.mult)
            pb = bank()
            S.mm(pb[R, :], ones64, t1[R])
            S.tt("dve", bon[R], pb[R, :], xv[R], ALU.mult)
            S.tt("dve", bb[R], kk[R], a_t[R], ALU.mult)
            v3 = lambda t: t[R].rearrange("p (c t) -> p c t", c=8)
            S.stt("dve", WL[R], v3(kk), -1.0, v3(E3), ALU.mult, ALU.mult)
            S.tt("dve", RA[R], v3(xr), v3(E1), ALU.mult)
            S.copy("pool", xvb[R], xvf[R])
            S.tt("dve", BK[R, :, 0, :], v3(bb), v3(E2), ALU.mult)
            S.tt("dve", BK[R, :, 1, :], v3(xk), v3(E2), ALU.mult)
            S.tt("dve", BKh[R, :, 0, :], v3(bb), v3(E4), ALU.mult)
            S.tt("dve", BKh[R, :, 1, :], v3(xk), v3(E4), ALU.mult)
            f2 = lambda t, c: t[R, c].rearrange("p a b -> p (a b)")
            m4 = maskAT.rearrange("p (c h t) -> p c h t", c=4, h=2)
            for half in range(2):
                pb = bank()
                for cc in range(4):
                    c = half * 4 + cc
                    S.mm(pb[:, cc * 128:cc * 128 + 64], f2(BK, c), WL[R, c, :])
                    S.mm(pb[:, cc * 128 + 64:(cc + 1) * 128], f2(BK, c), RA[R, c, :])
                p4 = pb.rearrange("p (c h t) -> p c h t", c=4, h=2)
                cs_ = slice(half * 4, half * 4 + 4)
                S.tt("dve", Mm[0][R, cs_, :], p4[R, :, 0, :], m4[R, :, 0, :], ALU.mult)
                S.tt("dve", ARB[R, cs_, :], p4[R, :, 1, :], m4[R, :, 1, :], ALU.mult)
                S.tt("dve", WL[64:128, cs_, :], p4[64:128, :, 0, :], m4[64:128, :, 0, :], ALU.mult)
                S.tt("dve", RA[64:128, cs_, :], p4[64:128, :, 1, :], m4[64:128, :, 1, :], ALU.mult)
            pb = bank()
            for c in range(8):
                S.mm(pb[R, c * 64:(c + 1) * 64], WL[R, c, :], BK[R, c, 0, :])
            S.tt("dve", Nm[0][R].rearrange("p a b -> p (a b)"), pb[R, :], maskN, ALU.mult)
            S.tt("pool", Pm[0][R].rearrange("p a b -> p (a b)"), Mm[0][R].rearrange("p a b -> p (a b)"), I64, ALU.add)
            S.copy("pool", Pmb[0][R], Pm[0][R])
            cur = 0
            for rnd in range(5):
                Mc, Nc, Pc = Mm[cur], Nm[cur], Pm[cur]
                Mn, Nn, Pn = Mm[1 - cur], Nm[1 - cur], Pm[1 - cur]
                Pcb, Pnb = Pmb[cur], Pmb[1 - cur]
                pbm = bank()
                pbn = bank()
                for c in range(8):
                    S.mm(pbm[R, c * 64:(c + 1) * 64], Nc[R, c, :], Mc[R, c, :])
                for c in range(8):
                    S.mm(pbn[R, c * 64:(c + 1) * 64], Mc[R, c, :], Nc[R, c, :])
                S.copy("act", Mn[R].rearrange("p a b -> p (a b)"), pbm[R, :])
                S.copy("dve", Nn[R].rearrange("p a b -> p (a b)"), pbn[R, :])
                pbp = bank()
                for c in range(8):
                    S.mm(pbp[R, c * 64:(c + 1) * 64], Nn[R, c, :], Pcb[R, c, :])
                S.tt("dve", Pn[R].rearrange("p a b -> p (a b)"), pbp[R, :], Pc[R].rearrange("p a b -> p (a b)"), ALU.add)
                S.copy("pool", Pnb[R], Pn[R])
                cur = 1 - cur
            Pf = Pmb[cur]
            pb = bank()
            for c in range(8):
                S.transpose(pb[:, c * 64:(c + 1) * 64], f2(BKh, c), id64)
            S.copy("act", BKhT.rearrange("p a b -> p (a b)"), pb)
            pb = bank()
            for c in range(8):
                S.transpose(pb[:, c * 64:(c + 1) * 64], xvb[R, c * 64:c * 64 + 128], id64)
            S.copy("dve", UV[64:128].rearrange("p a b -> p (a b)"), pb[64:128, :])
            S.copy("dve", SV[64:128, 0:8, :].rearrange("p a b -> p (a b)"), pb[64:128, :])
            ypb = ps[4]
            for c in range(8):
                pw = bank()
                S.mm(pw[R, 0:64], WL[:, c, :], SV[:, c, :])
                S.copy("act", Wsb[R], pw[R, 0:64])
                pu = bank()
                S.mm(pu[R, 0:64], Pf[R, c, :], Wsb[R])
                S.copy("dve", UV[R, c, :], pu[R, 0:64])
                S.mm(ypb[R, c * 64:(c + 1) * 64], SV[:, c, :], RA[:, c, :], start=True, stop=False)
                S.mm(ypb[R, c * 64:(c + 1) * 64], UV[R, c, :], ARB[R, c, :], start=False, stop=True)
                pn = bank()
                S.mm(pn[R, 0:64], BKhT[:, c, :], UV[:, c, :])
                S.stt("dve", SVf[R, c + 1, :], SVf[R, c, :], gC[R, c:c + 1], pn[R, 0:64], ALU.mult, ALU.add)
                S.copy("act", SV[R, c + 1, :], SVf[R, c + 1, :])
            S.copy("pool", SV[R, 0, :], SV[R, 8, :])
            S.copy("pool", SVf[R, 0, :], SVf[R, 8, :])
            S.copy("act", yT[R], ypb[R, :])
            pb = bank()
            S.mm(pb[R, :], ones64, yT[R])
            S.stt("dve", yT[R], pb[R, :], -1.0 / 64, yT[R], ALU.mult, ALU.add)
            S.act(t1[R], yT[R], AF.Square)
            pb = bank()
            S.mm(pb[R, :], ones64, t1[R])
            S.act(t1[R], pb[R, :], AF.Sqrt, bias=A_GN_EPS, scale=1.0 / 64)
            S.recip(t1[R], t1[R])
            S.tt("dve", yT[R], yT[R], t1[R], ALU.mult)
            S.ts("dve", yT[R], yT[R], V("gn_g", 64), ALU.mult, V("gn_b", 64), ALU.add)
            S.tt("dve", yT[R], yT[R], bon[R], ALU.add)
            S.tt("dve", ya_o[R], yT[R], g_t[R], ALU.mult)
            S.dma("sp", y_d[0:64, c0:c0 + TT], ya_o[R])

        def do_mla_prep():
            S.mute = "m" not in PHASES
            sc.reset()
            A = lambda n=TT, dt=F32: sc.get([n], dt)
            t1, t2 = A(), A()
            cq_sb = sc.get([3, TT])
            sqb = [A(TT, BF16) for i in range(2)]
            rsq = A()
            cqn = sc.get([3, TT], BF16)
            ckvn = sc.get([2, TT], BF16)
            posi = sc.get([TT], I32)
            cos2, sin2 = A(), A()
            qr = A()
            qsq = A(TT, BF16)
            ksq = A(TT, BF16)
            rdm = A()
            def latent_norm(off, nch, dst, gcol, nfeat):
                ssb = ps[4]
                for j in range(nch):
                    pb = proj(off + j * 128, 128, u_t)
                    S.copy("act", cq_sb[:, j, :], pb)
                    b = sqb[j % 2]
                    S.act(b, pb, AF.Square)
                    S.mm(ssb, cx.ones_b, b, start=(j == 0), stop=(j == nch - 1))
                S.act(rsq, ssb, AF.Sqrt, bias=float(nfeat * EPS))
                S.recip(rsq, rsq)
                for j in range(nch):
                    S.stt("dve", dst[:, j, :], cq_sb[:, j, :], dv[:, gcol + j:gcol + j + 1], rsq, ALU.mult, ALU.mult)
            latent_norm(CB["cq"], 3, cqn, 8, 384)
            pq = bank()
            for j in range(3):
                S.mm(pq[0:96, :], wuq[:, j, :], cqn[:, j, :], start=(j == 0), stop=(j == 2))
            pqs = bank()
            for j in range(3):
                S.mm(pqs[0:96, :], wuqs[:, j, :], cqn[:, j, :], start=(j == 0), stop=(j == 2))
            RR = slice(64, 96)
            S.dma("sp", posi[RR], pos_d[:, c0:c0 + TT])
            S.copy("dve", t1[RR], posi[RR])
            S.ts("dve", t1[RR], t1[RR], invf[RR], ALU.mult)
            def sincos(dst, shift):
                S.ts("dve", t2[RR], t1[RR], 1.0 / TWO_PI, ALU.mult, shift, ALU.add)
                S.copy("dve", posi[RR], t2[RR])
                S.copy("dve", rdm[RR], posi[RR])
                S.tt("dve", t2[RR], t2[RR], rdm[RR], ALU.subtract)
                S.ts("dve", rdm[RR], t2[RR], 0.0, ALU.is_lt)
                S.tt("dve", t2[RR], t2[RR], rdm[RR], ALU.add)
                S.act(dst[RR], t2[RR], AF.Sin, bias=mpi[RR, 0:1], scale=TWO_PI)
            sincos(sin2, 0.5)
            S.ts("dve", sin2[RR], sin2[RR], sgn[RR], ALU.mult)
            sincos(cos2, 0.75)
            S.act(QT[0:64], pq[0:64, :], AF.Copy, scale=SCALE)
            S.tt("dve", qr[RR], pq[RR, :], cos2[RR], ALU.mult)
            S.tt("dve", t1[RR], pqs[RR, :], sin2[RR], ALU.mult)
            S.tt("dve", qr[RR], qr[RR], t1[RR], ALU.add)
            S.act(QT[RR], qr[RR], AF.Copy, scale=SCALE)
            S.act(qsq[0:64], pq[0:64, :], AF.Square, scale=SCALE)
            S.act(qsq[RR], qr[RR], AF.Square, scale=SCALE)
            latent_norm(CB["ckv"], 2, ckvn, 11, 256)
            pkv = bank()
            for j in range(2):
                S.mm(pkv, wukv[:, j, :], ckvn[:, j, :], start=(j == 0), stop=(j == 1))
            S.copy("act", KT[0:64, c0:c0 + TT], pkv[0:64, :])
            pkr = proj(CB["kr"], 96, u_t)
            pkrs = proj(CB["krs"], 96, u_t)
            S.tt("dve", t1[RR], pkr[RR, :], cos2[RR], ALU.mult)
            S.tt("dve", t2[RR], pkrs[RR, :], sin2[RR], ALU.mult)
            S.tt("dve", KT[RR, c0:c0 + TT], t1[RR], t2[RR], ALU.add)
            S.act(ksq[0:96], KT[0:96, c0:c0 + TT], AF.Square)
            pb = bank()
            S.mm(pb[0:97, :], sel96[0:96, :], ksq[0:96])
            S.reduce("dve", kmax2[96:97, 1:2], pb[96:97, :], ALU.max)
            S.tt("dve", kmax2[96:97, 0:1], kmax2[96:97, 0:1], kmax2[96:97, 1:2], ALU.max)
            pb = bank()
            S.mm(pb[0:97, :], sel96[0:96, :], qsq[0:96])
            S.ts("dve", t1[96:97], pb[96:97, :], kmax2[96:97, 0:1], ALU.mult)
            S.act(t1[96:97], t1[96:97], AF.Sqrt)
            S.ts("dve", QT[96:97], t1[96:97], -1.0, ALU.mult)
            pb = bank()
            for blk in range(4):
                for j in range(2):
                    S.mm(pb[:, blk * 64:(blk + 1) * 64], ckvn[:, j, blk * 128:(blk + 1) * 128], wukv[:, j, 64:128],
                         start=(j == 0), stop=(j == 1))
            S.copy("act", Vaug[:, 4 * it:4 * it + 4, 0:64], pb[:, 0:256].rearrange("p (a b) -> p a b", a=4))
        def do_attn():
            ob_ps = ps[7]
            nkb = 4 * it + 4
            for kb in range(nkb):
                d = kb - 4 * it
                qlo = max(d, 0) * 128
                sp_ = ps[5 + kb % 2]
                pt = Pb[kb % 2]
                S.mm(sp_[:, qlo:TT], KT[0:97, kb * 128:(kb + 1) * 128], QT[0:97, qlo:TT])
                S.act(pt[:, qlo:TT], sp_[:, qlo:TT], AF.Exp)
                if d >= 0:
                    S.tt("pool", pt[:, qlo:qlo + 128], pt[:, qlo:qlo + 128], triA, ALU.mult)
                S.mm(ob_ps[0:65, qlo:TT], Vaug[:, kb, :], pt[:, qlo:TT], start=(kb == 0), stop=(kb == nkb - 1))
            S.recip(rd[0:1], ob_ps[64:65, :])
            S.copy("act", ob[R], ob_ps[R, :])
            pb = ps[5]
            S.mm(pb[R, :], cx.ones_f[0:1, 0:64], rd[0:1])
            S.tt("dve", yb_o[R], ob[R], pb[R, :], ALU.mult)
            S.dma("sp", y_d[64:128, c0:c0 + TT], yb_o[R])

        def do_hgrn():
            S.mute = "h" not in PHASES
            sc.reset()
            A = lambda n=TT, dt=F32: sc.get([n], dt)
            t1, t2 = A(), A()
            qh, kx, clh, qb, gh = A(), A(), A(), A(), A()
            qtl, ktl, khat = A(TT, BF16), A(TT, BF16), A(TT, BF16)
            Vh = sc.get([16, 128], BF16)
            KhT = sc.get([16, 128], BF16)
            KV = sc.get([16, 128])
            ATh = sc.get([16, 32], BF16)
            oh = A()
            sqb = [A(TT, BF16)]
            rsq = A()
            pb = proj(CB["hq"], 128, u_t)
            S.act(qh, pb, AF.Silu)
            pb = proj(CB["hf"], 128, u_t)
            S.act(t1, pb, AF.Sigmoid)
            S.ts("dve", t1, t1, dv[:, 14:15], ALU.mult, dv[:, 13:14], ALU.add)
            S.ts("dve", kx, t1, -1.0, ALU.mult, 1.0, ALU.add)
            S.ts("dve", t1, t1, 1e-6, ALU.max)
            S.act(t1, t1, AF.Ln)
            clh, t2 = emit_cumsum(S, t1, clh, t2, slice(0, 128), 16, 32)
            c16 = lambda t: t.rearrange("p (c t) -> p c t", c=16)
            S.act(t2, clh, AF.Exp)
            S.tt("dve", qb, qh, t2, ALU.mult)
            S.tt("dve", c16(t1), c16(clh), c16(clh)[:, :, 15:16].bcast([128, 16, 32]), ALU.subtract)
            S.act(t2, t1, AF.Exp)
            S.tt("dve", qtl, qh, t2, ALU.mult)
            S.act(t2, t1, AF.Exp, scale=-1.0)
            S.tt("dve", ktl, kx, t2, ALU.mult)
            S.tt("dve", c16(t1), c16(clh), c16(clh)[:, :, 31:32].bcast([128, 16, 32]), ALU.subtract)
            S.act(t2, t1, AF.Exp, scale=-1.0)
            S.tt("dve", khat, kx, t2, ALU.mult)
            S.act(gCh[:, 0:16], c16(clh)[:, :, 31:32].rearrange("p c o -> p (c o)"), AF.Exp)
            Q = slice(0, 32)
            for blk in range(4):
                pb = bank()
                for j in range(4):
                    c = blk * 4 + j
                    for k in range(NK):
                        S.mm(pb[Q, j * 128:(j + 1) * 128], u_t[:, k, c * 32:(c + 1) * 32],
                             wB[:, k, CB["hi"]:CB["hi"] + 128], start=(k == 0), stop=(k == NK - 1))
                S.copy("act", Vh[Q, blk * 4:blk * 4 + 4, :].rearrange("p a b -> p (a b)"), pb[Q, :])
            pb = proj(CB["hg"], 128, u_t)
            S.act(gh, pb, AF.Silu)
            for blk in range(4):
                pb = bank()
                for j in range(4):
                    c = blk * 4 + j
                    S.transpose(pb[Q, j * 128:(j + 1) * 128], khat[:, c * 32:(c + 1) * 32], cx.ident_b)
                S.copy("dve", KhT[Q, blk * 4:blk * 4 + 4, :].rearrange("p a b -> p (a b)"), pb[Q, :])
            for blk in range(4):
                pb = bank()
                for j in range(4):
                    c = blk * 4 + j
                    S.mm(pb[:, j * 128:(j + 1) * 128], KhT[Q, c, :], Vh[Q, c, :])
                S.copy("act" if blk % 2 == 0 else "dve", KV[:, 4 * blk:4 * blk + 4, :].rearrange("p a b -> p (a b)"), pb)
            pb = bank()
            for c in range(16):
                S.mm(pb[Q, c * 32:(c + 1) * 32], ktl[:, c * 32:(c + 1) * 32], qtl[:, c * 32:(c + 1) * 32])
            S.tt("dve", ATh[Q], pb[Q, :].rearrange("p (c t) -> p c t", c=16),
                 triH[Q, 0:1, :].bcast([32, 16, 32]), ALU.mult)
            for c in range(16):
                S.stt("dve", Sh[:, c + 1, :], Sh[:, c, :], gCh[:, c:c + 1], KV[:, c, :], ALU.mult, ALU.add)
            ohp = ps[4]
            for c in range(16):
                S.mm(ohp[:, c * 32:(c + 1) * 32], Sh[:, c, :], qb[:, c * 32:(c + 1) * 32], start=True, stop=False)
                S.mm(ohp[:, c * 32:(c + 1) * 32], Vh[Q, c, :], ATh[Q, c, :], start=False, stop=True)
            S.copy("pool", Sh[:, 0, :], Sh[:, 16, :])
            S.copy("act", oh, ohp)
            S.act(sqb[0], ohp, AF.Square)
            pb = bank()
            S.mm(pb, cx.ones_b, sqb[0])
            S.act(rsq, pb, AF.Sqrt, bias=float(128 * EPS))
            S.recip(rsq, rsq)
            S.stt("dve", oh, oh, dv[:, 15:16], rsq, ALU.mult, ALU.mult)
            S.tt("dve", yc_o, oh, gh, ALU.mult)
            S.dma("sp", y_d[128:192, c0:c0 + TT], yc_o[0:64])
        do_mla_prep()
        S.mute = False
        main_ops, main_lines = S.ops, S.lines
        S.ops, S.lines = [], []
        do_rwkv()
        do_hgrn()
        S.mute = False
        a_ops, a_lines = S.ops, S.lines
        S.ops, S.lines = [], []
        S.mute = "m" not in PHASES
        do_attn()
        S.mute = False
        b_ops, b_lines = S.ops, S.lines
        ia = ib = 0
        na, nb = len(a_ops), len(b_ops)
        while ia < na or ib < nb:
            if ib >= nb or (ia < na and ia * nb <= ib * na):
                main_ops.append(a_ops[ia]); main_lines.append(a_lines[ia]); ia += 1
            else:
                main_ops.append(b_ops[ib]); main_lines.append(b_lines[ib]); ib += 1
        S.ops, S.lines = main_ops, main_lines
    S.mute = False
    if cx.dbg is not None:
        for nm, t in (("QT", QT), ("cos2", cos2), ("sin2", sin2), ("rd", rd), ("ob", ob), ("qr", qr)):
            cx.dbg[nm] = t
        cx.dbg["KT"] = KT
        cx.dbg["Vaug"] = Vaug
    S.barrier()
    cx.scr_peak = sc.peak
    ar.release(m_all)


DEBUG_B = bool(int(os.environ.get("DEBUG_B", "0")))


def build_B(layer, SL):
    nc = bass.Bass("TRN2", target_bir_lowering=False)
    with ExitStack() as es:
        cx = make_ctx(nc, es, ARENA_WORDS)
        u_d = dram_in(nc, "u_full", [128, NK, SL], BF16)
        cd = dram_in(nc, "consts", [128, 384])
        wB_d = dram_in(nc, "wB", [128, NK, NCOLB])
        wsm_d = dram_in(nc, "wsm", [128, NSM])
        vec_d = dram_in(nc, "vecB", [128, NVB])
        pos_d = dram_in(nc, "pos", [32, SL], I32)
        cB_d = dram_in(nc, "cB", [128, NCB])
        y_d = dram_out(nc, "y_out", [192, SL], BF16)
        if layer == 0:
            vf_in, vf_out = None, dram_out(nc, "vf_out", [64, SL])
        else:
            vf_in, vf_out = dram_in(nc, "vf_in", [64, SL]), None
        load_consts(cx, cd)
        if DEBUG_B:
            cx.dbg = {}
        emit_mixer(cx, layer, SL, u_d, wB_d, wsm_d, vec_d, pos_d, cB_d, vf_in, vf_out, y_d)
        if DEBUG_B:
            for nm, t in cx.dbg.items():
                shp = [128] + list(t.ap.shape[1:])
                od = dram_out(nc, "dbg_" + nm, shp, t.ap.dtype)
                cx.S.dma("sp", od, t)
        cx.S.emit()
        print("B stats", cx.S.stats, "scratch peak", cx.scr_peak)
    return nc


def lay_rows(w, nch):
    w = np.asarray(w, np.float32)
    return np.ascontiguousarray(w.reshape(nch, 128, -1).transpose(1, 0, 2))


def prep_B_core(inp, l, c, SL):
    f32 = np.float32
    w_in = np.asarray(inp["w_in"][l], f32)
    hd, hf_ = c // 2, c % 2
    perm = np.concatenate([np.arange(hf_ * 64, hf_ * 64 + 64), np.arange((1 - hf_) * 64, (1 - hf_) * 64 + 64)])
    W = np.zeros((D, NCOLB), f32)

    def put(name, cols):
        W[:, CB[name]:CB[name] + cols.shape[1]] = cols
    put("r", w_in[:, c * 64:(c + 1) * 64])
    put("k", w_in[:, 512 + c * 64:512 + (c + 1) * 64])
    put("v", w_in[:, 1024 + c * 64:1024 + (c + 1) * 64])
    put("wl", w_in[:, 1536:1600])
    put("al", w_in[:, 1600:1664])
    put("gl", w_in[:, 1664:1792])
    if l > 0:
        put("vl", np.asarray(inp["rwkv_vres_down"][l - 1], f32))
    put("cq", w_in[:, OFF_CQ:OFF_CQ + 384])
    put("ckv", w_in[:, OFF_CKV:OFF_CKV + 256])
    kr = w_in[:, OFF_KR:OFF_KR + 32]
    W[:, CB["kr"] + 64:CB["kr"] + 96] = kr
    W[:, CB["krs"] + 64:CB["krs"] + 80] = kr[:, 16:32]
    W[:, CB["krs"] + 80:CB["krs"] + 96] = kr[:, 0:16]
    put("hq", w_in[:, OFF_HQ + hd * 128:OFF_HQ + (hd + 1) * 128])
    put("hf", w_in[:, OFF_HF + hd * 128:OFF_HF + (hd + 1) * 128])
    put("hi", w_in[:, OFF_HI + hd * 128:OFF_HI + (hd + 1) * 128][:, perm])
    put("hg", w_in[:, OFF_HG + hd * 128:OFF_HG + (hd + 1) * 128][:, perm])
    wB = lay_rows(W, NK)
    wsm = np.zeros((128, NSM), f32)
    hs = slice(c * 64, (c + 1) * 64)
    wsm[0:64, SM["w_up"]:SM["w_up"] + 64] = np.asarray(inp["rwkv_w_up"][l], f32)[:, hs]
    wsm[0:64, SM["a_up"]:SM["a_up"] + 64] = np.asarray(inp["rwkv_a_up"][l], f32)[:, hs]
    wsm[:, SM["g_up"]:SM["g_up"] + 64] = np.asarray(inp["rwkv_g_up"][l], f32)[:, hs]
    if l > 0:
        wsm[0:32, SM["vres_up"]:SM["vres_up"] + 64] = np.asarray(inp["rwkv_vres_up"][l - 1], f32)[:, hs]
    uq = np.asarray(inp["mla_w_uq"][l], f32)[:, c * 96:(c + 1) * 96]
    uqs = np.concatenate([uq[:, 0:64], uq[:, 80:96], uq[:, 64:80]], axis=1)
    wsm[:, SM["uq"]:SM["uq"] + 288] = lay_rows(uq, 3).reshape(128, 288)
    wsm[:, SM["uqs"]:SM["uqs"] + 288] = lay_rows(uqs, 3).reshape(128, 288)
    ukv = np.asarray(inp["mla_w_ukv"][l], f32)[:, c * 128:(c + 1) * 128]
    wsm[:, SM["ukv"]:SM["ukv"] + 256] = lay_rows(ukv, 2).reshape(128, 256)
    vec = np.zeros((128, NVB), f32)
    mu = np.asarray(inp["rwkv_mu"][l], f32)
    vec[0:64, VB["mu_r"]] = mu[c * 64:(c + 1) * 64]
    vec[0:64, VB["mu_k"]] = mu[512 + c * 64:512 + (c + 1) * 64]
    vec[0:64, VB["mu_v"]] = mu[1024 + c * 64:1024 + (c + 1) * 64]
    vec[0:64, VB["mu_wl"]] = mu[1536:1600]
    vec[0:64, VB["mu_al"]] = mu[1600:1664]
    vec[:, VB["mu_gl"]] = mu[1664:1792]
    if l > 0:
        vec[0:32, VB["mu_vl"]] = np.asarray(inp["rwkv_vres_mu"][l - 1], f32)
        vec[0:64, VB["v0"]] = np.asarray(inp["rwkv_v0"][l - 1], f32)[hs]
    for nm, key in (("w0", "rwkv_w0"), ("a0", "rwkv_a0"), ("k_k", "rwkv_k_k"), ("k_a", "rwkv_k_a"),
                    ("gn_g", "rwkv_gn_g"), ("gn_b", "rwkv_gn_b")):
        vec[0:64, VB[nm]] = np.asarray(inp[key][l], f32)[hs]
    vec[0:64, VB["r_k"]] = np.asarray(inp["rwkv_r_k"][l], f32)[c]
    vec[:, VB["qg"]:VB["qg"] + 3] = lay_vec(inp["mla_q_norm_g"][l])
    vec[:, VB["kvg"]:VB["kvg"] + 2] = lay_vec(inp["mla_kv_norm_g"][l])
    vec[:, VB["lb"]:VB["lb"] + 4] = np.asarray(inp["hgrn_lower_bounds"], f32)[:, hd * 128:(hd + 1) * 128].T
    vec[:, VB["hng"]] = np.asarray(inp["hgrn_norm_g"][l], f32)[perm]
    pos = np.ascontiguousarray(np.broadcast_to(np.asarray(inp["positions"]).reshape(1, -1)[:, :SL], (32, SL))).astype(np.int32)
    return dict(wB=wB, wsm=wsm, vecB=vec, pos=pos)


def emit_merge(cx, hT, uT, NT, y_d, wgate_d, wouts_d, wo_d, g_post):
    S, ar, ps = cx.S, cx.ar, cx.ps
    m0 = ar.mark()
    wouts = ar.alloc("wouts", [12, D], BF16)
    for j in range(12):
        S.dma("pool", wouts[:, j, :], wouts_d[:, j, :])
    wo = ar.alloc("wo", [NK, D], BF16)
    for k in range(NK):
        S.dma("pool", wo[:, k, :], wo_d[:, k, :])
    wgb = [ar.alloc("wgateb%d" % i, [NK, 384], BF16) for i in range(2)]
    yt = ar.alloc("yt", [12, TT], BF16)
    merged = ar.alloc("merged", [NK, TT], BF16)
    sig = [ar.alloc("sig%d" % i, [TT]) for i in range(3)]
    mt = [ar.alloc("mt%d" % i, [TT]) for i in range(2)]
    z = ar.alloc("z", [NK, TT])
    sq = [ar.alloc("msq%d" % i, [TT], BF16) for i in range(2)]
    rs = ar.alloc("mrs", [TT])
    tmp = [ar.alloc("mtmp%d" % i, [TT]) for i in range(2)]
    ntile = NT // TT
    nld = [0]

    def ld_gate(d):
        S.dma("pool", wgb[nld[0] % 2], wgate_d[d].rearrange("p (k m) -> p k m", k=NK))
        nld[0] += 1

    ld_gate(0)
    nsq = 0
    for t in range(ntile):
        c0 = t * TT
        for j in range(12):
            S.dma("sp", yt[:, j, :], y_d[:, j, c0:c0 + TT])
        for d in range(NK):
            wgt = wgb[(t * NK + d) % 2]
            if t * NK + d + 1 < ntile * NK:
                ld_gate((d + 1) % NK)
            for j in range(3):
                pj = ps[j]
                for k in range(4):
                    S.mm(pj, wouts[:, 4 * j + k, d * 128:(d + 1) * 128], yt[:, 4 * j + k, :], start=(k == 0), stop=(k == 3))
                gj = ps[3 + j]
                for k in range(NK):
                    S.mm(gj, wgt[:, k, j * 128:(j + 1) * 128], uT[:, k, c0:c0 + TT], start=(k == 0), stop=(k == NK - 1))
                S.act(sig[j], gj, AF.Sigmoid)
            S.tt("dve", mt[0], sig[0], ps[0], ALU.mult)
            S.tt("dve", mt[1], sig[1], ps[1], ALU.mult)
            S.tt("pool", mt[0], mt[0], mt[1], ALU.add)
            S.tt("dve", mt[1], sig[2], ps[2], ALU.mult)
            S.tt("pool", merged[:, d, :], mt[0], mt[1], ALU.add)
        for d in range(NK):
            zp = ps[6]
            for k in range(NK):
                S.mm(zp, wo[:, k, d * 128:(d + 1) * 128], merged[:, k, :], start=(k == 0), stop=(k == NK - 1))
            S.act(z[:, d, :], zp, AF.Copy)
            b = sq[nsq % 2]
            nsq += 1
            S.act(b, zp, AF.Square)
            S.mm(ps[7], cx.ones_b, b, start=(d == 0), stop=(d == NK - 1))
        emit_rstd(cx, ps[7], rs, D)
        for d in range(NK):
            tq = tmp[d % 2]
            S.stt("dve", tq, z[:, d, :], g_post[:, d:d + 1], rs, ALU.mult, ALU.mult)
            S.tt("pool", hT[:, d, c0:c0 + TT], hT[:, d, c0:c0 + TT], tq, ALU.add)
    S.barrier()
    ar.release(m0)


def lay_gate(w_in_l):
    g = np.asarray(w_in_l, np.float32)[:, OFF_GATE:OFF_GATE + 3 * D]
    g = g.reshape(NK, 128, 3, NK, 128)
    return np.ascontiguousarray(g.transpose(3, 1, 0, 2, 4).reshape(NK, 128, NK * 384))


def build_T(NT, merge, nxt):
    nc = bass.Bass("TRN2", target_bir_lowering=False)
    with ExitStack() as es:
        cx = make_ctx(nc, es, ARENA_WORDS)
        S, ar = cx.S, cx.ar
        nv = 24 * (int(merge) + int(nxt))
        h_in = dram_in(nc, "h_in", [128, NK, NT])
        cd = dram_in(nc, "consts", [128, 384])
        vd = dram_in(nc, "vec", [128, nv])
        load_consts(cx, cd)
        vec = load_vecs(cx, vd, nv)
        g = ar.alloc("gains", [nv])
        hT = ar.alloc("hT", [NK, NT])
        for k in range(NK):
            S.dma("sp", hT[:, k, :], h_in[:, k, :])
        o = 0
        if merge:
            u_in = dram_in(nc, "u_in", [128, NK, NT], BF16)
            y_in = dram_in(nc, "y_in", [128, 12, NT], BF16)
            wgate = dram_in(nc, "wgate", [NK, 128, NK * 384])
            wouts = dram_in(nc, "wouts", [128, 12, D])
            wo = dram_in(nc, "wo", [128, NK, D])
            wg2 = dram_in(nc, "wg2", [NF, 128, NK * 128])
            wu2 = dram_in(nc, "wu2", [NF, 128, NK * 128])
            wd2 = dram_in(nc, "wd2", [NK, 128, NF * 128])
            mk = ar.mark()
            uTm = ar.alloc("uTm", [NK, NT], BF16)
            for k in range(NK):
                S.dma("sp", uTm[:, k, :], u_in[:, k, :])
            S.ts("dve", g[:, 0:8], vec[:, 0:8], 32.0, ALU.mult)
            S.ts("dve", g[:, 8:16], vec[:, 8:16], 32.0, ALU.mult)
            S.ts("dve", g[:, 16:24], vec[:, 16:24], 16.0, ALU.mult)
            emit_merge(cx, hT, uTm, NT, y_in, wgate, wouts, wo, g[:, 0:8])
            ar.release(mk)
            emit_ffn(cx, hT, NT, wg2, wu2, wd2, g[:, 8:16], g[:, 16:24], G=2)
            o = 24
        h_out = dram_out(nc, "h_out", [128, NK, NT])
        if nxt:
            wg1 = dram_in(nc, "wg1", [NF, 128, NK * 128])
            wu1 = dram_in(nc, "wu1", [NF, 128, NK * 128])
            wd1 = dram_in(nc, "wd1", [NK, 128, NF * 128])
            u_out = dram_out(nc, "u_out", [128, NK, NT], BF16)
            S.ts("dve", g[:, o:o + 8], vec[:, o:o + 8], 32.0, ALU.mult)
            S.ts("dve", g[:, o + 8:o + 16], vec[:, o + 8:o + 16], 16.0, ALU.mult)
            S.ts("dve", g[:, o + 16:o + 24], vec[:, o + 16:o + 24], 32.0, ALU.mult)
            emit_ffn(cx, hT, NT, wg1, wu1, wd1, g[:, o:o + 8], g[:, o + 8:o + 16], G=2)
            uT = ar.alloc("uT", [NK, NT], BF16)
            emit_norm_to(cx, hT, NT, g[:, o + 16:o + 24], uT)
            for k in range(NK):
                S.dma("sp", u_out[:, k, :], uT[:, k, :])
        for k in range(NK):
            S.dma("sp", h_out[:, k, :], hT[:, k, :])
        S.emit()
    return nc


_PROG = {}


def _prog(key, fn):
    if key not in _PROG:
        _PROG[key] = fn()
    return _PROG[key]


def _run(nc, ins):
    res = run_bass_kernel_spmd(nc, ins, core_ids=list(range(NCORES)))
    return res.results


def kernel_impl(inp, SL, depth):
    NT = SL // NCORES
    consts = make_consts()
    cB = make_constsB()
    x = np.asarray(inp["x"], np.float32).reshape(SL, D)
    hs = [lay_act_T(x[c * NT:(c + 1) * NT]) for c in range(NCORES)]
    us = None
    vfirst = None

    def ffn_w(prefix, l, tag):
        return {"wg" + tag: lay_w_gu(inp[prefix + "_w_gate"][l]), "wu" + tag: lay_w_gu(inp[prefix + "_w_up"][l]),
                "wd" + tag: lay_w_d(inp[prefix + "_w_down"][l])}

    def vecs(names_l):
        return np.concatenate([lay_vec(inp[n][l]) for n, l in names_l], axis=1)

    nc = _prog(("T", NT, False, True), lambda: build_T(NT, False, True))
    com = dict(consts=consts, vec=vecs([("ffn1_pre_g", 0), ("ffn1_post_g", 0), ("mix_pre_g", 0)]))
    com.update(ffn_w("ffn1", 0, "1"))
    res = _run(nc, [dict(com, h_in=hs[c]) for c in range(NCORES)])
    hs = [np.asarray(r["h_out"]) for r in res]
    us = [np.asarray(r["u_out"]) for r in res]
    for l in range(depth):
        u_full = np.ascontiguousarray(np.concatenate(us, axis=2))
        ncb = _prog(("B", l if l < 2 else l, SL), lambda: build_B(l, SL))
        ins = []
        for c in range(NCORES):
            d = prep_B_core(inp, l, c, SL)
            d.update(u_full=u_full, consts=consts, cB=cB)
            if l > 0:
                d["vf_in"] = vfirst[c]
            ins.append(d)
        res = _run(ncb, ins)
        if l == 0:
            vfirst = [np.asarray(r["vf_out"]) for r in res]
        ys = [np.asarray(r["y_out"]) for r in res]
        y_ins = []
        for j in range(NCORES):
            yi = np.zeros((128, 12, NT), dtype=ys[0].dtype)
            for br in range(3):
                for q in range(4):
                    for half in range(2):
                        yi[half * 64:(half + 1) * 64, br * 4 + q, :] = ys[2 * q + half][br * 64:(br + 1) * 64, j * NT:(j + 1) * NT]
            y_ins.append(yi)
        last = (l == depth - 1)
        nct = _prog(("T", NT, True, not last), lambda: build_T(NT, True, not last))
        names = [("mix_post_g", l), ("ffn2_pre_g", l), ("ffn2_post_g", l)]
        if not last:
            names += [("ffn1_pre_g", l + 1), ("ffn1_post_g", l + 1), ("mix_pre_g", l + 1)]
        com = dict(consts=consts, vec=vecs(names), wgate=lay_gate(inp["w_in"][l]),
                   wouts=np.concatenate([lay_rows(inp["rwkv_out"][l], 4), lay_rows(inp["mla_out"][l], 4),
                                         lay_rows(inp["hgrn_out"][l], 4)], axis=1),
                   wo=lay_rows(inp["w_o"][l], NK))
        com.update(ffn_w("ffn2", l, "2"))
        if not last:
            com.update(ffn_w("ffn1", l + 1, "1"))
        res = _run(nct, [dict(com, h_in=hs[c], u_in=us[c], y_in=y_ins[c]) for c in range(NCORES)])
        hs = [np.asarray(r["h_out"]) for r in res]
        if not last:
            us = [np.asarray(r["u_out"]) for r in res]
    out = np.concatenate([unlay_act_T(h) for h in hs], axis=0)
    return out.reshape(1, SL, D).astype(np.float32)


def kernel(**inputs):
    return kernel_impl(inputs, 16384, 4)
```

```python
import numpy as np
from contextlib import ExitStack
import ml_dtypes
import concourse.bass as bass
import concourse.mybir as mybir
from concourse.bass_utils import run_bass_kernel_spmd

F32 = mybir.dt.float32
BF16 = mybir.dt.bfloat16
I32 = mybir.dt.int32
AF = mybir.ActivationFunctionType
ALU = mybir.AluOpType
AX = mybir.AxisListType

NCORES = 8
D = 1024
DFF = 2816
NF = DFF // 128
NK = D // 128
EPS = 1e-6
A_COLS = 1792
N_IN = 7584
OFF_CQ, OFF_CKV, OFF_KR = 1792, 2176, 2432
OFF_HQ, OFF_HF, OFF_HI, OFF_HG, OFF_GATE = 2464, 2976, 3488, 4000, 4512
TT = 512


class T:
    __slots__ = ("ap", "keys")

    def __init__(self, ap, keys):
        self.ap = ap
        self.keys = (keys,) if isinstance(keys, str) else tuple(keys)

    def __getitem__(self, idx):
        return T(self.ap[idx], self.keys)

    def k(self, *keys):
        return T(self.ap, keys)

    def rearrange(self, pat, **kw):
        return T(self.ap.rearrange(pat, **kw), self.keys)

    def bcast(self, shape):
        return T(self.ap.to_broadcast(list(shape)), self.keys)


def _keys(*xs):
    ks = []
    for x in xs:
        if isinstance(x, T):
            ks.extend(x.keys)
    return ks


def _ap(x):
    return x.ap if isinstance(x, T) else x


import os as _os
import sys
MAXOPS = int(_os.environ.get("CUTOPS", "100000000"))
EPOCH_KEY = "__epoch__"
SEM_EPOCH = 30000
NDSEM = 6


class Sched:
    COMPUTE = ("pe", "act", "dve", "pool")

    def __init__(self, nc):
        self.nc = nc
        self.ops = []
        self.lines = []

    mute = False

    def add(self, eng, fn, reads, writes, dma=False):
        if self.mute or len(self.ops) >= MAXOPS:
            return
        self.lines.append(sys._getframe(2).f_lineno)
        self.ops.append((eng, fn, tuple(reads) + (EPOCH_KEY,), tuple(writes), dma))

    def barrier(self):
        o = self.bar_tile.ap
        self.lines.append(0)
        self.ops.append(("dve", lambda e: e.memset(o, 0.0), (), (EPOCH_KEY,) + self.bar_tile.keys, False))

    def mm(self, out, lhsT, rhs, start=True, stop=True, **kw):
        o, l, r = out.ap, lhsT.ap, rhs.ap
        self.add("pe", lambda e: e.matmul(o, lhsT=l, rhs=r, start=start, stop=stop, **kw),
                 _keys(lhsT, rhs), _keys(out))

    def transpose(self, out, in_, ident):
        self.mm(out, in_, ident)

    def act(self, out, in_, func, bias=None, scale=None, accum=None):
        o, i = out.ap, in_.ap
        kw = {}
        if bias is not None:
            kw["bias"] = _ap(bias)
        if scale is not None:
            kw["scale"] = _ap(scale)
        if accum is not None:
            kw["accum_out"] = _ap(accum)
        self.add("act", lambda e: e.activation(out=o, in_=i, func=func, **kw),
                 _keys(in_, bias, scale), _keys(out, accum))

    def tt(self, eng, out, a, b, op):
        o, x, y = out.ap, a.ap, b.ap
        self.add(eng, lambda e: e.tensor_tensor(out=o, in0=x, in1=y, op=op), _keys(a, b), _keys(out))

    def ts(self, eng, out, a, s1, op0, s2=None, op1=None):
        o, x = out.ap, a.ap
        s1a, s2a = _ap(s1), _ap(s2)
        if op1 is None:
            self.add(eng, lambda e: e.tensor_scalar(out=o, in0=x, scalar1=s1a, scalar2=None, op0=op0),
                     _keys(a, s1), _keys(out))
        else:
            self.add(eng, lambda e: e.tensor_scalar(out=o, in0=x, scalar1=s1a, scalar2=s2a, op0=op0, op1=op1),
                     _keys(a, s1, s2), _keys(out))

    def stt(self, eng, out, a, scalar, b, op0, op1):
        o, x, y, s = out.ap, a.ap, b.ap, _ap(scalar)
        eng = "dve"
        self.add(eng, lambda e: e.scalar_tensor_tensor(out=o, in0=x, scalar=s, in1=y, op0=op0, op1=op1),
                 _keys(a, scalar, b), _keys(out))

    def copy(self, eng, out, in_):
        o, i = out.ap, in_.ap
        if eng == "act":
            self.add("act", lambda e: e.activation(out=o, in_=i, func=AF.Copy), _keys(in_), _keys(out))
        else:
            self.add(eng, lambda e: e.tensor_copy(out=o, in_=i), _keys(in_), _keys(out))

    def memset(self, eng, out, val):
        o = out.ap
        self.add(eng, lambda e: e.memset(o, val), (), _keys(out))

    def scan(self, out, d0, d1, init, op0, op1):
        o, a, b = out.ap, d0.ap, d1.ap
        self.add("dve", lambda e: e.tensor_tensor_scan(out=o, data0=a, data1=b, initial=init, op0=op0, op1=op1),
                 _keys(d0, d1), _keys(out))

    def reduce(self, eng, out, in_, op, axis=AX.X):
        o, i = out.ap, in_.ap
        self.add(eng, lambda e: e.tensor_reduce(out=o, in_=i, axis=axis, op=op), _keys(in_), _keys(out))

    def recip(self, out, in_):
        o, i = out.ap, in_.ap
        self.add("dve", lambda e: e.reciprocal(out=o, in_=i), _keys(in_), _keys(out))

    def dma(self, q, out, in_):
        o, i = out.ap, in_.ap
        self.add(q, lambda e: e.dma_start(out=o, in_=i), _keys(in_), _keys(out), dma=True)

    def emit(self):
        nc = self.nc
        ops = self.ops
        n = len(ops)
        last_w = {}
        rd_eng = {}
        rd_dma = {}
        deps = [None] * n
        signal = [False] * n
        for i, (eng, fn, rd, wr, dma) in enumerate(ops):
            cand = []
            for k in rd:
                j = last_w.get(k)
                if j is not None:
                    cand.append((j, 0))
            for k in wr:
                j = last_w.get(k)
                if j is not None:
                    cand.append((j, 1))
                for j in rd_eng.get(k, {}).values():
                    cand.append((j, 2))
                for j in rd_dma.get(k, ()):
                    cand.append((j, 2))
            best = {}
            dl = set()
            for j, kind in cand:
                if j == i:
                    continue
                ej, _, _, _, dj = ops[j]
                if dj:
                    dl.add(j)
                    continue
                if (not dma) and ej == eng:
                    if eng == "pe" or kind != 0:
                        continue
                if best.get(ej, -1) < j:
                    best[ej] = j
            dd = set(best.values()) | dl
            deps[i] = dd
            for j in dd:
                signal[j] = True
            for k in rd:
                if dma:
                    rd_dma.setdefault(k, []).append(i)
                else:
                    rd_eng.setdefault(k, {})[eng] = i
            for k in wr:
                last_w[k] = i
                rd_eng[k] = {}
                rd_dma[k] = []
        cnt = {e: 0 for e in self.COMPUTE}
        sigval = [None] * n
        dcnt = {}
        for i, (eng, fn, rd, wr, dma) in enumerate(ops):
            if dma:
                q = dcnt.get(eng, 0)
                dcnt[eng] = q + 1
                sigval[i] = ("d", eng, q % NDSEM, 16 * (q // NDSEM + 1), q)
            elif signal[i]:
                cnt[eng] += 1
                sigval[i] = ("c", eng, cnt[eng])
        self.stats = dict(n=n, cnt=dict(cnt), dcnt=dict(dcnt))
        by_eng = {}
        for i, op in enumerate(ops):
            by_eng.setdefault(op[0], []).append(i)
        with ExitStack() as es:
            csems = {}
            for e in self.COMPUTE:
                ne = max(1, (cnt[e] + SEM_EPOCH - 1) // SEM_EPOCH)
                csems[e] = [es.enter_context(nc.semaphore("c_%s_%d" % (e, t))) for t in range(ne)]
            dsems = {}
            for e in dcnt:
                dsems[e] = [es.enter_context(nc.semaphore("d_%s_%d" % (e, t))) for t in range(NDSEM)]
            block = es.enter_context(nc.Block())

            def make(engname):
                mine = by_eng.get(engname, [])

                def body(e):
                    cw = {x: 0 for x in self.COMPUTE}
                    dw = {}
                    for i in mine:
                        eng, fn, rd, wr, dma = ops[i]
                        if dma:
                            _, _, slot, val, q = sigval[i]
                            if q >= NDSEM:
                                key = (eng, slot)
                                if dw.get(key, 0) < val - 16:
                                    e.wait_ge(dsems[eng][slot], val - 16)
                                    dw[key] = val - 16
                        for j in sorted(deps[i]):
                            sv = sigval[j]
                            if sv[0] == "c":
                                _, ej, c = sv
                                if cw[ej] >= c:
                                    continue
                                cw[ej] = c
                                e.wait_ge(csems[ej][(c - 1) // SEM_EPOCH], (c - 1) % SEM_EPOCH + 1)
                            else:
                                _, ej, slot, val, q = sv
                                key = (ej, slot)
                                if dw.get(key, 0) >= val:
                                    continue
                                dw[key] = val
                                e.wait_ge(dsems[ej][slot], val)
                        inst = fn(e)
                        sv = sigval[i]
                        if sv is not None:
                            if sv[0] == "c":
                                c = sv[2]
                                inst.then_inc(csems[eng][(c - 1) // SEM_EPOCH], 1)
                            else:
                                inst.then_inc(dsems[eng][sv[2]], 16)
                    if engname in dcnt:
                        tot = dcnt[engname]
                        for slot in range(NDSEM):
                            uses = (tot - slot + NDSEM - 1) // NDSEM if tot > slot else 0
                            if uses > 0 and dw.get((engname, slot), 0) < 16 * uses:
                                e.wait_ge(dsems[engname][slot], 16 * uses)
                return body

            block.tensor(make("pe"))
            block.scalar(make("act"))
            block.vector(make("dve"))
            block.gpsimd(make("pool"))
            block.sync(make("sp"))


class Arena:
    def __init__(self, nc, es, words):
        self.t = es.enter_context(nc.sbuf_tensor("arena", [128, words], F32))
        self.words = words
        self.off = 0
        self.uid = 0

    def alloc(self, name, free_shape, dtype=F32, nslots=None):
        nel = int(np.prod(free_shape))
        w = (nel + 1) // 2 if dtype == BF16 else nel
        w = (w + 7) // 8 * 8
        assert self.off + w <= self.words, ("SBUF arena overflow", name, self.off, w, self.words)
        ap = self.t[:, self.off:self.off + w]
        if dtype != F32:
            ap = ap.bitcast(dtype)
        ap = ap[:, 0:nel]
        if len(free_shape) == 2:
            ap = ap.rearrange("p (a b) -> p a b", a=free_shape[0])
        elif len(free_shape) == 3:
            ap = ap.rearrange("p (a b c) -> p a b c", a=free_shape[0], b=free_shape[1])
        self.off += w
        self.uid += 1
        return T(ap, "%s#%d" % (name, self.uid))

    def mark(self):
        return self.off

    def release(self, m):
        self.off = m


class Scratch:
    def __init__(self, ar, ngran):
        self.base = ar.off
        self.t = ar.t
        self.ngran = ngran
        ar.off += ngran * 512
        assert ar.off <= ar.words, ("scratch overflow", ar.off, ar.words)
        self.pos = 0
        self.peak = 0

    def reset(self):
        self.pos = 0

    def get(self, free_shape, dtype=F32):
        nel = int(np.prod(free_shape))
        w = (nel + 1) // 2 if dtype == BF16 else nel
        g = (w + 511) // 512
        assert self.pos + g <= self.ngran, ("scratch granules exhausted", self.pos, g, self.ngran)
        o = self.base + self.pos * 512
        ap = self.t[:, o:o + g * 512]
        if dtype != F32:
            ap = ap.bitcast(dtype)
        ap = ap[:, 0:nel]
        if len(free_shape) == 2:
            ap = ap.rearrange("p (a b) -> p a b", a=free_shape[0])
        elif len(free_shape) == 3:
            ap = ap.rearrange("p (a b c) -> p a b c", a=free_shape[0], b=free_shape[1])
        keys = tuple("scr%d" % (self.pos + i) for i in range(g))
        self.pos += g
        self.peak = max(self.peak, self.pos)
        return T(ap, keys)


class Ctx:
    pass


def make_ctx(nc, es, arena_words):
    cx = Ctx()
    cx.nc = nc
    cx.S = Sched(nc)
    cx.ar = Arena(nc, es, arena_words)
    cx.ps = [T(es.enter_context(nc.psum_tensor("psb%d" % i, [128, 512], F32))[:], "psb%d" % i) for i in range(8)]
    cx.S.bar_tile = cx.ar.alloc("bar", [8])
    cx.dbg = None
    return cx


def dram_in(nc, name, shape, dt=F32):
    return T(nc.dram_tensor(name, list(shape), dt, kind="ExternalInput").ap(), "dram:" + name)


def dram_out(nc, name, shape, dt=F32):
    return T(nc.dram_tensor(name, list(shape), dt, kind="ExternalOutput").ap(), "dram:" + name)


def load_consts(cx, cdram):
    S, ar = cx.S, cx.ar
    cx.ident = ar.alloc("ident", [128])
    cx.ones_f = ar.alloc("ones_f", [128])
    cx.ones_b = ar.alloc("ones_b", [128], BF16)
    S.dma("sp", cx.ident, cdram[:, 0:128])
    S.dma("sp", cx.ones_f, cdram[:, 128:256])
    S.copy("dve", cx.ones_b, cx.ones_f)
    cx.ident_b = ar.alloc("ident_b", [128], BF16)
    S.copy("dve", cx.ident_b, cx.ident)


def emit_rstd(cx, ss_ps, rs, n_feat, rows=128):
    cx.S.act(rs[0:rows], ss_ps[0:rows], AF.Sqrt, bias=float(n_feat * EPS))
    cx.S.recip(rs[0:rows], rs[0:rows])


def emit_ffn(cx, hT, NT, wg_d, wu_d, wd_d, g_pre, g_post, G=1):
    S, ar, ps = cx.S, cx.ar, cx.ps
    if NT % (G * TT) != 0:
        G = 1
    GW = G * TT
    m0 = ar.mark()
    xn = ar.alloc("xn", [NK, GW], BF16)
    hid = ar.alloc("hid", [NF, GW], BF16)
    yb = ar.alloc("ffy", [NK, GW])
    sq = [ar.alloc("sq%d" % i, [TT], BF16) for i in range(2)]
    rs = [ar.alloc("rs%d" % i, [TT]) for i in range(G)]
    sg = [ar.alloc("sg%d" % i, [TT]) for i in range(2)]
    tmp = [ar.alloc("ftmp%d" % i, [TT]) for i in range(2)]
    NWB = 3
    wgb = [ar.alloc("wgb%d" % i, [NK, 128], BF16) for i in range(NWB)]
    wub = [ar.alloc("wub%d" % i, [NK, 128], BF16) for i in range(NWB)]
    wdb = [ar.alloc("wdb%d" % i, [NF, 128], BF16) for i in range(2)]
    ss_ps = [ps[6], ps[7]]
    nsq = 0
    for g in range(NT // GW):
        for s in range(G):
            c0 = g * GW + s * TT
            for k in range(NK):
                b = sq[nsq % 2]
                nsq += 1
                S.act(b, hT[:, k, c0:c0 + TT], AF.Square)
                S.mm(ss_ps[s], cx.ones_b, b, start=(k == 0), stop=(k == NK - 1))
            emit_rstd(cx, ss_ps[s], rs[s], D)
            for k in range(NK):
                S.stt("dve" if k % 2 == 0 else "pool", xn[:, k, s * TT:(s + 1) * TT], hT[:, k, c0:c0 + TT],
                      g_pre[:, k:k + 1], rs[s], ALU.mult, ALU.mult)
        it = 0

        def ld_gu(f):
            S.dma("pool", wgb[f % NWB], wg_d[f].rearrange("p (k m) -> p k m", k=NK))
            S.dma("pool", wub[f % NWB], wu_d[f].rearrange("p (k m) -> p k m", k=NK))

        def ld_d(d):
            S.dma("pool", wdb[d % 2], wd_d[d].rearrange("p (f m) -> p f m", f=NF))

        for f in range(NWB - 1):
            ld_gu(f)
        for f in range(NF):
            wgt, wut = wgb[f % NWB], wub[f % NWB]
            if f + NWB - 1 < NF:
                ld_gu(f + NWB - 1)
            if f == NF - 2:
                ld_d(0)
            for s in range(G):
                gp, up = ps[(it % 2) * 2], ps[(it % 2) * 2 + 1]
                sgt = sg[it % 2]
                it += 1
                for k in range(NK):
                    S.mm(gp, wgt[:, k, :], xn[:, k, s * TT:(s + 1) * TT], start=(k == 0), stop=(k == NK - 1))
                for k in range(NK):
                    S.mm(up, wut[:, k, :], xn[:, k, s * TT:(s + 1) * TT], start=(k == 0), stop=(k == NK - 1))
                S.act(sgt, gp, AF.Silu)
                S.tt("dve", hid[:, f, s * TT:(s + 1) * TT], sgt, up, ALU.mult)
        it = 0
        for d in range(NK):
            wdt = wdb[d % 2]
            if d + 1 < NK:
                ld_d(d + 1)
            for s in range(G):
                yp = ps[4 + it % 2]
                it += 1
                for f in range(NF):
                    S.mm(yp, wdt[:, f, :], hid[:, f, s * TT:(s + 1) * TT], start=(f == 0), stop=(f == NF - 1))
                S.act(yb[:, d, s * TT:(s + 1) * TT], yp, AF.Copy)
                b = sq[nsq % 2]
                nsq += 1
                S.act(b, yp, AF.Square)
                S.mm(ss_ps[s], cx.ones_b, b, start=(d == 0), stop=(d == NK - 1))
        for s in range(G):
            c0 = g * GW + s * TT
            emit_rstd(cx, ss_ps[s], rs[s], D)
            for d in range(NK):
                t = tmp[d % 2]
                S.stt("dve", t, yb[:, d, s * TT:(s + 1) * TT], g_post[:, d:d + 1], rs[s], ALU.mult, ALU.mult)
                S.tt("pool", hT[:, d, c0:c0 + TT], hT[:, d, c0:c0 + TT], t, ALU.add)
    S.barrier()
    ar.release(m0)


def emit_norm_to(cx, hT, NT, g32, out_bf, rows_feat=D):
    S, ar, ps = cx.S, cx.ar, cx.ps
    m0 = ar.mark()
    sq = [ar.alloc("nsq%d" % i, [TT], BF16) for i in range(2)]
    rs = ar.alloc("nrs", [TT])
    n = 0
    for t in range(NT // TT):
        c0 = t * TT
        for k in range(NK):
            b = sq[n % 2]
            n += 1
            S.act(b, hT[:, k, c0:c0 + TT], AF.Square)
            S.mm(ps[6], cx.ones_b, b, start=(k == 0), stop=(k == NK - 1))
        emit_rstd(cx, ps[6], rs, D)
        for k in range(NK):
            S.stt("dve" if k % 2 == 0 else "pool", out_bf[:, k, c0:c0 + TT], hT[:, k, c0:c0 + TT],
                  g32[:, k:k + 1], rs, ALU.mult, ALU.mult)
    S.barrier()
    ar.release(m0)


def lay_vec(v):
    v = np.asarray(v, np.float32)
    return np.ascontiguousarray(v.reshape(-1, 128).T)


def lay_w_gu(w):
    w = np.asarray(w, np.float32)
    return np.ascontiguousarray(w.reshape(NK, 128, NF, 128).transpose(2, 1, 0, 3).reshape(NF, 128, NK * 128))


def lay_w_d(w):
    w = np.asarray(w, np.float32)
    return np.ascontiguousarray(w.reshape(NF, 128, NK, 128).transpose(2, 1, 0, 3).reshape(NK, 128, NF * 128))


def lay_act_T(x):
    x = np.asarray(x)
    nt = x.shape[0]
    return np.ascontiguousarray(x.reshape(nt, -1, 128).transpose(2, 1, 0))


def unlay_act_T(xT):
    p, n, nt = xT.shape
    return np.ascontiguousarray(xT.transpose(2, 1, 0).reshape(nt, n * 128))


def make_consts():
    c = np.zeros((128, 384), np.float32)
    c[:, 0:128] = np.eye(128, dtype=np.float32)
    c[:, 128:256] = 1.0
    return c


ARENA_WORDS = 53000


def load_vecs(cx, vd, ncol):
    v = cx.ar.alloc("vecs", [ncol])
    cx.S.dma("sp", v, vd)
    return v


def build_TA(NT):
    nc = bass.Bass("TRN2", target_bir_lowering=False)
    with ExitStack() as es:
        cx = make_ctx(nc, es, ARENA_WORDS)
        S, ar = cx.S, cx.ar
        h_in = dram_in(nc, "h_in", [128, NK, NT])
        cd = dram_in(nc, "consts", [128, 384])
        vd = dram_in(nc, "vec", [128, 24])
        wg = dram_in(nc, "wg", [NF, 128, NK * 128])
        wu = dram_in(nc, "wu", [NF, 128, NK * 128])
        wd = dram_in(nc, "wd", [NK, 128, NF * 128])
        h_out = dram_out(nc, "h_out", [128, NK, NT])
        u_out = dram_out(nc, "u_out", [128, NK, NT], BF16)
        load_consts(cx, cd)
        vec = load_vecs(cx, vd, 24)
        hT = ar.alloc("hT", [NK, NT])
        for k in range(NK):
            S.dma("sp", hT[:, k, :], h_in[:, k, :])
        g = ar.alloc("gains", [24])
        S.ts("dve", g[:, 0:8], vec[:, 0:8], 32.0, ALU.mult)
        S.ts("dve", g[:, 8:16], vec[:, 8:16], 16.0, ALU.mult)
        S.ts("dve", g[:, 16:24], vec[:, 16:24], 32.0, ALU.mult)
        emit_ffn(cx, hT, NT, wg, wu, wd, g[:, 0:8], g[:, 8:16])
        uT = ar.alloc("uT", [NK, NT], BF16)
        emit_norm_to(cx, hT, NT, g[:, 16:24], uT)
        for k in range(NK):
            S.dma("sp", h_out[:, k, :], hT[:, k, :])
            S.dma("sp", u_out[:, k, :], uT[:, k, :])
        S.emit()
    return nc


CB = dict(r=0, k=64, v=128, wl=192, al=256, gl=320, vl=448, cq=512, ckv=896, kr=1152, krs=1248,
          hq=1344, hf=1472, hi=1600, hg=1728)
NCOLB = 1856
SM = dict(w_up=0, a_up=64, g_up=128, vres_up=192, uq=256, uqs=544, ukv=832)
NSM = 1088
VB = dict(mu_r=0, mu_k=1, mu_v=2, mu_wl=3, mu_al=4, mu_gl=5, mu_vl=6, w0=7, a0=8, k_k=9, k_a=10, r_k=11,
          gn_g=12, gn_b=13, v0=14, qg=15, kvg=18, lb=20, hng=24)
NVB = 25
CC = dict(maskAT=0, maskN=512, I64=1024, triH=1536, triA=1664, invf=1792, sgn=1793, sel96=1794)
NCB = 1896
C0 = float(np.exp(-0.5))
A_GN_EPS = 64e-5
TWO_PI = float(2 * np.pi)
PI = float(np.pi)


def make_constsB():
    c = np.zeros((128, NCB), np.float32)
    m = np.zeros((128, 128), np.float32)
    il = np.arange(64)[:, None]
    tl = np.arange(64)[None, :]
    for half in range(2):
        m[half * 64:(half + 1) * 64, 0:64] = (il < tl)
        m[half * 64:(half + 1) * 64, 64:128] = (il <= tl)
    c[:, 0:512] = np.tile(m, (1, 4))
    mn = (np.arange(64)[:, None] > np.arange(64)[None, :]).astype(np.float32)
    c[0:64, 512:1024] = np.tile(mn, (1, 8))
    c[0:64, 1024:1536] = np.tile(np.eye(64, dtype=np.float32), (1, 8))
    th = (np.arange(32)[:, None] <= np.arange(32)[None, :]).astype(np.float32)
    c[:, 1536:1664] = np.tile(np.tile(th, (4, 1)), (1, 4))
    c[:, 1664:1792] = (np.arange(128)[:, None] <= np.arange(128)[None, :])
    invf = (10000.0 ** (-np.arange(0, 32, 2, dtype=np.float32) / 32)).astype(np.float32)
    c[64:80, CC["invf"]] = invf
    c[80:96, CC["invf"]] = invf
    c[64:80, CC["sgn"]] = -1.0
    c[80:96, CC["sgn"]] = 1.0
    c[0:96, CC["sel96"] + 96] = 1.0
    return c


import os
PHASES = os.environ.get("MIX_PHASES", "rmh")


def emit_cumsum(S, src, bufA, bufB, rows, nch, C):
    v = lambda t: t[rows].rearrange("p (c t) -> p c t", c=nch)
    cur, bufs, i, s = src, [bufA, bufB], 0, 1
    while s < C:
        nxt = bufs[i % 2]
        S.copy("pool", v(nxt)[:, :, 0:s], v(cur)[:, :, 0:s])
        S.tt("dve", v(nxt)[:, :, s:C], v(cur)[:, :, s:C], v(cur)[:, :, 0:C - s], ALU.add)
        cur = nxt
        i += 1
        s *= 2
    return cur, bufs[i % 2]


def emit_mixer(cx, layer, SL, u_d, wB_d, wsm_d, vec_d, pos_d, cB_d, vf_in_d, vf_out_d, y_d):
    S, ar, ps = cx.S, cx.ar, cx.ps
    L0 = (layer == 0)
    ntile = SL // TT
    nblk = SL // 128
    m_all = ar.mark()
    SCALE = float(96 ** -0.5)
    wB = ar.alloc("wB", [NK, NCOLB], BF16)
    for k in range(NK):
        S.dma("pool", wB[:, k, :], wB_d[:, k, :])
    wsm = ar.alloc("wsm", [NSM])
    S.dma("sp", wsm, wsm_d)
    wuq = ar.alloc("wuq", [3, 96], BF16)
    wuqs = ar.alloc("wuqs", [3, 96], BF16)
    wukv = ar.alloc("wukv", [2, 128], BF16)
    S.copy("dve", wuq, wsm[:, SM["uq"]:SM["uq"] + 288].rearrange("p (a b) -> p a b", a=3))
    S.copy("dve", wuqs, wsm[:, SM["uqs"]:SM["uqs"] + 288].rearrange("p (a b) -> p a b", a=3))
    S.copy("dve", wukv, wsm[:, SM["ukv"]:SM["ukv"] + 256].rearrange("p (a b) -> p a b", a=2))
    w_up = wsm[0:64, SM["w_up"]:SM["w_up"] + 64]
    a_up = wsm[0:64, SM["a_up"]:SM["a_up"] + 64]
    g_up = wsm[:, SM["g_up"]:SM["g_up"] + 64]
    vres_up = wsm[0:32, SM["vres_up"]:SM["vres_up"] + 64]
    vec = ar.alloc("vecB", [NVB])
    S.dma("sp", vec, vec_d)
    cB = ar.alloc("cB", [NCB])
    S.dma("sp", cB, cB_d)
    triA = ar.alloc("triA", [128], BF16)
    S.copy("dve", triA, cB[:, CC["triA"]:CC["triA"] + 128])
    sel96 = ar.alloc("sel96", [97], BF16)
    S.copy("dve", sel96, cB[:, CC["sel96"]:CC["sel96"] + 97])
    maskAT = cB[:, CC["maskAT"]:CC["maskAT"] + 512]
    maskN = cB[0:64, CC["maskN"]:CC["maskN"] + 512]
    I64 = cB[0:64, CC["I64"]:CC["I64"] + 512]
    triH = cB[:, CC["triH"]:CC["triH"] + 128].rearrange("p (b t) -> p b t", b=4)
    invf = cB[:, CC["invf"]:CC["invf"] + 1]
    sgn = cB[:, CC["sgn"]:CC["sgn"] + 1]
    ones64 = cx.ones_f[0:64, 0:64]
    id64 = cx.ident_b[0:64, 0:64]

    def V(name, rows=128):
        return vec[0:rows, VB[name]:VB[name] + 1]

    dv = ar.alloc("dvec", [16])
    S.ts("dve", dv[:, 0:7], vec[:, 0:7], -1.0, ALU.mult, 1.0, ALU.add)
    S.ts("dve", dv[:, 7:8], vec[:, VB["k_a"]:VB["k_a"] + 1], -1.0, ALU.mult, 1.0, ALU.add)
    S.ts("dve", dv[:, 8:11], vec[:, VB["qg"]:VB["qg"] + 3], float(np.sqrt(384.0)), ALU.mult)
    S.ts("dve", dv[:, 11:13], vec[:, VB["kvg"]:VB["kvg"] + 2], 16.0, ALU.mult)
    S.ts("dve", dv[:, 15:16], vec[:, VB["hng"]:VB["hng"] + 1], float(np.sqrt(128.0)), ALU.mult)
    if L0:
        S.memset("dve", dv[:, 13:14], 0.0)
    else:
        le = ar.alloc("lbe", [8])
        S.act(le[:, 0:4], vec[:, VB["lb"]:VB["lb"] + 4], AF.Exp)
        S.reduce("dve", le[:, 4:5], le[:, 0:4], ALU.add)
        S.recip(le[:, 4:5], le[:, 4:5])
        S.reduce("dve", le[:, 5:6], le[:, 1:layer + 1], ALU.add)
        S.tt("dve", dv[:, 13:14], le[:, 5:6], le[:, 4:5], ALU.mult)
    S.ts("dve", dv[:, 14:15], dv[:, 13:14], -1.0, ALU.mult, 1.0, ALU.add)
    mpi = ar.alloc("mpi", [8])
    S.memset("dve", mpi, -PI)

    KT = ar.alloc("KT", [SL], BF16)
    Vaug = ar.alloc("Vaug", [nblk, 65], BF16)
    S.memset("pool", KT[96:97, :], 1.0)
    S.memset("pool", Vaug[:, :, 64:65], 1.0)
    kmax2 = ar.alloc("kmax2", [8])
    S.memset("dve", kmax2[96:97, :], 0.0)
    SV = ar.alloc("SV", [9, 64], BF16)
    SVf = ar.alloc("SVf", [9, 64])
    S.memset("dve", SV[0:64, 0, :], 0.0)
    S.memset("dve", SVf[0:64, 0, :], 0.0)
    Sh = ar.alloc("Sh", [17, 128])
    S.memset("pool", Sh[:, 0, :], 0.0)
    shifted = [("r", 64), ("k", 64), ("v", 64), ("wl", 64), ("al", 64), ("gl", 128)] + ([] if L0 else [("vl", 32)])
    halo = ar.alloc("halo", [8])
    S.memset("dve", halo, 0.0)
    ut = [ar.alloc("ut%d" % i, [NK, TT], BF16) for i in range(2)]
    ya_o = ar.alloc("ya_o", [TT], BF16)
    yb_o = ar.alloc("yb_o", [TT], BF16)
    yc_o = ar.alloc("yc_o", [TT], BF16)
    gC = ar.alloc("gC", [16])
    gCh = ar.alloc("gCh", [16])
    QT = ar.alloc("QT", [TT], BF16)
    Pb = [ar.alloc("Pb%d" % i, [TT], BF16) for i in range(3)]
    rd = ar.alloc("rd", [TT])
    ob = ar.alloc("ob", [TT])
    sc = Scratch(ar, (ar.words - ar.off) // 512)

    rot = [0]

    def bank():
        b = ps[rot[0] % 3]
        rot[0] += 1
        return b

    def proj(off, M, utile):
        pb = bank()
        for k in range(NK):
            S.mm(pb[0:M, :], wB[:, k, off:off + M], utile[:, k, :], start=(k == 0), stop=(k == NK - 1))
        return pb

    S.dma("sp", ut[0], u_d[:, :, 0:TT])
    for it in range(ntile):
        c0 = it * TT
        u_t = ut[it % 2]
        S.mute = False
        if it + 1 < ntile:
            S.dma("sp", ut[(it + 1) % 2], u_d[:, :, c0 + TT:c0 + 2 * TT])
        R = slice(0, 64)
        RR = slice(64, 96)

        def do_rwkv():
            S.mute = "r" not in PHASES
            sc.reset()
            A = lambda n=TT, dt=F32: sc.get([n], dt)
            rws = [A(), A()]
            tmpm = A()
            xr, xk, xwl, xal, xgl = A(), A(), A(), A(), A()
            xvf = A(TT + 64)
            xv = xvf[:, 64:64 + TT]
            xvl = A()
            cs, a_t, g_t, kk, bb, bon = A(), A(), A(), A(), A(), A()
            E1, E2, E3, E4 = rws[0], rws[1], tmpm, A()
            t1, t2 = A(), A()
            WL = sc.get([8, 64], BF16)
            RA = sc.get([8, 64], BF16)
            ARB = sc.get([8, 64], BF16)
            BK = sc.get([8, 2, 64], BF16)
            BKh = sc.get([8, 2, 64], BF16)
            xvb = sc.get([TT + 64], BF16)
            Nm = [sc.get([8, 64], BF16) for i in range(2)]
            Mm = [sc.get([8, 64], BF16) for i in range(2)]
            Pm = [sc.get([8, 64]) for i in range(2)]
            Pmb = [sc.get([8, 64], BF16) for i in range(2)]
            BKhT = sc.get([8, 64], BF16)
            UV = sc.get([8, 64], BF16)
            Wsb = A(64, BF16)
            yT = xal
            vft = xwl
            mixed = {"r": xr, "k": xk, "v": xv, "wl": xwl, "al": xal, "gl": xgl, "vl": xvl}
            for gi, (nm, M) in enumerate(shifted):
                pb = proj(CB[nm], M, u_t)
                rw = rws[gi % 2]
                S.copy("act", rw[0:M], pb[0:M, :])
                mu = vec[0:M, gi:gi + 1]
                omu = dv[0:M, gi:gi + 1]
                S.ts("dve", tmpm[0:M, 1:TT], rw[0:M, 0:TT - 1], mu, ALU.mult)
                S.ts("dve", tmpm[0:M, 0:1], halo[0:M, gi:gi + 1], mu, ALU.mult)
                S.stt("dve", mixed[nm][0:M], rw[0:M], omu, tmpm[0:M], ALU.mult, ALU.add)
                S.copy("pool", halo[0:M, gi:gi + 1], rw[0:M, TT - 1:TT])
            S.memset("pool", xvf[0:64, 0:64], 0.0)
            R = slice(0, 64)
            S.act(xwl[R], xwl[R], AF.Tanh)
            pb = bank()
            S.mm(pb[R, :], w_up, xwl[R])
            S.act(t1[R], pb[R, :], AF.Sigmoid, bias=V("w0", 64))
            cs, t2 = emit_cumsum(S, t1, cs, t2, R, 8, 64)
            S.tt("dve", t2[R], cs[R], t1[R], ALU.subtract)
            S.act(E1[R], cs[R], AF.Exp, scale=-C0)
            S.act(E2[R], cs[R], AF.Exp, scale=C0)
            S.act(E3[R], t2[R], AF.Exp, scale=-C0)
            cs3 = cs[R].rearrange("p (c t) -> p c t", c=8)
            S.tt("dve", t2[R].rearrange("p (c t) -> p c t", c=8), cs3[:, :, 63:64].bcast([64, 8, 64]), cs3, ALU.subtract)
            S.act(E4[R], t2[R], AF.Exp, scale=-C0)
            S.act(gC[R, 0:8], cs3[:, :, 63:64].rearrange("p c o -> p (c o)"), AF.Exp, scale=-C0)
            pb = bank()
            S.mm(pb[R, :], a_up, xal[R])
            S.act(a_t[R], pb[R, :], AF.Sigmoid, bias=V("a0", 64))
            S.act(xgl, xgl, AF.Sigmoid)
            pb = bank()
            S.mm(pb[R, :], g_up, xgl)
            S.copy("act", g_t[R], pb[R, :])
            if L0:
                S.dma("sp", vf_out_d[:, c0:c0 + TT], xv[R])
            else:
                S.dma("sp", vft[R], vf_in_d[:, c0:c0 + TT])
                pb = bank()
                S.mm(pb[R, :], vres_up, xvl[0:32])
                S.act(t1[R], pb[R, :], AF.Sigmoid, bias=V("v0", 64))
                S.tt("dve", vft[R], vft[R], xv[R], ALU.subtract)
                S.tt("dve", vft[R], vft[R], t1[R], ALU.mult)
                S.tt("dve", xv[R], xv[R], vft[R], ALU.add)
            S.ts("dve", kk[R], xk[R], V("k_k", 64), ALU.mult)
            S.tt("dve", t1[R], kk[R], kk[R], ALU.mult)
            pb = bank()
            S.mm(pb[R, :], ones64, t1[R])
            S.act(t1[R], pb[R, :], AF.Sqrt)
            S.ts("dve", t1[R], t1[R], 1e-12, ALU.max)
            S.recip(t1[R], t1[R])
            S.tt("dve", kk[R], kk[R], t1[R], ALU.mult)
            S.ts("dve", t1[R], a_t[R], V("k_a", 64), ALU.mult, dv[R, 7:8], ALU.add)
            S.tt("dve", xk[R], xk[R], t1[R], ALU.mult)
            S.stt("dve", t1[R], xr[R], V("r_k", 64), xk[R], ALU.mult, ALU.mult)
            pb = bank()
            S.mm(pb[R, :], ones64, t1[R])
            S.tt("dve", bon[R], pb[R, :], xv[R], ALU.mult)
            S.tt("dve", bb[R], kk[R], a_t[R], ALU.mult)
            v3 = lambda t: t[R].rearrange("p (c t) -> p c t", c=8)
            S.stt("dve", WL[R], v3(kk), -1.0, v3(E3), ALU.mult, ALU.mult)
            S.tt("dve", RA[R], v3(xr), v3(E1), ALU.mult)
            S.copy("pool", xvb[R], xvf[R])
            S.tt("dve", BK[R, :, 0, :], v3(bb), v3(E2), ALU.mult)
            S.tt("dve", BK[R, :, 1, :], v3(xk), v3(E2), ALU.mult)
            S.tt("dve", BKh[R, :, 0, :], v3(bb), v3(E4), ALU.mult)
            S.tt("dve", BKh[R, :, 1, :], v3(xk), v3(E4), ALU.mult)
            f2 = lambda t, c: t[R, c].rearrange("p a b -> p (a b)")
            m4 = maskAT.rearrange("p (c h t) -> p c h t", c=4, h=2)
            for half in range(2):
                pb = bank()
                for cc in range(4):
                    c = half * 4 + cc
                    S.mm(pb[:, cc * 128:cc * 128 + 64], f2(BK, c), WL[R, c, :])
                    S.mm(pb[:, cc * 128 + 64:(cc + 1) * 128], f2(BK, c), RA[R, c, :])
                p4 = pb.rearrange("p (c h t) -> p c h t", c=4, h=2)
                cs_ = slice(half * 4, half * 4 + 4)
                S.tt("dve", Mm[0][R, cs_, :], p4[R, :, 0, :], m4[R, :, 0, :], ALU.mult)
                S.tt("dve", ARB[R, cs_, :], p4[R, :, 1, :], m4[R, :, 1, :], ALU.mult)
                S.tt("dve", WL[64:128, cs_, :], p4[64:128, :, 0, :], m4[64:128, :, 0, :], ALU.mult)
                S.tt("dve", RA[64:128, cs_, :], p4[64:128, :, 1, :], m4[64:128, :, 1, :], ALU.mult)
            pb = bank()
            for c in range(8):
                S.mm(pb[R, c * 64:(c + 1) * 64], WL[R, c, :], BK[R, c, 0, :])
            S.tt("dve", Nm[0][R].rearrange("p a b -> p (a b)"), pb[R, :], maskN, ALU.mult)
            S.tt("pool", Pm[0][R].rearrange("p a b -> p (a b)"), Mm[0][R].rearrange("p a b -> p (a b)"), I64, ALU.add)
            S.copy("pool", Pmb[0][R], Pm[0][R])
            cur = 0
            for rnd in range(5):
                Mc, Nc, Pc = Mm[cur], Nm[cur], Pm[cur]
                Mn, Nn, Pn = Mm[1 - cur], Nm[1 - cur], Pm[1 - cur]
                Pcb, Pnb = Pmb[cur], Pmb[1 - cur]
                pbm = bank()
                pbn = bank()
                for c in range(8):
                    S.mm(pbm[R, c * 64:(c + 1) * 64], Nc[R, c, :], Mc[R, c, :])
                for c in range(8):
                    S.mm(pbn[R, c * 64:(c + 1) * 64], Mc[R, c, :], Nc[R, c, :])
                S.copy("act", Mn[R].rearrange("p a b -> p (a b)"), pbm[R, :])
                S.copy("dve", Nn[R].rearrange("p a b -> p (a b)"), pbn[R, :])
                pbp = bank()
                for c in range(8):
                    S.mm(pbp[R, c * 64:(c + 1) * 64], Nn[R, c, :], Pcb[R, c, :])
                S.tt("dve", Pn[R].rearrange("p a b -> p (a b)"), pbp[R, :], Pc[R].rearrange("p a b -> p (a b)"), ALU.add)
                S.copy("pool", Pnb[R], Pn[R])
                cur = 1 - cur
            Pf = Pmb[cur]
            pb = bank()
            for c in range(8):
                S.transpose(pb[:, c * 64:(c + 1) * 64], f2(BKh, c), id64)
            S.copy("act", BKhT.rearrange("p a b -> p (a b)"), pb)
            pb = bank()
            for c in range(8):
                S.transpose(pb[:, c * 64:(c + 1) * 64], xvb[R, c * 64:c * 64 + 128], id64)
            S.copy("dve", UV[64:128].rearrange("p a b -> p (a b)"), pb[64:128, :])
            S.copy("dve", SV[64:128, 0:8, :].rearrange("p a b -> p (a b)"), pb[64:128, :])
            ypb = ps[4]
            for c in range(8):
                pw = bank()
                S.mm(pw[R, 0:64], WL[:, c, :], SV[:, c, :])
                S.copy("act", Wsb[R], pw[R, 0:64])
                pu = bank()
                S.mm(pu[R, 0:64], Pf[R, c, :], Wsb[R])
                S.copy("dve", UV[R, c, :], pu[R, 0:64])
                S.mm(ypb[R, c * 64:(c + 1) * 64], SV[:, c, :], RA[:, c, :], start=True, stop=False)
                S.mm(ypb[R, c * 64:(c + 1) * 64], UV[R, c, :], ARB[R, c, :], start=False, stop=True)
                pn = bank()
                S.mm(pn[R, 0:64], BKhT[:, c, :], UV[:, c, :])
                S.stt("dve", SVf[R, c + 1, :], SVf[R, c, :], gC[R, c:c + 1], pn[R, 0:64], ALU.mult, ALU.add)
                S.copy("act", SV[R, c + 1, :], SVf[R, c + 1, :])
            S.copy("pool", SV[R, 0, :], SV[R, 8, :])
            S.copy("pool", SVf[R, 0, :], SVf[R, 8, :])
            S.copy("act", yT[R], ypb[R, :])
            pb = bank()
            S.mm(pb[R, :], ones64, yT[R])
            S.stt("dve", yT[R], pb[R, :], -1.0 / 64, yT[R], ALU.mult, ALU.add)
            S.act(t1[R], yT[R], AF.Square)
            pb = bank()
            S.mm(pb[R, :], ones64, t1[R])
            S.act(t1[R], pb[R, :], AF.Sqrt, bias=A_GN_EPS, scale=1.0 / 64)
            S.recip(t1[R], t1[R])
            S.tt("dve", yT[R], yT[R], t1[R], ALU.mult)
            S.ts("dve", yT[R], yT[R], V("gn_g", 64), ALU.mult, V("gn_b", 64), ALU.add)
            S.tt("dve", yT[R], yT[R], bon[R], ALU.add)
            S.tt("dve", ya_o[R], yT[R], g_t[R], ALU.mult)
            S.dma("sp", y_d[0:64, c0:c0 + TT], ya_o[R])

        def do_mla_prep():
            S.mute = "m" not in PHASES
            sc.reset()
            A = lambda n=TT, dt=F32: sc.get([n], dt)
            t1, t2 = A(), A()
            cq_sb = sc.get([3, TT])
            sqb = [A(TT, BF16) for i in range(2)]
            rsq = A()
            cqn = sc.get([3, TT], BF16)
            ckvn = sc.get([2, TT], BF16)
            posi = sc.get([TT], I32)
            cos2, sin2 = A(), A()
            qr = A()
            qsq = A(TT, BF16)
            ksq = A(TT, BF16)
            rdm = A()
            def latent_norm(off, nch, dst, gcol, nfeat):
                ssb = ps[4]
                for j in range(nch):
                    pb = proj(off + j * 128, 128, u_t)
                    S.copy("act", cq_sb[:, j, :], pb)
                    b = sqb[j % 2]
                    S.act(b, pb, AF.Square)
                    S.mm(ssb, cx.ones_b, b, start=(j == 0), stop=(j == nch - 1))
                S.act(rsq, ssb, AF.Sqrt, bias=float(nfeat * EPS))
                S.recip(rsq, rsq)
                for j in range(nch):
                    S.stt("dve", dst[:, j, :], cq_sb[:, j, :], dv[:, gcol + j:gcol + j + 1], rsq, ALU.mult, ALU.mult)
            latent_norm(CB["cq"], 3, cqn, 8, 384)
            pq = bank()
            for j in range(3):
                S.mm(pq[0:96, :], wuq[:, j, :], cqn[:, j, :], start=(j == 0), stop=(j == 2))
            pqs = bank()
            for j in range(3):
                S.mm(pqs[0:96, :], wuqs[:, j, :], cqn[:, j, :], start=(j == 0), stop=(j == 2))
            RR = slice(64, 96)
            S.dma("sp", posi[RR], pos_d[:, c0:c0 + TT])
            S.copy("dve", t1[RR], posi[RR])
            S.ts("dve", t1[RR], t1[RR], invf[RR], ALU.mult)
            def sincos(dst, shift):
                S.ts("dve", t2[RR], t1[RR], 1.0 / TWO_PI, ALU.mult, shift, ALU.add)
                S.copy("dve", posi[RR], t2[RR])
                S.copy("dve", rdm[RR], posi[RR])
                S.tt("dve", t2[RR], t2[RR], rdm[RR], ALU.subtract)
                S.ts("dve", rdm[RR], t2[RR], 0.0, ALU.is_lt)
                S.tt("dve", t2[RR], t2[RR], rdm[RR], ALU.add)
                S.act(dst[RR], t2[RR], AF.Sin, bias=mpi[RR, 0:1], scale=TWO_PI)
            sincos(sin2, 0.5)
            S.ts("dve", sin2[RR], sin2[RR], sgn[RR], ALU.mult)
            sincos(cos2, 0.75)
            S.act(QT[0:64], pq[0:64, :], AF.Copy, scale=SCALE)
            S.tt("dve", qr[RR], pq[RR, :], cos2[RR], ALU.mult)
            S.tt("dve", t1[RR], pqs[RR, :], sin2[RR], ALU.mult)
            S.tt("dve", qr[RR], qr[RR], t1[RR], ALU.add)
            S.act(QT[RR], qr[RR], AF.Copy, scale=SCALE)
            S.act(qsq[0:64], pq[0:64, :], AF.Square, scale=SCALE)
            S.act(qsq[RR], qr[RR], AF.Square, scale=SCALE)
            latent_norm(CB["ckv"], 2, ckvn, 11, 256)
            pkv = bank()
            for j in range(2):
                S.mm(pkv, wukv[:, j, :], ckvn[:, j, :], start=(j == 0), stop=(j == 1))
            S.copy("act", KT[0:64, c0:c0 + TT], pkv[0:64, :])
            pkr = proj(CB["kr"], 96, u_t)
            pkrs = proj(CB["krs"], 96, u_t)
            S.tt("dve", t1[RR], pkr[RR, :], cos2[RR], ALU.mult)
            S.tt("dve", t2[RR], pkrs[RR, :], sin2[RR], ALU.mult)
            S.tt("dve", KT[RR, c0:c0 + TT], t1[RR], t2[RR], ALU.add)
            S.act(ksq[0:96], KT[0:96, c0:c0 + TT], AF.Square)
            pb = bank()
            S.mm(pb[0:97, :], sel96[0:96, :], ksq[0:96])
            S.reduce("dve", kmax2[96:97, 1:2], pb[96:97, :], ALU.max)
            S.tt("dve", kmax2[96:97, 0:1], kmax2[96:97, 0:1], kmax2[96:97, 1:2], ALU.max)
            pb = bank()
            S.mm(pb[0:97, :], sel96[0:96, :], qsq[0:96])
            S.ts("dve", t1[96:97], pb[96:97, :], kmax2[96:97, 0:1], ALU.mult)
            S.act(t1[96:97], t1[96:97], AF.Sqrt)
            S.ts("dve", QT[96:97], t1[96:97], -1.0, ALU.mult)
            pb = bank()
            for blk in range(4):
                for j in range(2):
                    S.mm(pb[:, blk * 64:(blk + 1) * 64], ckvn[:, j, blk * 128:(blk + 1) * 128], wukv[:, j, 64:128],
                         start=(j == 0), stop=(j == 1))
            S.copy("act", Vaug[:, 4 * it:4 * it + 4, 0:64], pb[:, 0:256].rearrange("p (a b) -> p a b", a=4))
        def do_attn():
            ob_ps = ps[7]
            nkb = 4 * it + 4
            def qlo_of(kb):
                return max(kb - 4 * it, 0) * 128

            for kb in range(nkb + 2):
                if kb < nkb:
                    d = kb - 4 * it
                    qlo = qlo_of(kb)
                    sp_ = ps[(3, 5, 6)[kb % 3]]
                    pt = Pb[kb % 3]
                    S.mm(sp_[:, qlo:TT], KT[0:97, kb * 128:(kb + 1) * 128], QT[0:97, qlo:TT])
                    S.act(pt[:, qlo:TT], sp_[:, qlo:TT], AF.Exp)
                    if d >= 0:
                        S.tt("pool", pt[:, qlo:qlo + 128], pt[:, qlo:qlo + 128], triA, ALU.mult)
                if kb >= 2:
                    kp = kb - 2
                    qlo = qlo_of(kp)
                    S.mm(ob_ps[0:65, qlo:TT], Vaug[:, kp, :], Pb[kp % 3][:, qlo:TT], start=(kp == 0), stop=(kp == nkb - 1))
            S.recip(rd[0:1], ob_ps[64:65, :])
            S.copy("act", ob[R], ob_ps[R, :])
            pb = ps[5]
            S.mm(pb[R, :], cx.ones_f[0:1, 0:64], rd[0:1])
            S.tt("dve", yb_o[R], ob[R], pb[R, :], ALU.mult)
            S.dma("sp", y_d[64:128, c0:c0 + TT], yb_o[R])

        def do_hgrn():
            S.mute = "h" not in PHASES
            sc.reset()
            A = lambda n=TT, dt=F32: sc.get([n], dt)
            t1, t2 = A(), A()
            qh, kx, clh, qb, gh = A(), A(), A(), A(), A()
            qtl, ktl, khat = A(TT, BF16), A(TT, BF16), A(TT, BF16)
            Vh = sc.get([16, 128], BF16)
            KhT = sc.get([16, 128], BF16)
            KV = sc.get([16, 128])
            ATh = sc.get([16, 32], BF16)
            oh = A()
            sqb = [A(TT, BF16)]
            rsq = A()
            pb = proj(CB["hq"], 128, u_t)
            S.act(qh, pb, AF.Silu)
            pb = proj(CB["hf"], 128, u_t)
            S.act(t1, pb, AF.Sigmoid)
            S.ts("dve", t1, t1, dv[:, 14:15], ALU.mult, dv[:, 13:14], ALU.add)
            S.ts("dve", kx, t1, -1.0, ALU.mult, 1.0, ALU.add)
            S.ts("dve", t1, t1, 1e-6, ALU.max)
            S.act(t1, t1, AF.Ln)
            clh, t2 = emit_cumsum(S, t1, clh, t2, slice(0, 128), 16, 32)
            c16 = lambda t: t.rearrange("p (c t) -> p c t", c=16)
            S.act(t2, clh, AF.Exp)
            S.tt("dve", qb, qh, t2, ALU.mult)
            S.tt("dve", c16(t1), c16(clh), c16(clh)[:, :, 15:16].bcast([128, 16, 32]), ALU.subtract)
            S.act(t2, t1, AF.Exp)
            S.tt("dve", qtl, qh, t2, ALU.mult)
            S.act(t2, t1, AF.Exp, scale=-1.0)
            S.tt("dve", ktl, kx, t2, ALU.mult)
            S.tt("dve", c16(t1), c16(clh), c16(clh)[:, :, 31:32].bcast([128, 16, 32]), ALU.subtract)
            S.act(t2, t1, AF.Exp, scale=-1.0)
            S.tt("dve", khat, kx, t2, ALU.mult)
            S.act(gCh[:, 0:16], c16(clh)[:, :, 31:32].rearrange("p c o -> p (c o)"), AF.Exp)
            Q = slice(0, 32)
            for blk in range(4):
                pb = bank()
                for j in range(4):
                    c = blk * 4 + j
                    for k in range(NK):
                        S.mm(pb[Q, j * 128:(j + 1) * 128], u_t[:, k, c * 32:(c + 1) * 32],
                             wB[:, k, CB["hi"]:CB["hi"] + 128], start=(k == 0), stop=(k == NK - 1))
                S.copy("act", Vh[Q, blk * 4:blk * 4 + 4, :].rearrange("p a b -> p (a b)"), pb[Q, :])
            pb = proj(CB["hg"], 128, u_t)
            S.act(gh, pb, AF.Silu)
            for blk in range(4):
                pb = bank()
                for j in range(4):
                    c = blk * 4 + j
                    S.transpose(pb[Q, j * 128:(j + 1) * 128], khat[:, c * 32:(c + 1) * 32], cx.ident_b)
                S.copy("dve", KhT[Q, blk * 4:blk * 4 + 4, :].rearrange("p a b -> p (a b)"), pb[Q, :])
            for blk in range(4):
                pb = bank()
                for j in range(4):
                    c = blk * 4 + j
                    S.mm(pb[:, j * 128:(j + 1) * 128], KhT[Q, c, :], Vh[Q, c, :])
                S.copy("act" if blk % 2 == 0 else "dve", KV[:, 4 * blk:4 * blk + 4, :].rearrange("p a b -> p (a b)"), pb)
            pb = bank()
            for c in range(16):
                S.mm(pb[Q, c * 32:(c + 1) * 32], ktl[:, c * 32:(c + 1) * 32], qtl[:, c * 32:(c + 1) * 32])
            S.tt("dve", ATh[Q], pb[Q, :].rearrange("p (c t) -> p c t", c=16),
                 triH[Q, 0:1, :].bcast([32, 16, 32]), ALU.mult)
            for c in range(16):
                S.stt("dve", Sh[:, c + 1, :], Sh[:, c, :], gCh[:, c:c + 1], KV[:, c, :], ALU.mult, ALU.add)
            ohp = ps[4]
            for c in range(16):
                S.mm(ohp[:, c * 32:(c + 1) * 32], Sh[:, c, :], qb[:, c * 32:(c + 1) * 32], start=True, stop=False)
                S.mm(ohp[:, c * 32:(c + 1) * 32], Vh[Q, c, :], ATh[Q, c, :], start=False, stop=True)
            S.copy("pool", Sh[:, 0, :], Sh[:, 16, :])
            S.copy("act", oh, ohp)
            S.act(sqb[0], ohp, AF.Square)
            pb = bank()
            S.mm(pb, cx.ones_b, sqb[0])
            S.act(rsq, pb, AF.Sqrt, bias=float(128 * EPS))
            S.recip(rsq, rsq)
            S.stt("dve", oh, oh, dv[:, 15:16], rsq, ALU.mult, ALU.mult)
            S.tt("dve", yc_o, oh, gh, ALU.mult)
            S.dma("sp", y_d[128:192, c0:c0 + TT], yc_o[0:64])
        do_mla_prep()
        S.mute = False
        main_ops, main_lines = S.ops, S.lines
        S.ops, S.lines = [], []
        do_rwkv()
        do_hgrn()
        S.mute = False
        a_ops, a_lines = S.ops, S.lines
        S.ops, S.lines = [], []
        S.mute = "m" not in PHASES
        do_attn()
        S.mute = False
        b_ops, b_lines = S.ops, S.lines
        ia = ib = 0
        na, nb = len(a_ops), len(b_ops)
        while ia < na or ib < nb:
            if ib >= nb or (ia < na and ia * nb <= ib * na):
                main_ops.append(a_ops[ia]); main_lines.append(a_lines[ia]); ia += 1
            else:
                main_ops.append(b_ops[ib]); main_lines.append(b_lines[ib]); ib += 1
        S.ops, S.lines = main_ops, main_lines
    S.mute = False
    if cx.dbg is not None:
        for nm, t in (("QT", QT), ("cos2", cos2), ("sin2", sin2), ("rd", rd), ("ob", ob), ("qr", qr)):
            cx.dbg[nm] = t
        cx.dbg["KT"] = KT
        cx.dbg["Vaug"] = Vaug
    S.barrier()
    cx.scr_peak = sc.peak
    ar.release(m_all)


DEBUG_B = bool(int(os.environ.get("DEBUG_B", "0")))


def build_B(layer, SL):
    nc = bass.Bass("TRN2", target_bir_lowering=False)
    with ExitStack() as es:
        cx = make_ctx(nc, es, ARENA_WORDS)
        u_d = dram_in(nc, "u_full", [128, NK, SL], BF16)
        cd = dram_in(nc, "consts", [128, 384])
        wB_d = dram_in(nc, "wB", [128, NK, NCOLB])
        wsm_d = dram_in(nc, "wsm", [128, NSM])
        vec_d = dram_in(nc, "vecB", [128, NVB])
        pos_d = dram_in(nc, "pos", [32, SL], I32)
        cB_d = dram_in(nc, "cB", [128, NCB])
        y_d = dram_out(nc, "y_out", [192, SL], BF16)
        if layer == 0:
            vf_in, vf_out = None, dram_out(nc, "vf_out", [64, SL])
        else:
            vf_in, vf_out = dram_in(nc, "vf_in", [64, SL]), None
        load_consts(cx, cd)
        if DEBUG_B:
            cx.dbg = {}
        emit_mixer(cx, layer, SL, u_d, wB_d, wsm_d, vec_d, pos_d, cB_d, vf_in, vf_out, y_d)
        if DEBUG_B:
            for nm, t in cx.dbg.items():
                shp = [128] + list(t.ap.shape[1:])
                od = dram_out(nc, "dbg_" + nm, shp, t.ap.dtype)
                cx.S.dma("sp", od, t)
        cx.S.emit()
        print("B stats", cx.S.stats, "scratch peak", cx.scr_peak)
    return nc


def lay_rows(w, nch):
    w = np.asarray(w, np.float32)
    return np.ascontiguousarray(w.reshape(nch, 128, -1).transpose(1, 0, 2))


def prep_B_core(inp, l, c, SL):
    f32 = np.float32
    w_in = np.asarray(inp["w_in"][l], f32)
    hd, hf_ = c // 2, c % 2
    perm = np.concatenate([np.arange(hf_ * 64, hf_ * 64 + 64), np.arange((1 - hf_) * 64, (1 - hf_) * 64 + 64)])
    W = np.zeros((D, NCOLB), f32)

    def put(name, cols):
        W[:, CB[name]:CB[name] + cols.shape[1]] = cols
    put("r", w_in[:, c * 64:(c + 1) * 64])
    put("k", w_in[:, 512 + c * 64:512 + (c + 1) * 64])
    put("v", w_in[:, 1024 + c * 64:1024 + (c + 1) * 64])
    put("wl", w_in[:, 1536:1600])
    put("al", w_in[:, 1600:1664])
    put("gl", w_in[:, 1664:1792])
    if l > 0:
        put("vl", np.asarray(inp["rwkv_vres_down"][l - 1], f32))
    put("cq", w_in[:, OFF_CQ:OFF_CQ + 384])
    put("ckv", w_in[:, OFF_CKV:OFF_CKV + 256])
    kr = w_in[:, OFF_KR:OFF_KR + 32]
    W[:, CB["kr"] + 64:CB["kr"] + 96] = kr
    W[:, CB["krs"] + 64:CB["krs"] + 80] = kr[:, 16:32]
    W[:, CB["krs"] + 80:CB["krs"] + 96] = kr[:, 0:16]
    put("hq", w_in[:, OFF_HQ + hd * 128:OFF_HQ + (hd + 1) * 128])
    put("hf", w_in[:, OFF_HF + hd * 128:OFF_HF + (hd + 1) * 128])
    put("hi", w_in[:, OFF_HI + hd * 128:OFF_HI + (hd + 1) * 128][:, perm])
    put("hg", w_in[:, OFF_HG + hd * 128:OFF_HG + (hd + 1) * 128][:, perm])
    wB = lay_rows(W, NK)
    wsm = np.zeros((128, NSM), f32)
    hs = slice(c * 64, (c + 1) * 64)
    wsm[0:64, SM["w_up"]:SM["w_up"] + 64] = np.asarray(inp["rwkv_w_up"][l], f32)[:, hs]
    wsm[0:64, SM["a_up"]:SM["a_up"] + 64] = np.asarray(inp["rwkv_a_up"][l], f32)[:, hs]
    wsm[:, SM["g_up"]:SM["g_up"] + 64] = np.asarray(inp["rwkv_g_up"][l], f32)[:, hs]
    if l > 0:
        wsm[0:32, SM["vres_up"]:SM["vres_up"] + 64] = np.asarray(inp["rwkv_vres_up"][l - 1], f32)[:, hs]
    uq = np.asarray(inp["mla_w_uq"][l], f32)[:, c * 96:(c + 1) * 96]
    uqs = np.concatenate([uq[:, 0:64], uq[:, 80:96], uq[:, 64:80]], axis=1)
    wsm[:, SM["uq"]:SM["uq"] + 288] = lay_rows(uq, 3).reshape(128, 288)
    wsm[:, SM["uqs"]:SM["uqs"] + 288] = lay_rows(uqs, 3).reshape(128, 288)
    ukv = np.asarray(inp["mla_w_ukv"][l], f32)[:, c * 128:(c + 1) * 128]
    wsm[:, SM["ukv"]:SM["ukv"] + 256] = lay_rows(ukv, 2).reshape(128, 256)
    vec = np.zeros((128, NVB), f32)
    mu = np.asarray(inp["rwkv_mu"][l], f32)
    vec[0:64, VB["mu_r"]] = mu[c * 64:(c + 1) * 64]
    vec[0:64, VB["mu_k"]] = mu[512 + c * 64:512 + (c + 1) * 64]
    vec[0:64, VB["mu_v"]] = mu[1024 + c * 64:1024 + (c + 1) * 64]
    vec[0:64, VB["mu_wl"]] = mu[1536:1600]
    vec[0:64, VB["mu_al"]] = mu[1600:1664]
    vec[:, VB["mu_gl"]] = mu[1664:1792]
    if l > 0:
        vec[0:32, VB["mu_vl"]] = np.asarray(inp["rwkv_vres_mu"][l - 1], f32)
        vec[0:64, VB["v0"]] = np.asarray(inp["rwkv_v0"][l - 1], f32)[hs]
    for nm, key in (("w0", "rwkv_w0"), ("a0", "rwkv_a0"), ("k_k", "rwkv_k_k"), ("k_a", "rwkv_k_a"),
                    ("gn_g", "rwkv_gn_g"), ("gn_b", "rwkv_gn_b")):
        vec[0:64, VB[nm]] = np.asarray(inp[key][l], f32)[hs]
    vec[0:64, VB["r_k"]] = np.asarray(inp["rwkv_r_k"][l], f32)[c]
    vec[:, VB["qg"]:VB["qg"] + 3] = lay_vec(inp["mla_q_norm_g"][l])
    vec[:, VB["kvg"]:VB["kvg"] + 2] = lay_vec(inp["mla_kv_norm_g"][l])
    vec[:, VB["lb"]:VB["lb"] + 4] = np.asarray(inp["hgrn_lower_bounds"], f32)[:, hd * 128:(hd + 1) * 128].T
    vec[:, VB["hng"]] = np.asarray(inp["hgrn_norm_g"][l], f32)[perm]
    pos = np.ascontiguousarray(np.broadcast_to(np.asarray(inp["positions"]).reshape(1, -1)[:, :SL], (32, SL))).astype(np.int32)
    return dict(wB=wB, wsm=wsm, vecB=vec, pos=pos)


def emit_merge(cx, hT, uT, NT, y_d, wgate_d, wouts_d, wo_d, g_post):
    S, ar, ps = cx.S, cx.ar, cx.ps
    m0 = ar.mark()
    wouts = ar.alloc("wouts", [12, D], BF16)
    for j in range(12):
        S.dma("pool", wouts[:, j, :], wouts_d[:, j, :])
    wo = ar.alloc("wo", [NK, D], BF16)
    for k in range(NK):
        S.dma("pool", wo[:, k, :], wo_d[:, k, :])
    wgb = [ar.alloc("wgateb%d" % i, [NK, 384], BF16) for i in range(2)]
    yt = ar.alloc("yt", [12, TT], BF16)
    merged = ar.alloc("merged", [NK, TT], BF16)
    sig = [ar.alloc("sig%d" % i, [TT]) for i in range(3)]
    mt = [ar.alloc("mt%d" % i, [TT]) for i in range(2)]
    z = ar.alloc("z", [NK, TT])
    sq = [ar.alloc("msq%d" % i, [TT], BF16) for i in range(2)]
    rs = ar.alloc("mrs", [TT])
    tmp = [ar.alloc("mtmp%d" % i, [TT]) for i in range(2)]
    ntile = NT // TT
    nld = [0]

    def ld_gate(d):
        S.dma("pool", wgb[nld[0] % 2], wgate_d[d].rearrange("p (k m) -> p k m", k=NK))
        nld[0] += 1

    ld_gate(0)
    nsq = 0
    for t in range(ntile):
        c0 = t * TT
        for j in range(12):
            S.dma("sp", yt[:, j, :], y_d[:, j, c0:c0 + TT])
        for d in range(NK):
            wgt = wgb[(t * NK + d) % 2]
            if t * NK + d + 1 < ntile * NK:
                ld_gate((d + 1) % NK)
            for j in range(3):
                pj = ps[j]
                for k in range(4):
                    S.mm(pj, wouts[:, 4 * j + k, d * 128:(d + 1) * 128], yt[:, 4 * j + k, :], start=(k == 0), stop=(k == 3))
                gj = ps[3 + j]
                for k in range(NK):
                    S.mm(gj, wgt[:, k, j * 128:(j + 1) * 128], uT[:, k, c0:c0 + TT], start=(k == 0), stop=(k == NK - 1))
                S.act(sig[j], gj, AF.Sigmoid)
            S.tt("dve", mt[0], sig[0], ps[0], ALU.mult)
            S.tt("dve", mt[1], sig[1], ps[1], ALU.mult)
            S.tt("pool", mt[0], mt[0], mt[1], ALU.add)
            S.tt("dve", mt[1], sig[2], ps[2], ALU.mult)
            S.tt("pool", merged[:, d, :], mt[0], mt[1], ALU.add)
        for d in range(NK):
            zp = ps[6]
            for k in range(NK):
                S.mm(zp, wo[:, k, d * 128:(d + 1) * 128], merged[:, k, :], start=(k == 0), stop=(k == NK - 1))
            S.act(z[:, d, :], zp, AF.Copy)
            b = sq[nsq % 2]
            nsq += 1
            S.act(b, zp, AF.Square)
            S.mm(ps[7], cx.ones_b, b, start=(d == 0), stop=(d == NK - 1))
        emit_rstd(cx, ps[7], rs, D)
        for d in range(NK):
            tq = tmp[d % 2]
            S.stt("dve", tq, z[:, d, :], g_post[:, d:d + 1], rs, ALU.mult, ALU.mult)
            S.tt("pool", hT[:, d, c0:c0 + TT], hT[:, d, c0:c0 + TT], tq, ALU.add)
    S.barrier()
    ar.release(m0)


def lay_gate(w_in_l):
    g = np.asarray(w_in_l, np.float32)[:, OFF_GATE:OFF_GATE + 3 * D]
    g = g.reshape(NK, 128, 3, NK, 128)
    return np.ascontiguousarray(g.transpose(3, 1, 0, 2, 4).reshape(NK, 128, NK * 384))


def build_T(NT, merge, nxt):
    nc = bass.Bass("TRN2", target_bir_lowering=False)
    with ExitStack() as es:
        cx = make_ctx(nc, es, ARENA_WORDS)
        S, ar = cx.S, cx.ar
        nv = 24 * (int(merge) + int(nxt))
        h_in = dram_in(nc, "h_in", [128, NK, NT])
        cd = dram_in(nc, "consts", [128, 384])
        vd = dram_in(nc, "vec", [128, nv])
        load_consts(cx, cd)
        vec = load_vecs(cx, vd, nv)
        g = ar.alloc("gains", [nv])
        hT = ar.alloc("hT", [NK, NT])
        for k in range(NK):
            S.dma("sp", hT[:, k, :], h_in[:, k, :])
        o = 0
        if merge:
            u_in = dram_in(nc, "u_in", [128, NK, NT], BF16)
            y_in = dram_in(nc, "y_in", [128, 12, NT], BF16)
            wgate = dram_in(nc, "wgate", [NK, 128, NK * 384])
            wouts = dram_in(nc, "wouts", [128, 12, D])
            wo = dram_in(nc, "wo", [128, NK, D])
            wg2 = dram_in(nc, "wg2", [NF, 128, NK * 128])
            wu2 = dram_in(nc, "wu2", [NF, 128, NK * 128])
            wd2 = dram_in(nc, "wd2", [NK, 128, NF * 128])
            mk = ar.mark()
            uTm = ar.alloc("uTm", [NK, NT], BF16)
            for k in range(NK):
                S.dma("sp", uTm[:, k, :], u_in[:, k, :])
            S.ts("dve", g[:, 0:8], vec[:, 0:8], 32.0, ALU.mult)
            S.ts("dve", g[:, 8:16], vec[:, 8:16], 32.0, ALU.mult)
            S.ts("dve", g[:, 16:24], vec[:, 16:24], 16.0, ALU.mult)
            emit_merge(cx, hT, uTm, NT, y_in, wgate, wouts, wo, g[:, 0:8])
            ar.release(mk)
            emit_ffn(cx, hT, NT, wg2, wu2, wd2, g[:, 8:16], g[:, 16:24], G=2)
            o = 24
        h_out = dram_out(nc, "h_out", [128, NK, NT])
        if nxt:
            wg1 = dram_in(nc, "wg1", [NF, 128, NK * 128])
            wu1 = dram_in(nc, "wu1", [NF, 128, NK * 128])
            wd1 = dram_in(nc, "wd1", [NK, 128, NF * 128])
            u_out = dram_out(nc, "u_out", [128, NK, NT], BF16)
            S.ts("dve", g[:, o:o + 8], vec[:, o:o + 8], 32.0, ALU.mult)
            S.ts("dve", g[:, o + 8:o + 16], vec[:, o + 8:o + 16], 16.0, ALU.mult)
            S.ts("dve", g[:, o + 16:o + 24], vec[:, o + 16:o + 24], 32.0, ALU.mult)
            emit_ffn(cx, hT, NT, wg1, wu1, wd1, g[:, o:o + 8], g[:, o + 8:o + 16], G=2)
            uT = ar.alloc("uT", [NK, NT], BF16)
            emit_norm_to(cx, hT, NT, g[:, o + 16:o + 24], uT)
            for k in range(NK):
                S.dma("sp", u_out[:, k, :], uT[:, k, :])
        for k in range(NK):
            S.dma("sp", h_out[:, k, :], hT[:, k, :])
        S.emit()
    return nc


_PROG = {}


def _prog(key, fn):
    if key not in _PROG:
        _PROG[key] = fn()
    return _PROG[key]


def _run(nc, ins):
    res = run_bass_kernel_spmd(nc, ins, core_ids=list(range(NCORES)))
    return res.results


def kernel_impl(inp, SL, depth):
    NT = SL // NCORES
    consts = make_consts()
    cB = make_constsB()
    x = np.asarray(inp["x"], np.float32).reshape(SL, D)
    hs = [lay_act_T(x[c * NT:(c + 1) * NT]) for c in range(NCORES)]
    us = None
    vfirst = None

    def ffn_w(prefix, l, tag):
        return {"wg" + tag: lay_w_gu(inp[prefix + "_w_gate"][l]), "wu" + tag: lay_w_gu(inp[prefix + "_w_up"][l]),
                "wd" + tag: lay_w_d(inp[prefix + "_w_down"][l])}

    def vecs(names_l):
        return np.concatenate([lay_vec(inp[n][l]) for n, l in names_l], axis=1)

    nc = _prog(("T", NT, False, True), lambda: build_T(NT, False, True))
    com = dict(consts=consts, vec=vecs([("ffn1_pre_g", 0), ("ffn1_post_g", 0), ("mix_pre_g", 0)]))
    com.update(ffn_w("ffn1", 0, "1"))
    res = _run(nc, [dict(com, h_in=hs[c]) for c in range(NCORES)])
    hs = [np.asarray(r["h_out"]) for r in res]
    us = [np.asarray(r["u_out"]) for r in res]
    for l in range(depth):
        u_full = np.ascontiguousarray(np.concatenate(us, axis=2))
        ncb = _prog(("B", l if l < 2 else l, SL), lambda: build_B(l, SL))
        ins = []
        for c in range(NCORES):
            d = prep_B_core(inp, l, c, SL)
            d.update(u_full=u_full, consts=consts, cB=cB)
            if l > 0:
                d["vf_in"] = vfirst[c]
            ins.append(d)
        res = _run(ncb, ins)
        if l == 0:
            vfirst = [np.asarray(r["vf_out"]) for r in res]
        ys = [np.asarray(r["y_out"]) for r in res]
        y_ins = []
        for j in range(NCORES):
            yi = np.zeros((128, 12, NT), dtype=ys[0].dtype)
            for br in range(3):
                for q in range(4):
                    for half in range(2):
                        yi[half * 64:(half + 1) * 64, br * 4 + q, :] = ys[2 * q + half][br * 64:(br + 1) * 64, j * NT:(j + 1) * NT]
            y_ins.append(yi)
        last = (l == depth - 1)
        nct = _prog(("T", NT, True, not last), lambda: build_T(NT, True, not last))
        names = [("mix_post_g", l), ("ffn2_pre_g", l), ("ffn2_post_g", l)]
        if not last:
            names += [("ffn1_pre_g", l + 1), ("ffn1_post_g", l + 1), ("mix_pre_g", l + 1)]
        com = dict(consts=consts, vec=vecs(names), wgate=lay_gate(inp["w_in"][l]),
                   wouts=np.concatenate([lay_rows(inp["rwkv_out"][l], 4), lay_rows(inp["mla_out"][l], 4),
                                         lay_rows(inp["hgrn_out"][l], 4)], axis=1),
                   wo=lay_rows(inp["w_o"][l], NK))
        com.update(ffn_w("ffn2", l, "2"))
        if not last:
            com.update(ffn_w("ffn1", l + 1, "1"))
        res = _run(nct, [dict(com, h_in=hs[c], u_in=us[c], y_in=y_ins[c]) for c in range(NCORES)])
        hs = [np.asarray(r["h_out"]) for r in res]
        if not last:
            us = [np.asarray(r["u_out"]) for r in res]
    out = np.concatenate([unlay_act_T(h) for h in hs], axis=0)
    return out.reshape(1, SL, D).astype(np.float32)


def kernel(**inputs):
    return kernel_impl(inputs, 16384, 4)
```

```python
import numpy as np
from contextlib import ExitStack
import ml_dtypes
import concourse.bass as bass
import concourse.mybir as mybir
from concourse.bass_utils import run_bass_kernel_spmd

F32 = mybir.dt.float32
BF16 = mybir.dt.bfloat16
I32 = mybir.dt.int32
AF = mybir.ActivationFunctionType
ALU = mybir.AluOpType
AX = mybir.AxisListType

NCORES = 8
D = 1024
DFF = 2816
NF = DFF // 128
NK = D // 128
EPS = 1e-6
A_COLS = 1792
N_IN = 7584
OFF_CQ, OFF_CKV, OFF_KR = 1792, 2176, 2432
OFF_HQ, OFF_HF, OFF_HI, OFF_HG, OFF_GATE = 2464, 2976, 3488, 4000, 4512
TT = 512


class T:
    __slots__ = ("ap", "keys")

    def __init__(self, ap, keys):
        self.ap = ap
        self.keys = (keys,) if isinstance(keys, str) else tuple(keys)

    def __getitem__(self, idx):
        return T(self.ap[idx], self.keys)

    def k(self, *keys):
        return T(self.ap, keys)

    def rearrange(self, pat, **kw):
        return T(self.ap.rearrange(pat, **kw), self.keys)

    def bcast(self, shape):
        return T(self.ap.to_broadcast(list(shape)), self.keys)


def _keys(*xs):
    ks = []
    for x in xs:
        if isinstance(x, T):
            ks.extend(x.keys)
    return ks


def _ap(x):
    return x.ap if isinstance(x, T) else x


import os as _os
import sys
MAXOPS = int(_os.environ.get("CUTOPS", "100000000"))
EPOCH_KEY = "__epoch__"
SEM_EPOCH = 30000
NDSEM = 6


class Sched:
    COMPUTE = ("pe", "act", "dve", "pool")

    def __init__(self, nc):
        self.nc = nc
        self.ops = []
        self.lines = []

    mute = False

    def add(self, eng, fn, reads, writes, dma=False):
        if self.mute or len(self.ops) >= MAXOPS:
            return
        self.lines.append(sys._getframe(2).f_lineno)
        self.ops.append((eng, fn, tuple(reads) + (EPOCH_KEY,), tuple(writes), dma))

    def barrier(self):
        o = self.bar_tile.ap
        self.lines.append(0)
        self.ops.append(("dve", lambda e: e.memset(o, 0.0), (), (EPOCH_KEY,) + self.bar_tile.keys, False))

    def mm(self, out, lhsT, rhs, start=True, stop=True, **kw):
        o, l, r = out.ap, lhsT.ap, rhs.ap
        self.add("pe", lambda e: e.matmul(o, lhsT=l, rhs=r, start=start, stop=stop, **kw),
                 _keys(lhsT, rhs), _keys(out))

    def transpose(self, out, in_, ident):
        self.mm(out, in_, ident)

    def act(self, out, in_, func, bias=None, scale=None, accum=None):
        o, i = out.ap, in_.ap
        kw = {}
        if bias is not None:
            kw["bias"] = _ap(bias)
        if scale is not None:
            kw["scale"] = _ap(scale)
        if accum is not None:
            kw["accum_out"] = _ap(accum)
        self.add("act", lambda e: e.activation(out=o, in_=i, func=func, **kw),
                 _keys(in_, bias, scale), _keys(out, accum))

    def tt(self, eng, out, a, b, op):
        o, x, y = out.ap, a.ap, b.ap
        self.add(eng, lambda e: e.tensor_tensor(out=o, in0=x, in1=y, op=op), _keys(a, b), _keys(out))

    def ts(self, eng, out, a, s1, op0, s2=None, op1=None):
        o, x = out.ap, a.ap
        s1a, s2a = _ap(s1), _ap(s2)
        if op1 is None:
            self.add(eng, lambda e: e.tensor_scalar(out=o, in0=x, scalar1=s1a, scalar2=None, op0=op0),
                     _keys(a, s1), _keys(out))
        else:
            self.add(eng, lambda e: e.tensor_scalar(out=o, in0=x, scalar1=s1a, scalar2=s2a, op0=op0, op1=op1),
                     _keys(a, s1, s2), _keys(out))

    def stt(self, eng, out, a, scalar, b, op0, op1):
        o, x, y, s = out.ap, a.ap, b.ap, _ap(scalar)
        eng = "dve"
        self.add(eng, lambda e: e.scalar_tensor_tensor(out=o, in0=x, scalar=s, in1=y, op0=op0, op1=op1),
                 _keys(a, scalar, b), _keys(out))

    def copy(self, eng, out, in_):
        o, i = out.ap, in_.ap
        if eng == "act":
            self.add("act", lambda e: e.activation(out=o, in_=i, func=AF.Copy), _keys(in_), _keys(out))
        else:
            self.add(eng, lambda e: e.tensor_copy(out=o, in_=i), _keys(in_), _keys(out))

    def memset(self, eng, out, val):
        o = out.ap
        self.add(eng, lambda e: e.memset(o, val), (), _keys(out))

    def scan(self, out, d0, d1, init, op0, op1):
        o, a, b = out.ap, d0.ap, d1.ap
        self.add("dve", lambda e: e.tensor_tensor_scan(out=o, data0=a, data1=b, initial=init, op0=op0, op1=op1),
                 _keys(d0, d1), _keys(out))

    def reduce(self, eng, out, in_, op, axis=AX.X):
        o, i = out.ap, in_.ap
        self.add(eng, lambda e: e.tensor_reduce(out=o, in_=i, axis=axis, op=op), _keys(in_), _keys(out))

    def recip(self, out, in_):
        o, i = out.ap, in_.ap
        self.add("dve", lambda e: e.reciprocal(out=o, in_=i), _keys(in_), _keys(out))

    def dma(self, q, out, in_):
        o, i = out.ap, in_.ap
        self.add(q, lambda e: e.dma_start(out=o, in_=i), _keys(in_), _keys(out), dma=True)

    def emit(self):
        nc = self.nc
        ops = self.ops
        n = len(ops)
        last_w = {}
        rd_eng = {}
        rd_dma = {}
        deps = [None] * n
        signal = [False] * n
        for i, (eng, fn, rd, wr, dma) in enumerate(ops):
            cand = []
            for k in rd:
                j = last_w.get(k)
                if j is not None:
                    cand.append((j, 0))
            for k in wr:
                j = last_w.get(k)
                if j is not None:
                    cand.append((j, 1))
                for j in rd_eng.get(k, {}).values():
                    cand.append((j, 2))
                for j in rd_dma.get(k, ()):
                    cand.append((j, 2))
            best = {}
            dl = set()
            for j, kind in cand:
                if j == i:
                    continue
                ej, _, _, _, dj = ops[j]
                if dj:
                    dl.add(j)
                    continue
                if (not dma) and ej == eng:
                    if eng == "pe" or kind != 0:
                        continue
                if best.get(ej, -1) < j:
                    best[ej] = j
            dd = set(best.values()) | dl
            deps[i] = dd
            for j in dd:
                signal[j] = True
            for k in rd:
                if dma:
                    rd_dma.setdefault(k, []).append(i)
                else:
                    rd_eng.setdefault(k, {})[eng] = i
            for k in wr:
                last_w[k] = i
                rd_eng[k] = {}
                rd_dma[k] = []
        cnt = {e: 0 for e in self.COMPUTE}
        sigval = [None] * n
        dcnt = {}
        for i, (eng, fn, rd, wr, dma) in enumerate(ops):
            if dma:
                q = dcnt.get(eng, 0)
                dcnt[eng] = q + 1
                sigval[i] = ("d", eng, q % NDSEM, 16 * (q // NDSEM + 1), q)
            elif signal[i]:
                cnt[eng] += 1
                sigval[i] = ("c", eng, cnt[eng])
        self.stats = dict(n=n, cnt=dict(cnt), dcnt=dict(dcnt))
        by_eng = {}
        for i, op in enumerate(ops):
            by_eng.setdefault(op[0], []).append(i)
        with ExitStack() as es:
            csems = {}
            for e in self.COMPUTE:
                ne = max(1, (cnt[e] + SEM_EPOCH - 1) // SEM_EPOCH)
                csems[e] = [es.enter_context(nc.semaphore("c_%s_%d" % (e, t))) for t in range(ne)]
            dsems = {}
            for e in dcnt:
                dsems[e] = [es.enter_context(nc.semaphore("d_%s_%d" % (e, t))) for t in range(NDSEM)]
            block = es.enter_context(nc.Block())

            def make(engname):
                mine = by_eng.get(engname, [])

                def body(e):
                    cw = {x: 0 for x in self.COMPUTE}
                    dw = {}
                    for i in mine:
                        eng, fn, rd, wr, dma = ops[i]
                        if dma:
                            _, _, slot, val, q = sigval[i]
                            if q >= NDSEM:
                                key = (eng, slot)
                                if dw.get(key, 0) < val - 16:
                                    e.wait_ge(dsems[eng][slot], val - 16)
                                    dw[key] = val - 16
                        for j in sorted(deps[i]):
                            sv = sigval[j]
                            if sv[0] == "c":
                                _, ej, c = sv
                                if cw[ej] >= c:
                                    continue
                                cw[ej] = c
                                e.wait_ge(csems[ej][(c - 1) // SEM_EPOCH], (c - 1) % SEM_EPOCH + 1)
                            else:
                                _, ej, slot, val, q = sv
                                key = (ej, slot)
                                if dw.get(key, 0) >= val:
                                    continue
                                dw[key] = val
                                e.wait_ge(dsems[ej][slot], val)
                        inst = fn(e)
                        sv = sigval[i]
                        if sv is not None:
                            if sv[0] == "c":
                                c = sv[2]
                                inst.then_inc(csems[eng][(c - 1) // SEM_EPOCH], 1)
                            else:
                                inst.then_inc(dsems[eng][sv[2]], 16)
                    if engname in dcnt:
                        tot = dcnt[engname]
                        for slot in range(NDSEM):
                            uses = (tot - slot + NDSEM - 1) // NDSEM if tot > slot else 0
                            if uses > 0 and dw.get((engname, slot), 0) < 16 * uses:
                                e.wait_ge(dsems[engname][slot], 16 * uses)
                return body

            block.tensor(make("pe"))
            block.scalar(make("act"))
            block.vector(make("dve"))
            block.gpsimd(make("pool"))
            block.sync(make("sp"))


class Arena:
    def __init__(self, nc, es, words):
        self.t = es.enter_context(nc.sbuf_tensor("arena", [128, words], F32))
        self.words = words
        self.off = 0
        self.uid = 0

    def alloc(self, name, free_shape, dtype=F32, nslots=None):
        nel = int(np.prod(free_shape))
        w = (nel + 1) // 2 if dtype == BF16 else nel
        w = (w + 7) // 8 * 8
        assert self.off + w <= self.words, ("SBUF arena overflow", name, self.off, w, self.words)
        ap = self.t[:, self.off:self.off + w]
        if dtype != F32:
            ap = ap.bitcast(dtype)
        ap = ap[:, 0:nel]
        if len(free_shape) == 2:
            ap = ap.rearrange("p (a b) -> p a b", a=free_shape[0])
        elif len(free_shape) == 3:
            ap = ap.rearrange("p (a b c) -> p a b c", a=free_shape[0], b=free_shape[1])
        self.off += w
        self.uid += 1
        return T(ap, "%s#%d" % (name, self.uid))

    def mark(self):
        return self.off

    def release(self, m):
        self.off = m


class Scratch:
    def __init__(self, ar, ngran):
        self.base = ar.off
        self.t = ar.t
        self.ngran = ngran
        ar.off += ngran * 512
        assert ar.off <= ar.words, ("scratch overflow", ar.off, ar.words)
        self.pos = 0
        self.peak = 0

    def reset(self):
        self.pos = 0

    def get(self, free_shape, dtype=F32):
        nel = int(np.prod(free_shape))
        w = (nel + 1) // 2 if dtype == BF16 else nel
        g = (w + 511) // 512
        assert self.pos + g <= self.ngran, ("scratch granules exhausted", self.pos, g, self.ngran)
        o = self.base + self.pos * 512
        ap = self.t[:, o:o + g * 512]
        if dtype != F32:
            ap = ap.bitcast(dtype)
        ap = ap[:, 0:nel]
        if len(free_shape) == 2:
            ap = ap.rearrange("p (a b) -> p a b", a=free_shape[0])
        elif len(free_shape) == 3:
            ap = ap.rearrange("p (a b c) -> p a b c", a=free_shape[0], b=free_shape[1])
        keys = tuple("scr%d" % (self.pos + i) for i in range(g))
        self.pos += g
        self.peak = max(self.peak, self.pos)
        return T(ap, keys)


class Ctx:
    pass


def make_ctx(nc, es, arena_words):
    cx = Ctx()
    cx.nc = nc
    cx.S = Sched(nc)
    cx.ar = Arena(nc, es, arena_words)
    cx.ps = [T(es.enter_context(nc.psum_tensor("psb%d" % i, [128, 512], F32))[:], "psb%d" % i) for i in range(8)]
    cx.S.bar_tile = cx.ar.alloc("bar", [8])
    cx.dbg = None
    return cx


def dram_in(nc, name, shape, dt=F32):
    return T(nc.dram_tensor(name, list(shape), dt, kind="ExternalInput").ap(), "dram:" + name)


def dram_out(nc, name, shape, dt=F32):
    return T(nc.dram_tensor(name, list(shape), dt, kind="ExternalOutput").ap(), "dram:" + name)


def load_consts(cx, cdram):
    S, ar = cx.S, cx.ar
    cx.ident = ar.alloc("ident", [128])
    cx.ones_f = ar.alloc("ones_f", [128])
    cx.ones_b = ar.alloc("ones_b", [128], BF16)
    S.dma("sp", cx.ident, cdram[:, 0:128])
    S.dma("sp", cx.ones_f, cdram[:, 128:256])
    S.copy("dve", cx.ones_b, cx.ones_f)
    cx.ident_b = ar.alloc("ident_b", [128], BF16)
    S.copy("dve", cx.ident_b, cx.ident)


def emit_rstd(cx, ss_ps, rs, n_feat, rows=128):
    cx.S.act(rs[0:rows], ss_ps[0:rows], AF.Sqrt, bias=float(n_feat * EPS))
    cx.S.recip(rs[0:rows], rs[0:rows])


def emit_ffn(cx, hT, NT, wg_d, wu_d, wd_d, g_pre, g_post, G=1):
    S, ar, ps = cx.S, cx.ar, cx.ps
    if NT % (G * TT) != 0:
        G = 1
    GW = G * TT
    m0 = ar.mark()
    xn = ar.alloc("xn", [NK, GW], BF16)
    hid = ar.alloc("hid", [NF, GW], BF16)
    yb = ar.alloc("ffy", [NK, GW])
    sq = [ar.alloc("sq%d" % i, [TT], BF16) for i in range(2)]
    rs = [ar.alloc("rs%d" % i, [TT]) for i in range(G)]
    sg = [ar.alloc("sg%d" % i, [TT]) for i in range(2)]
    tmp = [ar.alloc("ftmp%d" % i, [TT]) for i in range(2)]
    NWB = 3
    wgb = [ar.alloc("wgb%d" % i, [NK, 128], BF16) for i in range(NWB)]
    wub = [ar.alloc("wub%d" % i, [NK, 128], BF16) for i in range(NWB)]
    wdb = [ar.alloc("wdb%d" % i, [NF, 128], BF16) for i in range(2)]
    ss_ps = [ps[6], ps[7]]
    nsq = 0
    for g in range(NT // GW):
        for s in range(G):
            c0 = g * GW + s * TT
            for k in range(NK):
                b = sq[nsq % 2]
                nsq += 1
                S.act(b, hT[:, k, c0:c0 + TT], AF.Square)
                S.mm(ss_ps[s], cx.ones_b, b, start=(k == 0), stop=(k == NK - 1))
            emit_rstd(cx, ss_ps[s], rs[s], D)
            for k in range(NK):
                S.stt("dve" if k % 2 == 0 else "pool", xn[:, k, s * TT:(s + 1) * TT], hT[:, k, c0:c0 + TT],
                      g_pre[:, k:k + 1], rs[s], ALU.mult, ALU.mult)
        it = 0

        def ld_gu(f):
            S.dma("pool", wgb[f % NWB], wg_d[f].rearrange("p (k m) -> p k m", k=NK))
            S.dma("pool", wub[f % NWB], wu_d[f].rearrange("p (k m) -> p k m", k=NK))

        def ld_d(d):
            S.dma("pool", wdb[d % 2], wd_d[d].rearrange("p (f m) -> p f m", f=NF))

        for f in range(NWB - 1):
            ld_gu(f)
        for f in range(NF):
            wgt, wut = wgb[f % NWB], wub[f % NWB]
            if f + NWB - 1 < NF:
                ld_gu(f + NWB - 1)
            if f == NF - 2:
                ld_d(0)
            for s in range(G):
                gp, up = ps[(it % 2) * 2], ps[(it % 2) * 2 + 1]
                sgt = sg[it % 2]
                it += 1
                for k in range(NK):
                    S.mm(gp, wgt[:, k, :], xn[:, k, s * TT:(s + 1) * TT], start=(k == 0), stop=(k == NK - 1))
                for k in range(NK):
                    S.mm(up, wut[:, k, :], xn[:, k, s * TT:(s + 1) * TT], start=(k == 0), stop=(k == NK - 1))
                S.act(sgt, gp, AF.Silu)
                S.tt("dve", hid[:, f, s * TT:(s + 1) * TT], sgt, up, ALU.mult)
        it = 0
        for d in range(NK):
            wdt = wdb[d % 2]
            if d + 1 < NK:
                ld_d(d + 1)
            for s in range(G):
                yp = ps[4 + it % 2]
                it += 1
                for f in range(NF):
                    S.mm(yp, wdt[:, f, :], hid[:, f, s * TT:(s + 1) * TT], start=(f == 0), stop=(f == NF - 1))
                S.act(yb[:, d, s * TT:(s + 1) * TT], yp, AF.Copy)
                b = sq[nsq % 2]
                nsq += 1
                S.act(b, yp, AF.Square)
                S.mm(ss_ps[s], cx.ones_b, b, start=(d == 0), stop=(d == NK - 1))
        for s in range(G):
            c0 = g * GW + s * TT
            emit_rstd(cx, ss_ps[s], rs[s], D)
            for d in range(NK):
                t = tmp[d % 2]
                S.stt("dve", t, yb[:, d, s * TT:(s + 1) * TT], g_post[:, d:d + 1], rs[s], ALU.mult, ALU.mult)
                S.tt("pool", hT[:, d, c0:c0 + TT], hT[:, d, c0:c0 + TT], t, ALU.add)
    S.barrier()
    ar.release(m0)


def emit_norm_to(cx, hT, NT, g32, out_bf, rows_feat=D):
    S, ar, ps = cx.S, cx.ar, cx.ps
    m0 = ar.mark()
    sq = [ar.alloc("nsq%d" % i, [TT], BF16) for i in range(2)]
    rs = ar.alloc("nrs", [TT])
    n = 0
    for t in range(NT // TT):
        c0 = t * TT
        for k in range(NK):
            b = sq[n % 2]
            n += 1
            S.act(b, hT[:, k, c0:c0 + TT], AF.Square)
            S.mm(ps[6], cx.ones_b, b, start=(k == 0), stop=(k == NK - 1))
        emit_rstd(cx, ps[6], rs, D)
        for k in range(NK):
            S.stt("dve" if k % 2 == 0 else "pool", out_bf[:, k, c0:c0 + TT], hT[:, k, c0:c0 + TT],
                  g32[:, k:k + 1], rs, ALU.mult, ALU.mult)
    S.barrier()
    ar.release(m0)


def lay_vec(v):
    v = np.asarray(v, np.float32)
    return np.ascontiguousarray(v.reshape(-1, 128).T)


def lay_w_gu(w):
    w = np.asarray(w, np.float32)
    return np.ascontiguousarray(w.reshape(NK, 128, NF, 128).transpose(2, 1, 0, 3).reshape(NF, 128, NK * 128))


def lay_w_d(w):
    w = np.asarray(w, np.float32)
    return np.ascontiguousarray(w.reshape(NF, 128, NK, 128).transpose(2, 1, 0, 3).reshape(NK, 128, NF * 128))


def lay_act_T(x):
    x = np.asarray(x)
    nt = x.shape[0]
    return np.ascontiguousarray(x.reshape(nt, -1, 128).transpose(2, 1, 0))


def unlay_act_T(xT):
    p, n, nt = xT.shape
    return np.ascontiguousarray(xT.transpose(2, 1, 0).reshape(nt, n * 128))


def make_consts():
    c = np.zeros((128, 384), np.float32)
    c[:, 0:128] = np.eye(128, dtype=np.float32)
    c[:, 128:256] = 1.0
    return c


ARENA_WORDS = 53000


def load_vecs(cx, vd, ncol):
    v = cx.ar.alloc("vecs", [ncol])
    cx.S.dma("sp", v, vd)
    return v


def build_TA(NT):
    nc = bass.Bass("TRN2", target_bir_lowering=False)
    with ExitStack() as es:
        cx = make_ctx(nc, es, ARENA_WORDS)
        S, ar = cx.S, cx.ar
        h_in = dram_in(nc, "h_in", [128, NK, NT])
        cd = dram_in(nc, "consts", [128, 384])
        vd = dram_in(nc, "vec", [128, 24])
        wg = dram_in(nc, "wg", [NF, 128, NK * 128])
        wu = dram_in(nc, "wu", [NF, 128, NK * 128])
        wd = dram_in(nc, "wd", [NK, 128, NF * 128])
        h_out = dram_out(nc, "h_out", [128, NK, NT])
        u_out = dram_out(nc, "u_out", [128, NK, NT], BF16)
        load_consts(cx, cd)
        vec = load_vecs(cx, vd, 24)
        hT = ar.alloc("hT", [NK, NT])
        for k in range(NK):
            S.dma("sp", hT[:, k, :], h_in[:, k, :])
        g = ar.alloc("gains", [24])
        S.ts("dve", g[:, 0:8], vec[:, 0:8], 32.0, ALU.mult)
        S.ts("dve", g[:, 8:16], vec[:, 8:16], 16.0, ALU.mult)
        S.ts("dve", g[:, 16:24], vec[:, 16:24], 32.0, ALU.mult)
        emit_ffn(cx, hT, NT, wg, wu, wd, g[:, 0:8], g[:, 8:16])
        uT = ar.alloc("uT", [NK, NT], BF16)
        emit_norm_to(cx, hT, NT, g[:, 16:24], uT)
        for k in range(NK):
            S.dma("sp", h_out[:, k, :], hT[:, k, :])
            S.dma("sp", u_out[:, k, :], uT[:, k, :])
        S.emit()
    return nc


CB = dict(r=0, k=64, v=128, wl=192, al=256, gl=320, vl=448, cq=512, ckv=896, kr=1152, krs=1248,
          hq=1344, hf=1472, hi=1600, hg=1728)
NCOLB = 1856
SM = dict(w_up=0, a_up=64, g_up=128, vres_up=192, uq=256, uqs=544, ukv=832)
NSM = 1088
VB = dict(mu_r=0, mu_k=1, mu_v=2, mu_wl=3, mu_al=4, mu_gl=5, mu_vl=6, w0=7, a0=8, k_k=9, k_a=10, r_k=11,
          gn_g=12, gn_b=13, v0=14, qg=15, kvg=18, lb=20, hng=24)
NVB = 25
CC = dict(maskAT=0, maskN=512, I64=1024, triH=1536, triA=1664, invf=1792, sgn=1793, sel96=1794)
NCB = 1896
C0 = float(np.exp(-0.5))
A_GN_EPS = 64e-5
TWO_PI = float(2 * np.pi)
PI = float(np.pi)


def make_constsB():
    c = np.zeros((128, NCB), np.float32)
    m = np.zeros((128, 128), np.float32)
    il = np.arange(64)[:, None]
    tl = np.arange(64)[None, :]
    for half in range(2):
        m[half * 64:(half + 1) * 64, 0:64] = (il < tl)
        m[half * 64:(half + 1) * 64, 64:128] = (il <= tl)
    c[:, 0:512] = np.tile(m, (1, 4))
    mn = (np.arange(64)[:, None] > np.arange(64)[None, :]).astype(np.float32)
    c[0:64, 512:1024] = np.tile(mn, (1, 8))
    c[0:64, 1024:1536] = np.tile(np.eye(64, dtype=np.float32), (1, 8))
    th = (np.arange(32)[:, None] <= np.arange(32)[None, :]).astype(np.float32)
    c[:, 1536:1664] = np.tile(np.tile(th, (4, 1)), (1, 4))
    c[:, 1664:1792] = (np.arange(128)[:, None] <= np.arange(128)[None, :])
    invf = (10000.0 ** (-np.arange(0, 32, 2, dtype=np.float32) / 32)).astype(np.float32)
    c[64:80, CC["invf"]] = invf
    c[80:96, CC["invf"]] = invf
    c[64:80, CC["sgn"]] = -1.0
    c[80:96, CC["sgn"]] = 1.0
    c[0:96, CC["sel96"] + 96] = 1.0
    return c


import os
PHASES = os.environ.get("MIX_PHASES", "rmh")


def emit_cumsum(S, src, bufA, bufB, rows, nch, C):
    v = lambda t: t[rows].rearrange("p (c t) -> p c t", c=nch)
    cur, bufs, i, s = src, [bufA, bufB], 0, 1
    while s < C:
        nxt = bufs[i % 2]
        S.copy("pool", v(nxt)[:, :, 0:s], v(cur)[:, :, 0:s])
        S.tt("dve", v(nxt)[:, :, s:C], v(cur)[:, :, s:C], v(cur)[:, :, 0:C - s], ALU.add)
        cur = nxt
        i += 1
        s *= 2
    return cur, bufs[i % 2]


def emit_mixer(cx, layer, SL, u_d, wB_d, wsm_d, vec_d, pos_d, cB_d, vf_in_d, vf_out_d, y_d):
    S, ar, ps = cx.S, cx.ar, cx.ps
    L0 = (layer == 0)
    ntile = SL // TT
    nblk = SL // 128
    m_all = ar.mark()
    SCALE = float(96 ** -0.5)
    wB = ar.alloc("wB", [NK, NCOLB], BF16)
    for k in range(NK):
        S.dma("pool", wB[:, k, :], wB_d[:, k, :])
    wsm = ar.alloc("wsm", [NSM])
    S.dma("sp", wsm, wsm_d)
    wuq = ar.alloc("wuq", [3, 96], BF16)
    wuqs = ar.alloc("wuqs", [3, 96], BF16)
    wukv = ar.alloc("wukv", [2, 128], BF16)
    S.copy("dve", wuq, wsm[:, SM["uq"]:SM["uq"] + 288].rearrange("p (a b) -> p a b", a=3))
    S.copy("dve", wuqs, wsm[:, SM["uqs"]:SM["uqs"] + 288].rearrange("p (a b) -> p a b", a=3))
    S.copy("dve", wukv, wsm[:, SM["ukv"]:SM["ukv"] + 256].rearrange("p (a b) -> p a b", a=2))
    w_up = wsm[0:64, SM["w_up"]:SM["w_up"] + 64]
    a_up = wsm[0:64, SM["a_up"]:SM["a_up"] + 64]
    g_up = wsm[:, SM["g_up"]:SM["g_up"] + 64]
    vres_up = wsm[0:32, SM["vres_up"]:SM["vres_up"] + 64]
    vec = ar.alloc("vecB", [NVB])
    S.dma("sp", vec, vec_d)
    cB = ar.alloc("cB", [NCB])
    S.dma("sp", cB, cB_d)
    triA = ar.alloc("triA", [128], BF16)
    S.copy("dve", triA, cB[:, CC["triA"]:CC["triA"] + 128])
    sel96 = ar.alloc("sel96", [97], BF16)
    S.copy("dve", sel96, cB[:, CC["sel96"]:CC["sel96"] + 97])
    maskAT = cB[:, CC["maskAT"]:CC["maskAT"] + 512]
    maskN = cB[0:64, CC["maskN"]:CC["maskN"] + 512]
    I64 = cB[0:64, CC["I64"]:CC["I64"] + 512]
    triH = cB[:, CC["triH"]:CC["triH"] + 128].rearrange("p (b t) -> p b t", b=4)
    invf = cB[:, CC["invf"]:CC["invf"] + 1]
    sgn = cB[:, CC["sgn"]:CC["sgn"] + 1]
    ones64 = cx.ones_f[0:64, 0:64]
    id64 = cx.ident_b[0:64, 0:64]

    def V(name, rows=128):
        return vec[0:rows, VB[name]:VB[name] + 1]

    dv = ar.alloc("dvec", [16])
    S.ts("dve", dv[:, 0:7], vec[:, 0:7], -1.0, ALU.mult, 1.0, ALU.add)
    S.ts("dve", dv[:, 7:8], vec[:, VB["k_a"]:VB["k_a"] + 1], -1.0, ALU.mult, 1.0, ALU.add)
    S.ts("dve", dv[:, 8:11], vec[:, VB["qg"]:VB["qg"] + 3], float(np.sqrt(384.0)), ALU.mult)
    S.ts("dve", dv[:, 11:13], vec[:, VB["kvg"]:VB["kvg"] + 2], 16.0, ALU.mult)
    S.ts("dve", dv[:, 15:16], vec[:, VB["hng"]:VB["hng"] + 1], float(np.sqrt(128.0)), ALU.mult)
    if L0:
        S.memset("dve", dv[:, 13:14], 0.0)
    else:
        le = ar.alloc("lbe", [8])
        S.act(le[:, 0:4], vec[:, VB["lb"]:VB["lb"] + 4], AF.Exp)
        S.reduce("dve", le[:, 4:5], le[:, 0:4], ALU.add)
        S.recip(le[:, 4:5], le[:, 4:5])
        S.reduce("dve", le[:, 5:6], le[:, 1:layer + 1], ALU.add)
        S.tt("dve", dv[:, 13:14], le[:, 5:6], le[:, 4:5], ALU.mult)
    S.ts("dve", dv[:, 14:15], dv[:, 13:14], -1.0, ALU.mult, 1.0, ALU.add)
    mpi = ar.alloc("mpi", [8])
    S.memset("dve", mpi, -PI)

    KT = ar.alloc("KT", [SL], BF16)
    Vaug = ar.alloc("Vaug", [nblk, 65], BF16)
    S.memset("pool", KT[96:97, :], 1.0)
    S.memset("pool", Vaug[:, :, 64:65], 1.0)
    kmax2 = ar.alloc("kmax2", [8])
    S.memset("dve", kmax2[96:97, :], 0.0)
    SV = ar.alloc("SV", [9, 64], BF16)
    SVf = ar.alloc("SVf", [9, 64])
    S.memset("dve", SV[0:64, 0, :], 0.0)
    S.memset("dve", SVf[0:64, 0, :], 0.0)
    Sh = ar.alloc("Sh", [17, 128])
    S.memset("pool", Sh[:, 0, :], 0.0)
    shifted = [("r", 64), ("k", 64), ("v", 64), ("wl", 64), ("al", 64), ("gl", 128)] + ([] if L0 else [("vl", 32)])
    halo = ar.alloc("halo", [8])
    S.memset("dve", halo, 0.0)
    ut = [ar.alloc("ut%d" % i, [NK, TT], BF16) for i in range(2)]
    ya_o = ar.alloc("ya_o", [TT], BF16)
    yb_o = ar.alloc("yb_o", [TT], BF16)
    yc_o = ar.alloc("yc_o", [TT], BF16)
    gC = ar.alloc("gC", [16])
    gCh = ar.alloc("gCh", [16])
    QT = ar.alloc("QT", [TT], BF16)
    Pb = [ar.alloc("Pb%d" % i, [TT], BF16) for i in range(3)]
    rd = ar.alloc("rd", [TT])
    ob = ar.alloc("ob", [TT])
    sc = Scratch(ar, (ar.words - ar.off) // 512)

    rot = [0]

    def bank():
        b = ps[rot[0] % 3]
        rot[0] += 1
        return b

    def proj(off, M, utile):
        pb = bank()
        for k in range(NK):
            S.mm(pb[0:M, :], wB[:, k, off:off + M], utile[:, k, :], start=(k == 0), stop=(k == NK - 1))
        return pb

    S.dma("sp", ut[0], u_d[:, :, 0:TT])
    for it in range(ntile):
        c0 = it * TT
        u_t = ut[it % 2]
        S.mute = False
        if it + 1 < ntile:
            S.dma("sp", ut[(it + 1) % 2], u_d[:, :, c0 + TT:c0 + 2 * TT])
        R = slice(0, 64)
        RR = slice(64, 96)

        def do_rwkv():
            S.mute = "r" not in PHASES
            sc.reset()
            A = lambda n=TT, dt=F32: sc.get([n], dt)
            rws = [A(), A()]
            tmpm = A()
            xr, xk, xwl, xal, xgl = A(), A(), A(), A(), A()
            xvf = A(TT + 64)
            xv = xvf[:, 64:64 + TT]
            xvl = A()
            cs, a_t, g_t, kk, bb, bon = A(), A(), A(), A(), A(), A()
            E1, E2, E3, E4 = rws[0], rws[1], tmpm, A()
            t1, t2 = A(), A()
            WL = sc.get([8, 64], BF16)
            RA = sc.get([8, 64], BF16)
            ARB = sc.get([8, 64], BF16)
            BK = sc.get([8, 2, 64], BF16)
            BKh = sc.get([8, 2, 64], BF16)
            xvb = sc.get([TT + 64], BF16)
            Nm = [sc.get([8, 64], BF16) for i in range(2)]
            Mm = [sc.get([8, 64], BF16) for i in range(2)]
            Pm = [sc.get([8, 64]) for i in range(2)]
            Pmb = [sc.get([8, 64], BF16) for i in range(2)]
            BKhT = sc.get([8, 64], BF16)
            UV = sc.get([8, 64], BF16)
            Wsb = A(64, BF16)
            yT = xal
            vft = xwl
            mixed = {"r": xr, "k": xk, "v": xv, "wl": xwl, "al": xal, "gl": xgl, "vl": xvl}
            for gi, (nm, M) in enumerate(shifted):
                pb = proj(CB[nm], M, u_t)
                rw = rws[gi % 2]
                S.copy("act", rw[0:M], pb[0:M, :])
                mu = vec[0:M, gi:gi + 1]
                omu = dv[0:M, gi:gi + 1]
                S.ts("dve", tmpm[0:M, 1:TT], rw[0:M, 0:TT - 1], mu, ALU.mult)
                S.ts("dve", tmpm[0:M, 0:1], halo[0:M, gi:gi + 1], mu, ALU.mult)
                S.stt("dve", mixed[nm][0:M], rw[0:M], omu, tmpm[0:M], ALU.mult, ALU.add)
                S.copy("pool", halo[0:M, gi:gi + 1], rw[0:M, TT - 1:TT])
            S.memset("pool", xvf[0:64, 0:64], 0.0)
            R = slice(0, 64)
            S.act(xwl[R], xwl[R], AF.Tanh)
            pb = bank()
            S.mm(pb[R, :], w_up, xwl[R])
            S.act(t1[R], pb[R, :], AF.Sigmoid, bias=V("w0", 64))
            cs, t2 = emit_cumsum(S, t1, cs, t2, R, 8, 64)
            S.tt("dve", t2[R], cs[R], t1[R], ALU.subtract)
            S.act(E1[R], cs[R], AF.Exp, scale=-C0)
            S.act(E2[R], cs[R], AF.Exp, scale=C0)
            S.act(E3[R], t2[R], AF.Exp, scale=-C0)
            cs3 = cs[R].rearrange("p (c t) -> p c t", c=8)
            S.tt("dve", t2[R].rearrange("p (c t) -> p c t", c=8), cs3[:, :, 63:64].bcast([64, 8, 64]), cs3, ALU.subtract)
            S.act(E4[R], t2[R], AF.Exp, scale=-C0)
            S.act(gC[R, 0:8], cs3[:, :, 63:64].rearrange("p c o -> p (c o)"), AF.Exp, scale=-C0)
            pb = bank()
            S.mm(pb[R, :], a_up, xal[R])
            S.act(a_t[R], pb[R, :], AF.Sigmoid, bias=V("a0", 64))
            S.act(xgl, xgl, AF.Sigmoid)
            pb = bank()
            S.mm(pb[R, :], g_up, xgl)
            S.copy("act", g_t[R], pb[R, :])
            if L0:
                S.dma("sp", vf_out_d[:, c0:c0 + TT], xv[R])
            else:
                S.dma("sp", vft[R], vf_in_d[:, c0:c0 + TT])
                pb = bank()
                S.mm(pb[R, :], vres_up, xvl[0:32])
                S.act(t1[R], pb[R, :], AF.Sigmoid, bias=V("v0", 64))
                S.tt("dve", vft[R], vft[R], xv[R], ALU.subtract)
                S.tt("dve", vft[R], vft[R], t1[R], ALU.mult)
                S.tt("dve", xv[R], xv[R], vft[R], ALU.add)
            S.ts("dve", kk[R], xk[R], V("k_k", 64), ALU.mult)
            S.tt("dve", t1[R], kk[R], kk[R], ALU.mult)
            pb = bank()
            S.mm(pb[R, :], ones64, t1[R])
            S.act(t1[R], pb[R, :], AF.Sqrt)
            S.ts("dve", t1[R], t1[R], 1e-12, ALU.max)
            S.recip(t1[R], t1[R])
            S.tt("dve", kk[R], kk[R], t1[R], ALU.mult)
            S.ts("dve", t1[R], a_t[R], V("k_a", 64), ALU.mult, dv[R, 7:8], ALU.add)
            S.tt("dve", xk[R], xk[R], t1[R], ALU.mult)
            S.stt("dve", t1[R], xr[R], V("r_k", 64), xk[R], ALU.mult, ALU.mult)
            pb = bank()
            S.mm(pb[R, :], ones64, t1[R])
            S.tt("dve", bon[R], pb[R, :], xv[R], ALU.mult)
            S.tt("dve", bb[R], kk[R], a_t[R], ALU.mult)
            v3 = lambda t: t[R].rearrange("p (c t) -> p c t", c=8)
            S.stt("dve", WL[R], v3(kk), -1.0, v3(E3), ALU.mult, ALU.mult)
            S.tt("dve", RA[R], v3(xr), v3(E1), ALU.mult)
            S.copy("pool", xvb[R], xvf[R])
            S.tt("dve", BK[R, :, 0, :], v3(bb), v3(E2), ALU.mult)
            S.tt("dve", BK[R, :, 1, :], v3(xk), v3(E2), ALU.mult)
            S.tt("dve", BKh[R, :, 0, :], v3(bb), v3(E4), ALU.mult)
            S.tt("dve", BKh[R, :, 1, :], v3(xk), v3(E4), ALU.mult)
            f2 = lambda t, c: t[R, c].rearrange("p a b -> p (a b)")
            m4 = maskAT.rearrange("p (c h t) -> p c h t", c=4, h=2)
            for half in range(2):
                pb = bank()
                for cc in range(4):
                    c = half * 4 + cc
                    S.mm(pb[:, cc * 128:cc * 128 + 64], f2(BK, c), WL[R, c, :])
                    S.mm(pb[:, cc * 128 + 64:(cc + 1) * 128], f2(BK, c), RA[R, c, :])
                p4 = pb.rearrange("p (c h t) -> p c h t", c=4, h=2)
                cs_ = slice(half * 4, half * 4 + 4)
                S.tt("dve", Mm[0][R, cs_, :], p4[R, :, 0, :], m4[R, :, 0, :], ALU.mult)
                S.tt("dve", ARB[R, cs_, :], p4[R, :, 1, :], m4[R, :, 1, :], ALU.mult)
                S.tt("dve", WL[64:128, cs_, :], p4[64:128, :, 0, :], m4[64:128, :, 0, :], ALU.mult)
                S.tt("dve", RA[64:128, cs_, :], p4[64:128, :, 1, :], m4[64:128, :, 1, :], ALU.mult)
            pb = bank()
            for c in range(8):
                S.mm(pb[R, c * 64:(c + 1) * 64], WL[R, c, :], BK[R, c, 0, :])
            S.tt("dve", Nm[0][R].rearrange("p a b -> p (a b)"), pb[R, :], maskN, ALU.mult)
            S.tt("pool", Pm[0][R].rearrange("p a b -> p (a b)"), Mm[0][R].rearrange("p a b -> p (a b)"), I64, ALU.add)
            S.copy("pool", Pmb[0][R], Pm[0][R])
            cur = 0
            for rnd in range(5):
                Mc, Nc, Pc = Mm[cur], Nm[cur], Pm[cur]
                Mn, Nn, Pn = Mm[1 - cur], Nm[1 - cur], Pm[1 - cur]
                Pcb, Pnb = Pmb[cur], Pmb[1 - cur]
                pbm = bank()
                pbn = bank()
                for c in range(8):
                    S.mm(pbm[R, c * 64:(c + 1) * 64], Nc[R, c, :], Mc[R, c, :])
                for c in range(8):
                    S.mm(pbn[R, c * 64:(c + 1) * 64], Mc[R, c, :], Nc[R, c, :])
                S.copy("act", Mn[R].rearrange("p a b -> p (a b)"), pbm[R, :])
                S.copy("dve", Nn[R].rearrange("p a b -> p (a b)"), pbn[R, :])
                pbp = bank()
                for c in range(8):
                    S.mm(pbp[R, c * 64:(c + 1) * 64], Nn[R, c, :], Pcb[R, c, :])
                S.tt("dve", Pn[R].rearrange("p a b -> p (a b)"), pbp[R, :], Pc[R].rearrange("p a b -> p (a b)"), ALU.add)
                S.copy("pool", Pnb[R], Pn[R])
                cur = 1 - cur
            Pf = Pmb[cur]
            pb = bank()
            for c in range(8):
                S.transpose(pb[:, c * 64:(c + 1) * 64], f2(BKh, c), id64)
            S.copy("act", BKhT.rearrange("p a b -> p (a b)"), pb)
            pb = bank()
            for c in range(8):
                S.transpose(pb[:, c * 64:(c + 1) * 64], xvb[R, c * 64:c * 64 + 128], id64)
            S.copy("dve", UV[64:128].rearrange("p a b -> p (a b)"), pb[64:128, :])
            S.copy("dve", SV[64:128, 0:8, :].rearrange("p a b -> p (a b)"), pb[64:128, :])
            ypb = ps[4]
            for c in range(8):
                pw = bank()
                S.mm(pw[R, 0:64], WL[:, c, :], SV[:, c, :])
                S.copy("act", Wsb[R], pw[R, 0:64])
                pu = bank()
                S.mm(pu[R, 0:64], Pf[R, c, :], Wsb[R])
                S.copy("dve", UV[R, c, :], pu[R, 0:64])
                S.mm(ypb[R, c * 64:(c + 1) * 64], SV[:, c, :], RA[:, c, :], start=True, stop=False)
                S.mm(ypb[R, c * 64:(c + 1) * 64], UV[R, c, :], ARB[R, c, :], start=False, stop=True)
                pn = bank()
                S.mm(pn[R, 0:64], BKhT[:, c, :], UV[:, c, :])
                S.stt("dve", SVf[R, c + 1, :], SVf[R, c, :], gC[R, c:c + 1], pn[R, 0:64], ALU.mult, ALU.add)
                S.copy("act", SV[R, c + 1, :], SVf[R, c + 1, :])
            S.copy("pool", SV[R, 0, :], SV[R, 8, :])
            S.copy("pool", SVf[R, 0, :], SVf[R, 8, :])
            S.copy("act", yT[R], ypb[R, :])
            pb = bank()
            S.mm(pb[R, :], ones64, yT[R])
            S.stt("dve", yT[R], pb[R, :], -1.0 / 64, yT[R], ALU.mult, ALU.add)
            S.act(t1[R], yT[R], AF.Square)
            pb = bank()
            S.mm(pb[R, :], ones64, t1[R])
            S.act(t1[R], pb[R, :], AF.Sqrt, bias=A_GN_EPS, scale=1.0 / 64)
            S.recip(t1[R], t1[R])
            S.tt("dve", yT[R], yT[R], t1[R], ALU.mult)
            S.ts("dve", yT[R], yT[R], V("gn_g", 64), ALU.mult, V("gn_b", 64), ALU.add)
            S.tt("dve", yT[R], yT[R], bon[R], ALU.add)
            S.tt("dve", ya_o[R], yT[R], g_t[R], ALU.mult)
            S.dma("sp", y_d[0:64, c0:c0 + TT], ya_o[R])

        def do_mla_prep():
            S.mute = "m" not in PHASES
            sc.reset()
            A = lambda n=TT, dt=F32: sc.get([n], dt)
            t1, t2 = A(), A()
            cq_sb = sc.get([3, TT])
            sqb = [A(TT, BF16) for i in range(2)]
            rsq = A()
            cqn = sc.get([3, TT], BF16)
            ckvn = sc.get([2, TT], BF16)
            posi = sc.get([TT], I32)
            cos2, sin2 = A(), A()
            qr = A()
            qsq = A(TT, BF16)
            ksq = A(TT, BF16)
            rdm = A()
            def latent_norm(off, nch, dst, gcol, nfeat):
                ssb = ps[4]
                for j in range(nch):
                    pb = proj(off + j * 128, 128, u_t)
                    S.copy("act", cq_sb[:, j, :], pb)
                    b = sqb[j % 2]
                    S.act(b, pb, AF.Square)
                    S.mm(ssb, cx.ones_b, b, start=(j == 0), stop=(j == nch - 1))
                S.act(rsq, ssb, AF.Sqrt, bias=float(nfeat * EPS))
                S.recip(rsq, rsq)
                for j in range(nch):
                    S.stt("dve", dst[:, j, :], cq_sb[:, j, :], dv[:, gcol + j:gcol + j + 1], rsq, ALU.mult, ALU.mult)
            latent_norm(CB["cq"], 3, cqn, 8, 384)
            pq = bank()
            for j in range(3):
                S.mm(pq[0:96, :], wuq[:, j, :], cqn[:, j, :], start=(j == 0), stop=(j == 2))
            pqs = bank()
            for j in range(3):
                S.mm(pqs[0:96, :], wuqs[:, j, :], cqn[:, j, :], start=(j == 0), stop=(j == 2))
            RR = slice(64, 96)
            S.dma("sp", posi[RR], pos_d[:, c0:c0 + TT])
            S.copy("dve", t1[RR], posi[RR])
            S.ts("dve", t1[RR], t1[RR], invf[RR], ALU.mult)
            def sincos(dst, shift):
                S.ts("dve", t2[RR], t1[RR], 1.0 / TWO_PI, ALU.mult, shift, ALU.add)
                S.copy("dve", posi[RR], t2[RR])
                S.copy("dve", rdm[RR], posi[RR])
                S.tt("dve", t2[RR], t2[RR], rdm[RR], ALU.subtract)
                S.ts("dve", rdm[RR], t2[RR], 0.0, ALU.is_lt)
                S.tt("dve", t2[RR], t2[RR], rdm[RR], ALU.add)
                S.act(dst[RR], t2[RR], AF.Sin, bias=mpi[RR, 0:1], scale=TWO_PI)
            sincos(sin2, 0.5)
            S.ts("dve", sin2[RR], sin2[RR], sgn[RR], ALU.mult)
            sincos(cos2, 0.75)
            S.act(QT[0:64], pq[0:64, :], AF.Copy, scale=SCALE)
            S.tt("dve", qr[RR], pq[RR, :], cos2[RR], ALU.mult)
            S.tt("dve", t1[RR], pqs[RR, :], sin2[RR], ALU.mult)
            S.tt("dve", qr[RR], qr[RR], t1[RR], ALU.add)
            S.act(QT[RR], qr[RR], AF.Copy, scale=SCALE)
            S.act(qsq[0:64], pq[0:64, :], AF.Square, scale=SCALE)
            S.act(qsq[RR], qr[RR], AF.Square, scale=SCALE)
            latent_norm(CB["ckv"], 2, ckvn, 11, 256)
            pkv = bank()
            for j in range(2):
                S.mm(pkv, wukv[:, j, :], ckvn[:, j, :], start=(j == 0), stop=(j == 1))
            S.copy("act", KT[0:64, c0:c0 + TT], pkv[0:64, :])
            pkr = proj(CB["kr"], 96, u_t)
            pkrs = proj(CB["krs"], 96, u_t)
            S.tt("dve", t1[RR], pkr[RR, :], cos2[RR], ALU.mult)
            S.tt("dve", t2[RR], pkrs[RR, :], sin2[RR], ALU.mult)
            S.tt("dve", KT[RR, c0:c0 + TT], t1[RR], t2[RR], ALU.add)
            S.act(ksq[0:96], KT[0:96, c0:c0 + TT], AF.Square)
            pb = bank()
            S.mm(pb[0:97, :], sel96[0:96, :], ksq[0:96])
            S.reduce("dve", kmax2[96:97, 1:2], pb[96:97, :], ALU.max)
            S.tt("dve", kmax2[96:97, 0:1], kmax2[96:97, 0:1], kmax2[96:97, 1:2], ALU.max)
            pb = bank()
            S.mm(pb[0:97, :], sel96[0:96, :], qsq[0:96])
            S.ts("dve", t1[96:97], pb[96:97, :], kmax2[96:97, 0:1], ALU.mult)
            S.act(t1[96:97], t1[96:97], AF.Sqrt)
            S.ts("dve", QT[96:97], t1[96:97], -1.0, ALU.mult)
            pb = bank()
            for blk in range(4):
                for j in range(2):
                    S.mm(pb[:, blk * 64:(blk + 1) * 64], ckvn[:, j, blk * 128:(blk + 1) * 128], wukv[:, j, 64:128],
                         start=(j == 0), stop=(j == 1))
            S.copy("act", Vaug[:, 4 * it:4 * it + 4, 0:64], pb[:, 0:256].rearrange("p (a b) -> p a b", a=4))
        def do_attn():
            ob_ps = ps[7]
            nkb = 4 * it + 4
            def qlo_of(kb):
                return max(kb - 4 * it, 0) * 128

            for kb in range(nkb + 2):
                if kb < nkb:
                    d = kb - 4 * it
                    qlo = qlo_of(kb)
                    sp_ = ps[(3, 5, 6)[kb % 3]]
                    pt = Pb[kb % 3]
                    S.mm(sp_[:, qlo:TT], KT[0:97, kb * 128:(kb + 1) * 128], QT[0:97, qlo:TT])
                    S.act(pt[:, qlo:TT], sp_[:, qlo:TT], AF.Exp)
                    if d >= 0:
                        S.tt("pool", pt[:, qlo:qlo + 128], pt[:, qlo:qlo + 128], triA, ALU.mult)
                if kb >= 2:
                    kp = kb - 2
                    qlo = qlo_of(kp)
                    S.mm(ob_ps[0:65, qlo:TT], Vaug[:, kp, :], Pb[kp % 3][:, qlo:TT], start=(kp == 0), stop=(kp == nkb - 1))
            S.recip(rd[0:1], ob_ps[64:65, :])
            S.copy("act", ob[R], ob_ps[R, :])
            pb = ps[5]
            S.mm(pb[R, :], cx.ones_f[0:1, 0:64], rd[0:1])
            S.tt("dve", yb_o[R], ob[R], pb[R, :], ALU.mult)
            S.dma("sp", y_d[64:128, c0:c0 + TT], yb_o[R])

        def do_hgrn():
            S.mute = "h" not in PHASES
            sc.reset()
            A = lambda n=TT, dt=F32: sc.get([n], dt)
            t1, t2 = A(), A()
            qh, kx, clh, qb, gh = A(), A(), A(), A(), A()
            qtl, ktl, khat = A(TT, BF16), A(TT, BF16), A(TT, BF16)
            vTb = A(TT, BF16)
            Vh = sc.get([16, 128], BF16)
            KhT = sc.get([16, 128], BF16)
            KV = sc.get([16, 128])
            ATh = sc.get([16, 32], BF16)
            oh = A()
            sqb = [A(TT, BF16)]
            rsq = A()
            pb = proj(CB["hq"], 128, u_t)
            S.act(qh, pb, AF.Silu)
            pb = proj(CB["hf"], 128, u_t)
            S.act(t1, pb, AF.Sigmoid)
            S.ts("dve", t1, t1, dv[:, 14:15], ALU.mult, dv[:, 13:14], ALU.add)
            S.ts("dve", kx, t1, -1.0, ALU.mult, 1.0, ALU.add)
            S.ts("dve", t1, t1, 1e-6, ALU.max)
            S.act(t1, t1, AF.Ln)
            clh, t2 = emit_cumsum(S, t1, clh, t2, slice(0, 128), 16, 32)
            c16 = lambda t: t.rearrange("p (c t) -> p c t", c=16)
            S.act(t2, clh, AF.Exp)
            S.tt("dve", qb, qh, t2, ALU.mult)
            S.tt("dve", c16(t1), c16(clh), c16(clh)[:, :, 15:16].bcast([128, 16, 32]), ALU.subtract)
            S.act(t2, t1, AF.Exp)
            S.tt("dve", qtl, qh, t2, ALU.mult)
            S.act(t2, t1, AF.Exp, scale=-1.0)
            S.tt("dve", ktl, kx, t2, ALU.mult)
            S.tt("dve", c16(t1), c16(clh), c16(clh)[:, :, 31:32].bcast([128, 16, 32]), ALU.subtract)
            S.act(t2, t1, AF.Exp, scale=-1.0)
            S.tt("dve", khat, kx, t2, ALU.mult)
            S.act(gCh[:, 0:16], c16(clh)[:, :, 31:32].rearrange("p c o -> p (c o)"), AF.Exp)
            Q = slice(0, 32)
            pb = proj(CB["hi"], 128, u_t)
            S.copy("act", vTb, pb)
            for blk in range(4):
                pb = bank()
                for j in range(4):
                    c = blk * 4 + j
                    S.transpose(pb[Q, j * 128:(j + 1) * 128], vTb[:, c * 32:(c + 1) * 32], cx.ident_b)
                S.copy("act", Vh[Q, blk * 4:blk * 4 + 4, :].rearrange("p a b -> p (a b)"), pb[Q, :])
            pb = proj(CB["hg"], 128, u_t)
            S.act(gh, pb, AF.Silu)
            for blk in range(4):
                pb = bank()
                for j in range(4):
                    c = blk * 4 + j
                    S.transpose(pb[Q, j * 128:(j + 1) * 128], khat[:, c * 32:(c + 1) * 32], cx.ident_b)
                S.copy("dve", KhT[Q, blk * 4:blk * 4 + 4, :].rearrange("p a b -> p (a b)"), pb[Q, :])
            for blk in range(4):
                pb = bank()
                for j in range(4):
                    c = blk * 4 + j
                    S.mm(pb[:, j * 128:(j + 1) * 128], KhT[Q, c, :], Vh[Q, c, :])
                S.copy("act" if blk % 2 == 0 else "dve", KV[:, 4 * blk:4 * blk + 4, :].rearrange("p a b -> p (a b)"), pb)
            pb = bank()
            for c in range(16):
                S.mm(pb[Q, c * 32:(c + 1) * 32], ktl[:, c * 32:(c + 1) * 32], qtl[:, c * 32:(c + 1) * 32])
            S.tt("dve", ATh[Q], pb[Q, :].rearrange("p (c t) -> p c t", c=16),
                 triH[Q, 0:1, :].bcast([32, 16, 32]), ALU.mult)
            for c in range(16):
                S.stt("dve", Sh[:, c + 1, :], Sh[:, c, :], gCh[:, c:c + 1], KV[:, c, :], ALU.mult, ALU.add)
            ohp = ps[4]
            for c in range(16):
                S.mm(ohp[:, c * 32:(c + 1) * 32], Sh[:, c, :], qb[:, c * 32:(c + 1) * 32], start=True, stop=False)
                S.mm(ohp[:, c * 32:(c + 1) * 32], Vh[Q, c, :], ATh[Q, c, :], start=False, stop=True)
            S.copy("pool", Sh[:, 0, :], Sh[:, 16, :])
            S.copy("act", oh, ohp)
            S.act(sqb[0], ohp, AF.Square)
            pb = bank()
            S.mm(pb, cx.ones_b, sqb[0])
            S.act(rsq, pb, AF.Sqrt, bias=float(128 * EPS))
            S.recip(rsq, rsq)
            S.stt("dve", oh, oh, dv[:, 15:16], rsq, ALU.mult, ALU.mult)
            S.tt("dve", yc_o, oh, gh, ALU.mult)
            S.dma("sp", y_d[128:192, c0:c0 + TT], yc_o[0:64])
        do_mla_prep()
        S.mute = False
        main_ops, main_lines = S.ops, S.lines
        S.ops, S.lines = [], []
        do_rwkv()
        do_hgrn()
        S.mute = False
        a_ops, a_lines = S.ops, S.lines
        S.ops, S.lines = [], []
        S.mute = "m" not in PHASES
        do_attn()
        S.mute = False
        b_ops, b_lines = S.ops, S.lines
        ia = ib = 0
        na, nb = len(a_ops), len(b_ops)
        while ia < na or ib < nb:
            if ib >= nb or (ia < na and ia * nb <= ib * na):
                main_ops.append(a_ops[ia]); main_lines.append(a_lines[ia]); ia += 1
            else:
                main_ops.append(b_ops[ib]); main_lines.append(b_lines[ib]); ib += 1
        S.ops, S.lines = main_ops, main_lines
    S.mute = False
    if cx.dbg is not None:
        for nm, t in (("QT", QT), ("cos2", cos2), ("sin2", sin2), ("rd", rd), ("ob", ob), ("qr", qr)):
            cx.dbg[nm] = t
        cx.dbg["KT"] = KT
        cx.dbg["Vaug"] = Vaug
    S.barrier()
    cx.scr_peak = sc.peak
    ar.release(m_all)


DEBUG_B = bool(int(os.environ.get("DEBUG_B", "0")))


def build_B(layer, SL):
    nc = bass.Bass("TRN2", target_bir_lowering=False)
    with ExitStack() as es:
        cx = make_ctx(nc, es, ARENA_WORDS)
        u_d = dram_in(nc, "u_full", [128, NK, SL], BF16)
        cd = dram_in(nc, "consts", [128, 384])
        wB_d = dram_in(nc, "wB", [128, NK, NCOLB])
        wsm_d = dram_in(nc, "wsm", [128, NSM])
        vec_d = dram_in(nc, "vecB", [128, NVB])
        pos_d = dram_in(nc, "pos", [32, SL], I32)
        cB_d = dram_in(nc, "cB", [128, NCB])
        y_d = dram_out(nc, "y_out", [192, SL], BF16)
        if layer == 0:
            vf_in, vf_out = None, dram_out(nc, "vf_out", [64, SL])
        else:
            vf_in, vf_out = dram_in(nc, "vf_in", [64, SL]), None
        load_consts(cx, cd)
        if DEBUG_B:
            cx.dbg = {}
        emit_mixer(cx, layer, SL, u_d, wB_d, wsm_d, vec_d, pos_d, cB_d, vf_in, vf_out, y_d)
        if DEBUG_B:
            for nm, t in cx.dbg.items():
                shp = [128] + list(t.ap.shape[1:])
                od = dram_out(nc, "dbg_" + nm, shp, t.ap.dtype)
                cx.S.dma("sp", od, t)
        cx.S.emit()
        print("B stats", cx.S.stats, "scratch peak", cx.scr_peak)
    return nc


def lay_rows(w, nch):
    w = np.asarray(w, np.float32)
    return np.ascontiguousarray(w.reshape(nch, 128, -1).transpose(1, 0, 2))


def prep_B_core(inp, l, c, SL):
    f32 = np.float32
    w_in = np.asarray(inp["w_in"][l], f32)
    hd, hf_ = c // 2, c % 2
    perm = np.concatenate([np.arange(hf_ * 64, hf_ * 64 + 64), np.arange((1 - hf_) * 64, (1 - hf_) * 64 + 64)])
    W = np.zeros((D, NCOLB), f32)

    def put(name, cols):
        W[:, CB[name]:CB[name] + cols.shape[1]] = cols
    put("r", w_in[:, c * 64:(c + 1) * 64])
    put("k", w_in[:, 512 + c * 64:512 + (c + 1) * 64])
    put("v", w_in[:, 1024 + c * 64:1024 + (c + 1) * 64])
    put("wl", w_in[:, 1536:1600])
    put("al", w_in[:, 1600:1664])
    put("gl", w_in[:, 1664:1792])
    if l > 0:
        put("vl", np.asarray(inp["rwkv_vres_down"][l - 1], f32))
    put("cq", w_in[:, OFF_CQ:OFF_CQ + 384])
    put("ckv", w_in[:, OFF_CKV:OFF_CKV + 256])
    kr = w_in[:, OFF_KR:OFF_KR + 32]
    W[:, CB["kr"] + 64:CB["kr"] + 96] = kr
    W[:, CB["krs"] + 64:CB["krs"] + 80] = kr[:, 16:32]
    W[:, CB["krs"] + 80:CB["krs"] + 96] = kr[:, 0:16]
    put("hq", w_in[:, OFF_HQ + hd * 128:OFF_HQ + (hd + 1) * 128])
    put("hf", w_in[:, OFF_HF + hd * 128:OFF_HF + (hd + 1) * 128])
    put("hi", w_in[:, OFF_HI + hd * 128:OFF_HI + (hd + 1) * 128][:, perm])
    put("hg", w_in[:, OFF_HG + hd * 128:OFF_HG + (hd + 1) * 128][:, perm])
    wB = lay_rows(W, NK)
    wsm = np.zeros((128, NSM), f32)
    hs = slice(c * 64, (c + 1) * 64)
    wsm[0:64, SM["w_up"]:SM["w_up"] + 64] = np.asarray(inp["rwkv_w_up"][l], f32)[:, hs]
    wsm[0:64, SM["a_up"]:SM["a_up"] + 64] = np.asarray(inp["rwkv_a_up"][l], f32)[:, hs]
    wsm[:, SM["g_up"]:SM["g_up"] + 64] = np.asarray(inp["rwkv_g_up"][l], f32)[:, hs]
    if l > 0:
        wsm[0:32, SM["vres_up"]:SM["vres_up"] + 64] = np.asarray(inp["rwkv_vres_up"][l - 1], f32)[:, hs]
    uq = np.asarray(inp["mla_w_uq"][l], f32)[:, c * 96:(c + 1) * 96]
    uqs = np.concatenate([uq[:, 0:64], uq[:, 80:96], uq[:, 64:80]], axis=1)
    wsm[:, SM["uq"]:SM["uq"] + 288] = lay_rows(uq, 3).reshape(128, 288)
    wsm[:, SM["uqs"]:SM["uqs"] + 288] = lay_rows(uqs, 3).reshape(128, 288)
    ukv = np.asarray(inp["mla_w_ukv"][l], f32)[:, c * 128:(c + 1) * 128]
    wsm[:, SM["ukv"]:SM["ukv"] + 256] = lay_rows(ukv, 2).reshape(128, 256)
    vec = np.zeros((128, NVB), f32)
    mu = np.asarray(inp["rwkv_mu"][l], f32)
    vec[0:64, VB["mu_r"]] = mu[c * 64:(c + 1) * 64]
    vec[0:64, VB["mu_k"]] = mu[512 + c * 64:512 + (c + 1) * 64]
    vec[0:64, VB["mu_v"]] = mu[1024 + c * 64:1024 + (c + 1) * 64]
    vec[0:64, VB["mu_wl"]] = mu[1536:1600]
    vec[0:64, VB["mu_al"]] = mu[1600:1664]
    vec[:, VB["mu_gl"]] = mu[1664:1792]
    if l > 0:
        vec[0:32, VB["mu_vl"]] = np.asarray(inp["rwkv_vres_mu"][l - 1], f32)
        vec[0:64, VB["v0"]] = np.asarray(inp["rwkv_v0"][l - 1], f32)[hs]
    for nm, key in (("w0", "rwkv_w0"), ("a0", "rwkv_a0"), ("k_k", "rwkv_k_k"), ("k_a", "rwkv_k_a"),
                    ("gn_g", "rwkv_gn_g"), ("gn_b", "rwkv_gn_b")):
        vec[0:64, VB[nm]] = np.asarray(inp[key][l], f32)[hs]
    vec[0:64, VB["r_k"]] = np.asarray(inp["rwkv_r_k"][l], f32)[c]
    vec[:, VB["qg"]:VB["qg"] + 3] = lay_vec(inp["mla_q_norm_g"][l])
    vec[:, VB["kvg"]:VB["kvg"] + 2] = lay_vec(inp["mla_kv_norm_g"][l])
    vec[:, VB["lb"]:VB["lb"] + 4] = np.asarray(inp["hgrn_lower_bounds"], f32)[:, hd * 128:(hd + 1) * 128].T
    vec[:, VB["hng"]] = np.asarray(inp["hgrn_norm_g"][l], f32)[perm]
    pos = np.ascontiguousarray(np.broadcast_to(np.asarray(inp["positions"]).reshape(1, -1)[:, :SL], (32, SL))).astype(np.int32)
    return dict(wB=wB, wsm=wsm, vecB=vec, pos=pos)


def emit_merge(cx, hT, uT, NT, y_d, wgate_d, wouts_d, wo_d, g_post):
    S, ar, ps = cx.S, cx.ar, cx.ps
    m0 = ar.mark()
    wouts = ar.alloc("wouts", [12, D], BF16)
    for j in range(12):
        S.dma("pool", wouts[:, j, :], wouts_d[:, j, :])
    wo = ar.alloc("wo", [NK, D], BF16)
    for k in range(NK):
        S.dma("pool", wo[:, k, :], wo_d[:, k, :])
    wgb = [ar.alloc("wgateb%d" % i, [NK, 384], BF16) for i in range(2)]
    yt = ar.alloc("yt", [12, TT], BF16)
    merged = ar.alloc("merged", [NK, TT], BF16)
    sig = [ar.alloc("sig%d" % i, [TT]) for i in range(3)]
    mt = [ar.alloc("mt%d" % i, [TT]) for i in range(2)]
    z = ar.alloc("z", [NK, TT])
    sq = [ar.alloc("msq%d" % i, [TT], BF16) for i in range(2)]
    rs = ar.alloc("mrs", [TT])
    tmp = [ar.alloc("mtmp%d" % i, [TT]) for i in range(2)]
    ntile = NT // TT
    nld = [0]

    def ld_gate(d):
        S.dma("pool", wgb[nld[0] % 2], wgate_d[d].rearrange("p (k m) -> p k m", k=NK))
        nld[0] += 1

    ld_gate(0)
    nsq = 0
    for t in range(ntile):
        c0 = t * TT
        for j in range(12):
            S.dma("sp", yt[:, j, :], y_d[:, j, c0:c0 + TT])
        for d in range(NK):
            wgt = wgb[(t * NK + d) % 2]
            if t * NK + d + 1 < ntile * NK:
                ld_gate((d + 1) % NK)
            for j in range(3):
                pj = ps[j]
                for k in range(4):
                    S.mm(pj, wouts[:, 4 * j + k, d * 128:(d + 1) * 128], yt[:, 4 * j + k, :], start=(k == 0), stop=(k == 3))
                gj = ps[3 + j]
                for k in range(NK):
                    S.mm(gj, wgt[:, k, j * 128:(j + 1) * 128], uT[:, k, c0:c0 + TT], start=(k == 0), stop=(k == NK - 1))
                S.act(sig[j], gj, AF.Sigmoid)
            S.tt("dve", mt[0], sig[0], ps[0], ALU.mult)
            S.tt("dve", mt[1], sig[1], ps[1], ALU.mult)
            S.tt("pool", mt[0], mt[0], mt[1], ALU.add)
            S.tt("dve", mt[1], sig[2], ps[2], ALU.mult)
            S.tt("pool", merged[:, d, :], mt[0], mt[1], ALU.add)
        for d in range(NK):
            zp = ps[6]
            for k in range(NK):
                S.mm(zp, wo[:, k, d * 128:(d + 1) * 128], merged[:, k, :], start=(k == 0), stop=(k == NK - 1))
            S.act(z[:, d, :], zp, AF.Copy)
            b = sq[nsq % 2]
            nsq += 1
            S.act(b, zp, AF.Square)
            S.mm(ps[7], cx.ones_b, b, start=(d == 0), stop=(d == NK - 1))
        emit_rstd(cx, ps[7], rs, D)
        for d in range(NK):
            tq = tmp[d % 2]
            S.stt("dve", tq, z[:, d, :], g_post[:, d:d + 1], rs, ALU.mult, ALU.mult)
            S.tt("pool", hT[:, d, c0:c0 + TT], hT[:, d, c0:c0 + TT], tq, ALU.add)
    S.barrier()
    ar.release(m0)


def lay_gate(w_in_l):
    g = np.asarray(w_in_l, np.float32)[:, OFF_GATE:OFF_GATE + 3 * D]
    g = g.reshape(NK, 128, 3, NK, 128)
    return np.ascontiguousarray(g.transpose(3, 1, 0, 2, 4).reshape(NK, 128, NK * 384))


def build_T(NT, merge, nxt):
    nc = bass.Bass("TRN2", target_bir_lowering=False)
    with ExitStack() as es:
        cx = make_ctx(nc, es, ARENA_WORDS)
        S, ar = cx.S, cx.ar
        nv = 24 * (int(merge) + int(nxt))
        h_in = dram_in(nc, "h_in", [128, NK, NT])
        cd = dram_in(nc, "consts", [128, 384])
        vd = dram_in(nc, "vec", [128, nv])
        load_consts(cx, cd)
        vec = load_vecs(cx, vd, nv)
        g = ar.alloc("gains", [nv])
        hT = ar.alloc("hT", [NK, NT])
        for k in range(NK):
            S.dma("sp", hT[:, k, :], h_in[:, k, :])
        o = 0
        if merge:
            u_in = dram_in(nc, "u_in", [128, NK, NT], BF16)
            y_in = dram_in(nc, "y_in", [128, 12, NT], BF16)
            wgate = dram_in(nc, "wgate", [NK, 128, NK * 384])
            wouts = dram_in(nc, "wouts", [128, 12, D])
            wo = dram_in(nc, "wo", [128, NK, D])
            wg2 = dram_in(nc, "wg2", [NF, 128, NK * 128])
            wu2 = dram_in(nc, "wu2", [NF, 128, NK * 128])
            wd2 = dram_in(nc, "wd2", [NK, 128, NF * 128])
            mk = ar.mark()
            uTm = ar.alloc("uTm", [NK, NT], BF16)
            for k in range(NK):
                S.dma("sp", uTm[:, k, :], u_in[:, k, :])
            S.ts("dve", g[:, 0:8], vec[:, 0:8], 32.0, ALU.mult)
            S.ts("dve", g[:, 8:16], vec[:, 8:16], 32.0, ALU.mult)
            S.ts("dve", g[:, 16:24], vec[:, 16:24], 16.0, ALU.mult)
            emit_merge(cx, hT, uTm, NT, y_in, wgate, wouts, wo, g[:, 0:8])
            ar.release(mk)
            emit_ffn(cx, hT, NT, wg2, wu2, wd2, g[:, 8:16], g[:, 16:24], G=2)
            o = 24
        h_out = dram_out(nc, "h_out", [128, NK, NT])
        if nxt:
            wg1 = dram_in(nc, "wg1", [NF, 128, NK * 128])
            wu1 = dram_in(nc, "wu1", [NF, 128, NK * 128])
            wd1 = dram_in(nc, "wd1", [NK, 128, NF * 128])
            u_out = dram_out(nc, "u_out", [128, NK, NT], BF16)
            S.ts("dve", g[:, o:o + 8], vec[:, o:o + 8], 32.0, ALU.mult)
            S.ts("dve", g[:, o + 8:o + 16], vec[:, o + 8:o + 16], 16.0, ALU.mult)
            S.ts("dve", g[:, o + 16:o + 24], vec[:, o + 16:o + 24], 32.0, ALU.mult)
            emit_ffn(cx, hT, NT, wg1, wu1, wd1, g[:, o:o + 8], g[:, o + 8:o + 16], G=2)
            uT = ar.alloc("uT", [NK, NT], BF16)
            emit_norm_to(cx, hT, NT, g[:, o + 16:o + 24], uT)
            for k in range(NK):
                S.dma("sp", u_out[:, k, :], uT[:, k, :])
        for k in range(NK):
            S.dma("sp", h_out[:, k, :], hT[:, k, :])
        S.emit()
    return nc


_PROG = {}


def _prog(key, fn):
    if key not in _PROG:
        _PROG[key] = fn()
    return _PROG[key]


def _run(nc, ins):
    res = run_bass_kernel_spmd(nc, ins, core_ids=list(range(NCORES)))
    return res.results


def kernel_impl(inp, SL, depth):
    NT = SL // NCORES
    consts = make_consts()
    cB = make_constsB()
    x = np.asarray(inp["x"], np.float32).reshape(SL, D)
    hs = [lay_act_T(x[c * NT:(c + 1) * NT]) for c in range(NCORES)]
    us = None
    vfirst = None

    def ffn_w(prefix, l, tag):
        return {"wg" + tag: lay_w_gu(inp[prefix + "_w_gate"][l]), "wu" + tag: lay_w_gu(inp[prefix + "_w_up"][l]),
                "wd" + tag: lay_w_d(inp[prefix + "_w_down"][l])}

    def vecs(names_l):
        return np.concatenate([lay_vec(inp[n][l]) for n, l in names_l], axis=1)

    nc = _prog(("T", NT, False, True), lambda: build_T(NT, False, True))
    com = dict(consts=consts, vec=vecs([("ffn1_pre_g", 0), ("ffn1_post_g", 0), ("mix_pre_g", 0)]))
    com.update(ffn_w("ffn1", 0, "1"))
    res = _run(nc, [dict(com, h_in=hs[c]) for c in range(NCORES)])
    hs = [np.asarray(r["h_out"]) for r in res]
    us = [np.asarray(r["u_out"]) for r in res]
    for l in range(depth):
        u_full = np.ascontiguousarray(np.concatenate(us, axis=2))
        ncb = _prog(("B", l if l < 2 else l, SL), lambda: build_B(l, SL))
        ins = []
        for c in range(NCORES):
            d = prep_B_core(inp, l, c, SL)
            d.update(u_full=u_full, consts=consts, cB=cB)
            if l > 0:
                d["vf_in"] = vfirst[c]
            ins.append(d)
        res = _run(ncb, ins)
        if l == 0:
            vfirst = [np.asarray(r["vf_out"]) for r in res]
        ys = [np.asarray(r["y_out"]) for r in res]
        y_ins = []
        for j in range(NCORES):
            yi = np.zeros((128, 12, NT), dtype=ys[0].dtype)
            for br in range(3):
                for q in range(4):
                    for half in range(2):
                        yi[half * 64:(half + 1) * 64, br * 4 + q, :] = ys[2 * q + half][br * 64:(br + 1) * 64, j * NT:(j + 1) * NT]
            y_ins.append(yi)
        last = (l == depth - 1)
        nct = _prog(("T", NT, True, not last), lambda: build_T(NT, True, not last))
        names = [("mix_post_g", l), ("ffn2_pre_g", l), ("ffn2_post_g", l)]
        if not last:
            names += [("ffn1_pre_g", l + 1), ("ffn1_post_g", l + 1), ("mix_pre_g", l + 1)]
        com = dict(consts=consts, vec=vecs(names), wgate=lay_gate(inp["w_in"][l]),
                   wouts=np.concatenate([lay_rows(inp["rwkv_out"][l], 4), lay_rows(inp["mla_out"][l], 4),
                                         lay_rows(inp["hgrn_out"][l], 4)], axis=1),
                   wo=lay_rows(inp["w_o"][l], NK))
        com.update(ffn_w("ffn2", l, "2"))
        if not last:
            com.update(ffn_w("ffn1", l + 1, "1"))
        res = _run(nct, [dict(com, h_in=hs[c], u_in=us[c], y_in=y_ins[c]) for c in range(NCORES)])
        hs = [np.asarray(r["h_out"]) for r in res]
        if not last:
            us = [np.asarray(r["u_out"]) for r in res]
    out = np.concatenate([unlay_act_T(h) for h in hs], axis=0)
    return out.reshape(1, SL, D).astype(np.float32)


def kernel(**inputs):
    return kernel_impl(inputs, 16384, 4)
```
